# Optimizing a Trainium2 kernel written in Bass

```python
import jax, jax.numpy as jnp
from jax import lax
import numpy as np

D_MODEL = 2048
BATCH = 1
SEQ = 16384
DEPTH = 2

D_A = D_MODEL // 2
CONF_KERNEL = 31
D_B = D_MODEL // 2
FNET_GROUPS = 8
FNET_GROUP_CH = D_B // FNET_GROUPS
IN_EVEN = 2 * D_A + D_B
MIX_EVEN = D_A + D_B

D_C = D_MODEL // 2
SHORT_KERNEL = 3
MLA_HEADS = 8
Q_LORA = 512
KV_LORA = 256
QK_NOPE = 128
QK_ROPE = 64
V_HEAD = 128
QK_HEAD = QK_NOPE + QK_ROPE
D_ATT = MLA_HEADS * V_HEAD
IN_ODD = 3 * D_C + Q_LORA + KV_LORA + QK_ROPE
MIX_ODD = D_C + D_ATT
ROPE_THETA = 10000.0
Q_BLOCK = 128

D_FF = 4 * D_MODEL
EPS = 1e-6

kernel_name = "hybrid_conformer_fnet_shortconv_mla_encoder"


def rmsnorm(x, g):
    xf = x.astype(jnp.float32)
    y = xf * lax.rsqrt(jnp.mean(xf * xf, axis=-1, keepdims=True) + EPS)
    return (y * g.astype(jnp.float32)).astype(x.dtype)


def layernorm(x, g, b):
    xf = x.astype(jnp.float32)
    mu = jnp.mean(xf, axis=-1, keepdims=True)
    var = jnp.mean(jnp.square(xf - mu), axis=-1, keepdims=True)
    y = (xf - mu) * lax.rsqrt(var + EPS)
    return (y * g.astype(jnp.float32) + b.astype(jnp.float32)).astype(x.dtype)


def dwconv(x, w):
    k, c = w.shape
    pad = k // 2
    return lax.conv_general_dilated(
        x, w[:, None, :].astype(x.dtype), window_strides=(1,),
        padding=[(pad, pad)], dimension_numbers=("NWC", "WIO", "NWC"),
        feature_group_count=c)


def conformer_conv(u, conv_w, conv_b, ln_g, ln_b):
    val, gate = u[..., :D_A], u[..., D_A:]
    h = val * jax.nn.sigmoid(gate)
    h = dwconv(h, conv_w) + conv_b.astype(h.dtype)
    h = layernorm(h, ln_g, ln_b)
    return jax.nn.silu(h)


def fourier_mix(u):
    b, s, _ = u.shape
    z = u.astype(jnp.float32).reshape(b, s, FNET_GROUPS, FNET_GROUP_CH)
    f = jnp.fft.fftn(z, axes=(1, 3), norm="ortho")
    return jnp.real(f).reshape(b, s, D_B).astype(u.dtype)


def short_gated_conv(u, conv_w):
    b_gate = u[..., :D_C]
    c_gate = u[..., D_C:2 * D_C]
    h = u[..., 2 * D_C:]
    return b_gate * dwconv(c_gate * h, conv_w)


def rope_tables(seq):
    inv = 1.0 / (ROPE_THETA ** (jnp.arange(0, QK_ROPE, 2, dtype=jnp.float32) / QK_ROPE))
    ang = jnp.arange(seq, dtype=jnp.float32)[:, None] * inv[None, :]
    return jnp.cos(ang), jnp.sin(ang)


def apply_rope(x, cos, sin):
    half = x.shape[-1] // 2
    x1, x2 = x[..., :half], x[..., half:]
    cos = cos.astype(x.dtype)
    sin = sin.astype(x.dtype)
    return jnp.concatenate([x1 * cos - x2 * sin, x1 * sin + x2 * cos], axis=-1)


def mla(u, q_norm_g, w_uq, kv_norm_g, w_ukv, cos, sin):
    b, s, _ = u.shape
    c_q = u[..., :Q_LORA]
    c_kv = u[..., Q_LORA:Q_LORA + KV_LORA]
    k_r = u[..., Q_LORA + KV_LORA:]
    q = jnp.einsum('bsr,re->bse', rmsnorm(c_q, q_norm_g), w_uq)
    q = q.reshape(b, s, MLA_HEADS, QK_HEAD)
    q = jnp.concatenate([q[..., :QK_NOPE],
                         apply_rope(q[..., QK_NOPE:], cos[None, :, None, :], sin[None, :, None, :])], axis=-1)
    kv = jnp.einsum('bsr,re->bse', rmsnorm(c_kv, kv_norm_g), w_ukv)
    kv = kv.reshape(b, s, MLA_HEADS, QK_NOPE + V_HEAD)
    k_nope, v = kv[..., :QK_NOPE], kv[..., QK_NOPE:]
    k_r = apply_rope(k_r, cos[None], sin[None])
    k = jnp.concatenate([k_nope, jnp.broadcast_to(k_r[:, :, None, :], (b, s, MLA_HEADS, QK_ROPE))], axis=-1)
    scale = QK_HEAD ** -0.5
    nblk = s // Q_BLOCK
    qb = q.reshape(b, nblk, Q_BLOCK, MLA_HEADS, QK_HEAD).transpose(1, 0, 2, 3, 4)

    def attend(qblk):
        sc = jnp.einsum('bqhd,bkhd->bhqk', qblk, k).astype(jnp.float32) * scale
        p = jax.nn.softmax(sc, axis=-1).astype(v.dtype)
        return jnp.einsum('bhqk,bkhd->bqhd', p, v)

    o = lax.map(attend, qb)
    return o.transpose(1, 0, 2, 3, 4).reshape(b, s, D_ATT)


def sq_relu_mlp(h, w_up, w_down):
    a = jnp.einsum('bsd,df->bsf', h, w_up)
    a = jnp.square(jax.nn.relu(a))
    return jnp.einsum('bsf,fd->bsd', a, w_down)


def setup_inputs(seed: int = 0) -> dict:
    key = jax.random.key(seed)
    ks = iter(jax.random.split(key, 32))
    n_even = (DEPTH + 1) // 2
    n_odd = DEPTH // 2

    def w(shape, fan_in):
        return jax.random.normal(next(ks), shape, jnp.float32) * (fan_in ** -0.5)

    def gain(shape):
        return 1.0 + 0.01 * jax.random.normal(next(ks), shape, jnp.float32)

    def bias(shape):
        return 0.02 * jax.random.normal(next(ks), shape, jnp.float32)

    return {
        "x": jax.random.normal(next(ks), (BATCH, SEQ, D_MODEL), jnp.float32),
        "mix_norm_e": gain((n_even, D_MODEL)),
        "w_in_e": w((n_even, D_MODEL, IN_EVEN), D_MODEL),
        "conv_a_w": w((n_even, CONF_KERNEL, D_A), CONF_KERNEL),
        "conv_a_b": bias((n_even, D_A)),
        "ln_a_g": gain((n_even, D_A)),
        "ln_a_b": bias((n_even, D_A)),
        "w_out_e": w((n_even, MIX_EVEN, D_MODEL), MIX_EVEN),
        "mix_norm_o": gain((n_odd, D_MODEL)),
        "w_in_o": w((n_odd, D_MODEL, IN_ODD), D_MODEL),
        "conv_c_w": w((n_odd, SHORT_KERNEL, D_C), SHORT_KERNEL),
        "q_norm_g": gain((n_odd, Q_LORA)),
        "w_uq": w((n_odd, Q_LORA, MLA_HEADS * QK_HEAD), Q_LORA),
        "kv_norm_g": gain((n_odd, KV_LORA)),
        "w_ukv": w((n_odd, KV_LORA, MLA_HEADS * (QK_NOPE + V_HEAD)), KV_LORA),
        "w_out_o": w((n_odd, MIX_ODD, D_MODEL), MIX_ODD),
        "mlp_norm": gain((DEPTH, D_MODEL)),
        "w_up": w((DEPTH, D_MODEL, D_FF), D_MODEL),
        "w_down": w((DEPTH, D_FF, D_MODEL), D_FF),
        "final_norm": gain((D_MODEL,)),
    }


def reference(x, mix_norm_e, w_in_e, conv_a_w, conv_a_b, ln_a_g, ln_a_b, w_out_e,
              mix_norm_o, w_in_o, conv_c_w, q_norm_g, w_uq, kv_norm_g, w_ukv, w_out_o,
              mlp_norm, w_up, w_down, final_norm):
    cos, sin = rope_tables(x.shape[1])
    for i in range(DEPTH):
        j = i // 2
        if i % 2 == 0:
            h = rmsnorm(x, mix_norm_e[j])
            u = jnp.einsum('bsd,de->bse', h, w_in_e[j])
            ya = conformer_conv(u[..., :2 * D_A], conv_a_w[j], conv_a_b[j], ln_a_g[j], ln_a_b[j])
            yb = fourier_mix(u[..., 2 * D_A:])
            y = jnp.concatenate([ya, yb], axis=-1)
            x = x + jnp.einsum('bse,ed->bsd', y, w_out_e[j])
        else:
            h = rmsnorm(x, mix_norm_o[j])
            u = jnp.einsum('bsd,de->bse', h, w_in_o[j])
            yc = short_gated_conv(u[..., :3 * D_C], conv_c_w[j])
            yd = mla(u[..., 3 * D_C:], q_norm_g[j], w_uq[j], kv_norm_g[j], w_ukv[j], cos, sin)
            y = jnp.concatenate([yc, yd], axis=-1)
            x = x + jnp.einsum('bse,ed->bsd', y, w_out_o[j])
        x = x + sq_relu_mlp(rmsnorm(x, mlp_norm[i]), w_up[i], w_down[i])
    return rmsnorm(x, final_norm)
```

```python
import numpy as np
from contextlib import ExitStack
import concourse.bass as bass
import concourse.mybir as mybir
from concourse.bass_utils import run_bass_kernel_spmd

F32 = mybir.dt.float32
BF16 = mybir.dt.bfloat16
AF = mybir.ActivationFunctionType
ALU = mybir.AluOpType

NCORES = 8
S = 16384
T = 2048
D = 2048
EPS = 1e-6
ENG_NAMES = ['sync', 'scalar', 'vector', 'gpsimd', 'tensor']
LAST_PHASE = 99
DEBUG = False
DEBUG_SRC = ('mixT', 'mixT')


class Reg:
    __slots__ = ('w', 'r')

    def __init__(self):
        self.w = {}
        self.r = {}


class Rot:
    def __init__(self, c, bufs):
        self.bufs = bufs
        self.regs = [c.reg() for _ in bufs]
        self.i = 0

    def next(self):
        i = self.i % len(self.bufs)
        self.i += 1
        return self.bufs[i], self.regs[i]

    def nexti(self):
        i = self.i % len(self.bufs)
        self.i += 1
        return i, self.bufs[i], self.regs[i]


class Ctx:
    def __init__(self, nc, stack):
        self.nc = nc
        self.stack = stack
        self.sems = {}
        self.cur = None
        self.eng = None
        self.count = {}
        self.waited = {}
        self.waited_by = {n: {} for n in ENG_NAMES}
        self.pregs = []

    def reg(self):
        return Reg()

    def preg(self):
        r = Reg()
        self.pregs.append(r)
        return r

    def snapshot(self):
        return (dict(self.count), [(dict(r.w), dict(r.r)) for r in self.pregs])

    def restore(self, snap):
        self.count = dict(snap[0])
        for r, (w, rr) in zip(self.pregs, snap[1]):
            r.w = dict(w)
            r.r = dict(rr)

    def _deps(self, reads, writes, accs, extra):
        need = {}
        for r in reads:
            for s, v in r.w.items():
                if need.get(s, 0) < v:
                    need[s] = v
        for r in writes:
            for d in (r.w, r.r):
                for s, v in d.items():
                    if need.get(s, 0) < v:
                        need[s] = v
        for r in accs:
            for s, v in r.r.items():
                if need.get(s, 0) < v:
                    need[s] = v
        for t in extra:
            if t is not None and need.get(t[0], 0) < t[1]:
                need[t[0]] = t[1]
        return need

    def _emit_waits(self, need, skip_sem=None):
        for s, v in need.items():
            if s == skip_sem:
                continue
            if self.waited.get(s, 0) >= v:
                continue
            self.eng.wait_ge(self.sems[s], v)
            self.waited[s] = v

    def _update(self, tok, reads, writes, accs):
        s, v = tok
        for r in reads:
            if r.r.get(s, 0) < v:
                r.r[s] = v
        for r in writes:
            r.w = {s: v}
            r.r = {}
        for r in accs:
            if r.w.get(s, 0) < v:
                r.w[s] = v

    def op(self, engname, fn, reads=(), writes=(), accs=(), extra=(), sig=True):
        sname = 'm_' + engname
        if not sig:
            if engname == self.cur:
                need = self._deps(reads, writes, accs, extra)
                self._emit_waits(need, skip_sem=sname if engname == 'tensor' else None)
                fn(self.eng)
            return None
        need = self._deps(reads, writes, accs, extra)
        for r in accs:
            for s_, v_ in r.w.items():
                if s_ != sname and need.get(s_, 0) < v_:
                    need[s_] = v_
        self.count[sname] = self.count.get(sname, 0) + 1
        tok = (sname, self.count[sname])
        if engname == self.cur:
            self._emit_waits(need, skip_sem=sname if engname == 'tensor' else None)
            ins = fn(self.eng)
            ins.then_inc(self.sems[sname], 1)
        self._update(tok, reads, writes, accs)
        return tok

    def dma(self, qname, out, in_, sem, reads=(), writes=(), accs=(), extra=(), fn=None, inc=16):
        need = self._deps(reads, writes, accs, extra)
        self.count[sem] = self.count.get(sem, 0) + inc
        tok = (sem, self.count[sem])
        if qname == self.cur:
            self._emit_waits(need)
            if fn is not None:
                ins = fn(self.eng)
            else:
                ins = self.eng.dma_start(out=out, in_=in_)
            ins.then_inc(self.sems[sem], inc)
        self._update(tok, reads, writes, accs)
        return tok

    def barrier(self):
        if self.cur is not None:
            self._emit_waits(dict(self.count))

    def run_phase(self, alloc_fn, prog_fn):
        nc = self.nc
        with ExitStack() as st:
            bufs = alloc_fn(nc, st)
            snap = self.snapshot()
            self.cur = None
            self.eng = None
            prog_fn(self, bufs)
            end = self.snapshot()
            for name in sorted(self.count.keys()):
                if name not in self.sems:
                    self.sems[name] = self.stack.enter_context(nc.semaphore(name))
            with nc.Block() as block:
                for name in ENG_NAMES:
                    def body(eng, name=name):
                        self.restore(snap)
                        self.cur = name
                        self.eng = eng
                        self.waited = self.waited_by[name]
                        prog_fn(self, bufs)
                        self.barrier()
                    getattr(block, name)(body)
            self.cur = None
            self.eng = None
            self.restore(end)


def stream(items, nslots, load_fn, compute_fn):
    n = len(items)
    for i in range(min(nslots - 1, n)):
        load_fn(items[i], i % nslots)
    for i in range(n):
        j = i + nslots - 1
        if j < n:
            load_fn(items[j], j % nslots)
        compute_fn(items[i], i % nslots)


VOFF = {}
_o = 0
for _name, _n in [('mix_norm_e', 16), ('mlp_norm0', 16), ('mix_norm_o', 16), ('mlp_norm1', 16),
                  ('final_norm', 16), ('conv_a_b', 8), ('ln_a_g', 8), ('ln_a_b', 8),
                  ('conv_a_w', 31 * 8), ('conv_c_w', 3 * 8), ('q_norm_g', 4), ('kv_norm_g', 2)]:
    VOFF[_name] = _o
    _o += _n
NV = _o


def build_program():
    nc = bass.Bass("TRN2", target_bir_lowering=False)
    stack = ExitStack()
    G = {}

    def din(name, shape, dt=F32):
        G[name] = nc.dram_tensor(name, list(shape), dt, kind="ExternalInput").ap()

    din('xT', [D, T + 32])
    din('w_in_e', [D, 3072])
    din('w_out_e', [D, D])
    din('w_in_o', [D, 3968])
    din('w_uq3', [512, 2048])
    din('w_ukv2', [256, 2048])
    din('w_out_o', [D, D])
    din('w_up0', [D, 8192])
    din('w_up1', [D, 8192])
    din('w_down0', [8192, D])
    din('w_down1', [8192, D])
    din('vecs', [128, NV])
    din('dft', [128, 512])
    din('tcs', [128, 2 * 2048])
    din('rope', [64, 2 * 2048])
    din('sel', [8, 2])
    din('ident', [128, 128])
    G['outT'] = nc.dram_tensor('outT', [D, T], F32, kind="ExternalOutput").ap()
    if DEBUG:
        G['dbg'] = nc.dram_tensor('dbg', [D, T], BF16, kind="ExternalOutput").ap()
        G['dbg2'] = nc.dram_tensor('dbg2', [D, T], F32, kind="ExternalOutput").ap()
    G['mixT'] = nc.dram_tensor('mixT', [D, T], BF16).ap()
    G['agin1'] = nc.dram_tensor('agin1', [16, 262144], BF16).ap()
    G['agout1'] = nc.dram_tensor('agout1', [128, 262144], BF16).ap()
    G['x1T'] = nc.dram_tensor('x1T', [D, T], F32).ap()
    G['ag2in'] = nc.dram_tensor('ag2in', [2112, T], BF16).ap()
    G['ag2out'] = nc.dram_tensor('ag2out', [8 * 2112, T], BF16).ap()
    G['ag3in'] = nc.dram_tensor('ag3in', [1, 2048], F32).ap()
    G['ag3out'] = nc.dram_tensor('ag3out', [8, 2048], F32).ap()
    G['qT'] = nc.dram_tensor('qT', [1536, T], BF16).ap()
    G['vT'] = nc.dram_tensor('vT', [1024, T], F32).ap()
    G['bT'] = nc.dram_tensor('bT', [1024, T], F32).ap()

    def gsb(name, shape, dt):
        return stack.enter_context(nc.sbuf_tensor(name, shape, dt))

    G['vecs_sb'] = gsb('vecs_sb', [128, NV], F32)
    G['ones_bf'] = gsb('ones_bf', [128, 128], BF16)
    G['ones32'] = gsb('ones32', [128, 128], F32)
    G['dft_sb'] = gsb('dft_sb', [128, 512], BF16)
    G['ident32'] = gsb('ident32', [128, 128], F32)
    G['ps'] = [stack.enter_context(nc.psum_tensor(f'ps{i}', [128, 512], F32)) for i in range(8)]

    c = Ctx(nc, stack)
    R = {k: c.preg() for k in ['mixT', 'agin1', 'agout1', 'x1T', 'ag2in', 'ag2out', 'ag3in', 'ag3out',
                               'qT', 'vT', 'bT', 'outT', 'const', 'dbg']}

    def vec(name, i):
        o = VOFF[name] + i
        return G['vecs_sb'][:, o:o + 1]

    def p0_alloc(nc, st):
        return {}

    def p0_prog(c, b):
        c.dma('sync', G['vecs_sb'][:], G['vecs'][:], 'ld0', accs=[R['const']])
        c.dma('gpsimd', G['dft_sb'][:], G['dft'][:], 'ld1', accs=[R['const']])
        c.dma('sync', G['ident32'][:], G['ident'][:], 'ld2', accs=[R['const']])
        c.op('vector', lambda e: e.memset(G['ones_bf'][:], 1.0), accs=[R['const']])
        c.op('vector', lambda e: e.memset(G['ones32'][:], 1.0), accs=[R['const']])

    c.run_phase(p0_alloc, p0_prog)

    def rstd_from(c, ps_ap, rps, out_ap, rout, n, acc=False):
        c.op('scalar', lambda e: e.activation(out=out_ap, in_=ps_ap, func=AF.Sqrt, bias=EPS, scale=1.0 / n),
             reads=[rps], accs=[rout] if acc else [], writes=[] if acc else [rout])
        c.op('vector', lambda e: e.reciprocal(out=out_ap, in_=out_ap), reads=[rout], accs=[rout])

    def pa_alloc(nc, st):
        sb = lambda name, shape, dt: st.enter_context(nc.sbuf_tensor(name, shape, dt))
        b = {}
        b['xs'] = sb('a_xs', [128, 16, 544], F32)
        b['hT'] = sb('a_hT', [128, 16, 544], BF16)
        b['w'] = [sb(f'a_w{i}', [128, 16, 512], BF16) for i in range(3)]
        b['val'] = sb('a_val', [128, 4, 544], F32)
        b['gT'] = sb('a_gT', [128, 8, 544], F32)
        b['acc'] = sb('a_acc', [128, 8, 512], F32)
        b['sq'] = [sb(f'a_sq{i}', [128, 544], BF16) for i in range(2)]
        b['sig'] = [sb(f'a_sig{i}', [128, 512], F32) for i in range(2)]
        b['ub'] = [sb(f'a_ub{i}', [128, 512], BF16) for i in range(2)]
        b['stg'] = [sb(f'a_stg{i}', [128, 2, 512], BF16) for i in range(2)]
        b['rstd'] = sb('a_rstd', [128, 544], F32)
        b['ln'] = sb('a_ln', [128, 5, 512], F32)
        b['t'] = [sb(f'a_t{i}', [128, 512], F32) for i in range(2)]
        b['ya'] = [sb(f'a_ya{i}', [128, 512], BF16) for i in range(2)]
        return b

    def pa_prog(c, b):
        pb = Rot(c, G['ps'])
        wrot_regs = [c.reg() for _ in range(3)]
        sq = Rot(c, b['sq'])
        sig = Rot(c, b['sig'])
        ubr = Rot(c, b['ub'])
        stg = Rot(c, b['stg'])
        tt = Rot(c, b['t'])
        yar = Rot(c, b['ya'])
        rxs, rhT, rval, rgT, racc, rrstd, rln = [c.reg() for _ in range(7)]
        agin1 = G['agin1'].rearrange("a (ri g kc n) -> a ri g kc n", ri=2, g=8, kc=128)
        xs, hT = b['xs'], b['hT']
        for q in range(4):
            c.dma('sync', xs[:], G['xT'][:, 512 * q:512 * q + 544].rearrange("(kc p) t -> p kc t", p=128),
                  'lda', writes=[rxs])
            psA, rA = pb.next()
            psB, rB = pb.next()
            for kc in range(16):
                s, rs = sq.next()
                c.op('scalar', lambda e, kc=kc, s=s: e.activation(out=s[:], in_=xs[:, kc, :], func=AF.Square),
                     reads=[rxs], writes=[rs])
                c.op('tensor', lambda e, kc=kc, s=s: e.matmul(psA[:], G['ones_bf'][:], s[:, 0:512],
                                                              start=(kc == 0), stop=(kc == 15)),
                     reads=[rs, R['const']], writes=[rA] if kc == 0 else [], accs=[rA] if kc else [])
                c.op('tensor', lambda e, kc=kc, s=s: e.matmul(psB[:, 0:32], G['ones_bf'][:], s[:, 512:544],
                                                              start=(kc == 0), stop=(kc == 15)),
                     reads=[rs, R['const']], writes=[rB] if kc == 0 else [], accs=[rB] if kc else [])
            rstd_from(c, psA[:], rA, b['rstd'][:, 0:512], rrstd, D)
            rstd_from(c, psB[:, 0:32], rB, b['rstd'][:, 512:544], rrstd, D, acc=True)
            for kc in range(16):
                c.op('vector', lambda e, kc=kc: e.scalar_tensor_tensor(
                    out=hT[:, kc, :], in0=xs[:, kc, :], scalar=vec('mix_norm_e', kc), in1=b['rstd'][:],
                    op0=ALU.mult, op1=ALU.mult),
                    reads=[rxs, rrstd, R['const']], writes=[rhT] if kc == 0 else [], accs=[rhT] if kc else [])

            groups = [(0, 'val', 0), (1024, 'gate', 0), (512, 'val', 1), (1536, 'gate', 1),
                      (2048, 'ub', 0), (2560, 'ub', 1)]

            def load_w(item, slot):
                col0 = item[0]
                c.dma('gpsimd', b['w'][slot][:], G['w_in_e'][:, col0:col0 + 512].rearrange("(kc p) n -> p kc n", p=128),
                      f'w{slot}', writes=[wrot_regs[slot]])

            def compute(item, slot):
                col0, kind, gi = item
                w = b['w'][slot]
                rw = wrot_regs[slot]
                for mi in range(4):
                    parts = [(16, 512)] if kind == 'ub' else [(0, 512), (512, 32)]
                    for (c0, n) in parts:
                        ps, rp = pb.next()
                        for kc in range(16):
                            last = kc == 15
                            c.op('tensor', lambda e, kc=kc, ps=ps, c0=c0, n=n, w=w, mi=mi: e.matmul(
                                ps[:, 0:n], w[:, kc, mi * 128:(mi + 1) * 128], hT[:, kc, c0:c0 + n],
                                start=(kc == 0), stop=(kc == 15)),
                                reads=[rw, rhT], writes=[rp] if kc == 0 else [], accs=[rp] if last and kc else [],
                                sig=(last or kc == 0))
                        if kind == 'val':
                            c.op('scalar', lambda e, ps=ps, c0=c0, n=n, mi=mi: e.activation(
                                out=b['val'][:, mi, c0:c0 + n], in_=ps[:, 0:n], func=AF.Copy),
                                reads=[rp], accs=[rval])
                        elif kind == 'gate':
                            sg, rsg = sig.next()
                            c.op('scalar', lambda e, ps=ps, n=n, sg=sg: e.activation(
                                out=sg[:, 0:n], in_=ps[:, 0:n], func=AF.Sigmoid),
                                reads=[rp], writes=[rsg])
                            ch = gi * 4 + mi
                            c.op('vector', lambda e, sg=sg, c0=c0, n=n, mi=mi, ch=ch: e.tensor_tensor(
                                out=b['gT'][:, ch, c0:c0 + n], in0=b['val'][:, mi, c0:c0 + n], in1=sg[:, 0:n], op=ALU.mult),
                                reads=[rsg, rval], accs=[rgT])
                        else:
                            u, ru = ubr.next()
                            c.op('scalar', lambda e, ps=ps, u=u: e.activation(out=u[:], in_=ps[:], func=AF.Copy),
                                 reads=[rp], writes=[ru])
                            g = gi * 4 + mi
                            sgi, sgb, rsgb = stg.nexti()
                            for ri in range(2):
                                ps2, rp2 = pb.next()
                                c.op('tensor', lambda e, ps2=ps2, u=u, ri=ri: e.matmul(
                                    ps2[:], G['dft_sb'][:, ri * 128:(ri + 1) * 128], u[:], start=True, stop=True),
                                    reads=[ru, R['const']], writes=[rp2])
                                if ri == 0:
                                    c.op('scalar', lambda e, ps2=ps2, sgb=sgb: e.activation(
                                        out=sgb[:, 0, :], in_=ps2[:], func=AF.Copy), reads=[rp2], writes=[rsgb])
                                else:
                                    c.op('vector', lambda e, ps2=ps2, sgb=sgb: e.tensor_copy(
                                        out=sgb[:, 1, :], in_=ps2[:]), reads=[rp2], accs=[rsgb])
                            for ri in range(2):
                                c.dma('sync', agin1[4 * q:4 * q + 4, ri, g, :, :].rearrange("a k n -> k a n"),
                                      sgb[:, ri, :].rearrange("k (a n) -> k a n", a=4), f'sta1{sgi}',
                                      reads=[rsgb], accs=[R['agin1']])
                if kind == 'val':
                    pass

            stream(groups, 3, load_w, compute)

            for ch in range(8):
                c.op('vector', lambda e, ch=ch: e.tensor_scalar(
                    out=b['acc'][:, ch, :], in0=b['gT'][:, ch, 1:513], scalar1=vec('conv_a_w', 0 * 8 + ch),
                    scalar2=vec('conv_a_b', ch), op0=ALU.mult, op1=ALU.add),
                    reads=[rgT, R['const']], writes=[racc] if ch == 0 else [], accs=[racc] if ch else [])
                for k in range(1, 31):
                    c.op('vector', lambda e, ch=ch, k=k: e.scalar_tensor_tensor(
                        out=b['acc'][:, ch, :], in0=b['gT'][:, ch, 1 + k:513 + k], scalar=vec('conv_a_w', k * 8 + ch),
                        in1=b['acc'][:, ch, :], op0=ALU.mult, op1=ALU.add),
                        reads=[rgT, racc], accs=[racc])
            psM, rM = pb.next()
            psQ, rQ = pb.next()
            for ch in range(8):
                c.op('tensor', lambda e, ch=ch: e.matmul(psM[:], G['ones32'][:], b['acc'][:, ch, :],
                                                         start=(ch == 0), stop=(ch == 7)),
                     reads=[racc, R['const']], writes=[rM] if ch == 0 else [], accs=[rM] if ch else [])
                s, rs = sq.next()
                c.op('scalar', lambda e, ch=ch, s=s: e.activation(out=s[:, 0:512], in_=b['acc'][:, ch, :], func=AF.Square),
                     reads=[racc], writes=[rs])
                c.op('tensor', lambda e, ch=ch, s=s: e.matmul(psQ[:], G['ones_bf'][:], s[:, 0:512],
                                                              start=(ch == 0), stop=(ch == 7)),
                     reads=[rs], writes=[rQ] if ch == 0 else [], accs=[rQ] if ch else [])
            ln = b['ln']
            mean, msq, var, rs2, nmr = [ln[:, i, :] for i in range(5)]
            c.op('vector', lambda e: e.tensor_scalar(out=mean, in0=psM[:], scalar1=1.0 / 1024, scalar2=None, op0=ALU.mult),
                 reads=[rM], writes=[rln])
            c.op('vector', lambda e: e.tensor_tensor(out=msq, in0=mean, in1=mean, op=ALU.mult), reads=[rln], accs=[rln])
            c.op('vector', lambda e: e.scalar_tensor_tensor(out=var, in0=psQ[:], scalar=1.0 / 1024, in1=msq,
                                                           op0=ALU.mult, op1=ALU.subtract),
                 reads=[rQ, rln], accs=[rln])
            c.op('scalar', lambda e: e.activation(out=rs2, in_=var, func=AF.Sqrt, bias=EPS, scale=1.0), reads=[rln], accs=[rln])
            c.op('vector', lambda e: e.reciprocal(out=rs2, in_=rs2), reads=[rln], accs=[rln])
            c.op('vector', lambda e: e.scalar_tensor_tensor(out=nmr, in0=mean, scalar=-1.0, in1=rs2,
                                                           op0=ALU.mult, op1=ALU.mult), reads=[rln], accs=[rln])
            for ch in range(8):
                t, rt = tt.next()
                c.op('vector', lambda e, ch=ch, t=t: e.tensor_tensor(out=t[:], in0=b['acc'][:, ch, :], in1=rs2, op=ALU.mult),
                     reads=[racc, rln], writes=[rt])
                c.op('vector', lambda e, t=t: e.tensor_tensor(out=t[:], in0=t[:], in1=nmr, op=ALU.add),
                     reads=[rt, rln], accs=[rt])
                yi, ya, rya = yar.nexti()
                c.op('scalar', lambda e, ch=ch, t=t, ya=ya: e.activation(
                    out=ya[:], in_=t[:], func=AF.Silu, bias=vec('ln_a_b', ch), scale=vec('ln_a_g', ch)),
                    reads=[rt, R['const']], writes=[rya])
                c.dma('sync', G['mixT'][ch * 128:(ch + 1) * 128, 512 * q:512 * q + 512], ya[:], f'sta2{yi}',
                      reads=[rya], accs=[R['mixT']])

    c.run_phase(pa_alloc, pa_prog)

    def ag_phase(kind_in, kind_out, semname):
        def alloc(nc, st):
            return {}

        def prog(c, b):
            c.dma('gpsimd', None, None, semname, reads=[R[kind_in]], writes=[R[kind_out]], inc=1,
                  fn=lambda e: e.collective_compute("AllGather", ALU.bypass, replica_groups=[list(range(NCORES))],
                                                    ins=[G[kind_in][:]], outs=[G[kind_out][:]]))
        c.run_phase(alloc, prog)

    if LAST_PHASE >= 1:
        ag_phase('agin1', 'agout1', 'cc1')

    def pb_alloc(nc, st):
        sb = lambda name, shape, dt: st.enter_context(nc.sbuf_tensor(name, shape, dt))
        b = {}
        b['At'] = [sb(f'b_At{i}', [128, 2, 64, 128], BF16) for i in range(2)]
        b['Bt'] = [sb(f'b_Bt{i}', [128, 2, 64, 128], BF16) for i in range(2)]
        b['tcs'] = sb('b_tcs', [128, 2, 2048], BF16)
        b['yb'] = [sb(f'b_yb{i}', [128, 2048], BF16) for i in range(2)]
        return b

    def pb_prog(c, b):
        pbk = Rot(c, G['ps'][0:4])
        ybanks = G['ps'][4:8]
        rY = [c.reg() for _ in range(4)]
        rAt = [c.reg() for _ in range(2)]
        Btr = Rot(c, b['Bt'])
        ybr = Rot(c, b['yb'])
        rtcs = c.reg()
        c.dma('gpsimd', b['tcs'][:], G['tcs'].rearrange("p (a n) -> p a n", a=2), 'ldb0', writes=[rtcs])
        agv = G['agout1'].rearrange("a (ri g kc n) -> a ri g kc n", ri=2, g=8, kc=128)
        items = [(g, kh) for g in range(8) for kh in range(2)]

        def load(item, slot):
            g, kh = item
            for ri in range(2):
                c.dma('sync', b['At'][slot][:, ri, :, :], agv[:, ri, g, kh * 64:(kh + 1) * 64, :], f'ldb{slot + 1}',
                      reads=[R['agout1']], writes=[rAt[slot]] if ri == 0 else [], accs=[rAt[slot]] if ri else [])

        def compute(item, slot):
            g, kh = item
            At = b['At'][slot]
            Bt, rBt = Btr.next()
            first = True
            for kcp in range(32):
                ps, rp = pbk.next()
                for j in range(2):
                    kc = 2 * kcp + j
                    c.op('tensor', lambda e, ps=ps, j=j, kc=kc: e.matmul(
                        ps[:, j * 256:(j + 1) * 256], At[:, 0, kc, :], G['dft_sb'][:, 0:256], start=True, stop=False),
                        reads=[rAt[slot], R['const']], writes=[rp] if j == 0 else [], sig=(j == 0))
                    c.op('tensor', lambda e, ps=ps, j=j, kc=kc: e.matmul(
                        ps[:, j * 256:(j + 1) * 256], At[:, 1, kc, :], G['dft_sb'][:, 256:512], start=False, stop=True),
                        reads=[rAt[slot]], accs=[rp] if j == 1 else [], sig=(j == 1))
                src = ps[:].rearrange("p (j ri k) -> p ri j k", j=2, ri=2)
                dst = Bt[:, :, 2 * kcp:2 * kcp + 2, :]
                if kcp % 2 == 0:
                    c.op('scalar', lambda e, src=src, dst=dst: e.activation(out=dst, in_=src, func=AF.Copy),
                         reads=[rp], writes=[rBt] if first else [], accs=[] if first else [rBt])
                else:
                    c.op('vector', lambda e, src=src, dst=dst: e.tensor_copy(out=dst, in_=src),
                         reads=[rp], writes=[rBt] if first else [], accs=[] if first else [rBt])
                first = False
            for k1 in range(128):
                bk = k1 // 32
                o = (k1 % 32) * 16
                yb_ps = ybanks[bk]
                fst = (k1 % 32 == 0)
                lst = (k1 % 32 == 31)
                c.op('tensor', lambda e, yb_ps=yb_ps, o=o, k1=k1: e.matmul(
                    yb_ps[0:64, o:o + 16], Bt[:, 0, :, k1], b['tcs'][:, 0, k1 * 16:(k1 + 1) * 16], start=True, stop=False),
                    reads=[rBt, rtcs], writes=[rY[bk]] if fst else [], sig=fst)
                c.op('tensor', lambda e, yb_ps=yb_ps, o=o, k1=k1: e.matmul(
                    yb_ps[0:64, o:o + 16], Bt[:, 1, :, k1], b['tcs'][:, 1, k1 * 16:(k1 + 1) * 16], start=False, stop=True),
                    reads=[rBt], accs=[rY[bk]] if lst else [], sig=lst)
            ybi, yb, ryb = ybr.nexti()
            for bk in range(4):
                src = ybanks[bk][0:64, :].rearrange("p (k1 k2) -> p k2 k1", k2=16)
                dst = yb[0:64, :].rearrange("p (k2 k1) -> p k2 k1", k2=16)[:, :, bk * 32:(bk + 1) * 32]
                c.op('scalar', lambda e, src=src, dst=dst: e.activation(out=dst, in_=src, func=AF.Copy),
                     reads=[rY[bk]], writes=[ryb] if bk == 0 else [], accs=[ryb] if bk else [])
            r0 = 1024 + g * 128 + kh * 64
            c.dma('sync', G['mixT'][r0:r0 + 64, :], yb[0:64, :], f'stb{ybi}', reads=[ryb], accs=[R['mixT']])

        stream(items, 2, load, compute)

    if LAST_PHASE >= 2:
        c.run_phase(pb_alloc, pb_prog)

    def make_mixmlp(layer, W_out, W_up, W_down, xin, xin_off, xin_reg, norm_name, final):
        def alloc(nc, st):
            sb = lambda name, shape, dt: st.enter_context(nc.sbuf_tensor(name, shape, dt))
            b = {}
            b['xacc'] = sb(f'c{layer}_xacc', [128, 16, 1024], F32)
            b['act'] = sb(f'c{layer}_act', [128, 16, 1024], BF16)
            b['w'] = [sb(f'c{layer}_w{i}', [128, 16, 512], BF16) for i in range(2)]
            b['dw'] = [sb(f'c{layer}_dw{i}', [128, 4, 2048], BF16) for i in range(2)]
            b['aT'] = [sb(f'c{layer}_aT{i}', [128, 4, 1024], BF16) for i in range(2)]
            b['rl'] = [sb(f'c{layer}_rl{i}', [128, 512], F32) for i in range(2)]
            b['sq'] = [sb(f'c{layer}_sq{i}', [128, 1024], BF16) for i in range(2)]
            b['rstd'] = sb(f'c{layer}_rstd', [128, 1024], F32)
            return b

        def rmsnorm_acc(c, b, pb, sq, rx, rrstd):
            xacc = b['xacc']
            psA, rA = pb.next()
            psB, rB = pb.next()
            for kc in range(16):
                s, rs = sq.next()
                c.op('scalar', lambda e, kc=kc, s=s: e.activation(out=s[:], in_=xacc[:, kc, :], func=AF.Square),
                     reads=[rx], writes=[rs])
                c.op('tensor', lambda e, kc=kc, s=s: e.matmul(psA[:], G['ones_bf'][:], s[:, 0:512],
                                                              start=(kc == 0), stop=(kc == 15)),
                     reads=[rs, R['const']], writes=[rA] if kc == 0 else [], accs=[rA] if kc else [])
                c.op('tensor', lambda e, kc=kc, s=s: e.matmul(psB[:], G['ones_bf'][:], s[:, 512:1024],
                                                              start=(kc == 0), stop=(kc == 15)),
                     reads=[rs, R['const']], writes=[rB] if kc == 0 else [], accs=[rB] if kc else [])
            rstd_from(c, psA[:], rA, b['rstd'][:, 0:512], rrstd, D)
            rstd_from(c, psB[:], rB, b['rstd'][:, 512:1024], rrstd, D, acc=True)

        def prog(c, b):
            pb = Rot(c, G['ps'])
            sq = Rot(c, b['sq'])
            rl = Rot(c, b['rl'])
            xacc, act = b['xacc'], b['act']
            rx = [c.reg() for _ in range(16)]
            ract, rrstd = c.reg(), c.reg()
            rw = [c.reg() for _ in range(2)]
            rdw = [c.reg() for _ in range(2)]
            raT = [c.reg() for _ in range(2)]
            for half in range(2):
                t0 = half * 1024
                for kc in range(16):
                    c.dma('sync', xacc[:, kc, :], xin[kc * 128:(kc + 1) * 128, xin_off + t0:xin_off + t0 + 1024], 'ldc0',
                          reads=[xin_reg] if xin_reg is not None else [], writes=[rx[kc]])
                for kc in range(16):
                    rx[kc].w = {'ldc0': c.count['ldc0']}
                c.dma('sync', act[:], G['mixT'][:, t0:t0 + 1024].rearrange("(kc p) t -> p kc t", p=128), 'ldc1',
                      reads=[R['mixT']], writes=[ract])

                def load_o(item, slot):
                    c.dma('gpsimd', b['w'][slot][:], W_out[:, item * 512:(item + 1) * 512].rearrange("(kc p) n -> p kc n", p=128),
                          f'w{slot}', writes=[rw[slot]])

                def comp_o(item, slot):
                    w = b['w'][slot]
                    for mi in range(4):
                        m = item * 4 + mi
                        for tl in range(2):
                            ps, rp = pb.next()
                            for kc in range(16):
                                c.op('tensor', lambda e, ps=ps, kc=kc, w=w, mi=mi, tl=tl: e.matmul(
                                    ps[:], w[:, kc, mi * 128:(mi + 1) * 128], act[:, kc, tl * 512:(tl + 1) * 512],
                                    start=(kc == 0), stop=(kc == 15)),
                                    reads=[rw[slot], ract], writes=[rp] if kc == 0 else [], accs=[rp] if kc == 15 else [],
                                    sig=(kc in (0, 15)))
                            c.op('vector', lambda e, ps=ps, m=m, tl=tl: e.tensor_tensor(
                                out=xacc[:, m, tl * 512:(tl + 1) * 512], in0=xacc[:, m, tl * 512:(tl + 1) * 512], in1=ps[:], op=ALU.add),
                                reads=[rp, rx[m]], accs=[rx[m]])

                stream(list(range(4)), 2, load_o, comp_o)
                psA, rA = pb.next()
                psB, rB = pb.next()
                for kc in range(16):
                    s, rs = sq.next()
                    c.op('scalar', lambda e, kc=kc, s=s: e.activation(out=s[:], in_=xacc[:, kc, :], func=AF.Square),
                         reads=[rx[kc]], writes=[rs])
                    c.op('tensor', lambda e, kc=kc, s=s: e.matmul(psA[:], G['ones_bf'][:], s[:, 0:512],
                                                                  start=(kc == 0), stop=(kc == 15)),
                         reads=[rs, R['const']], writes=[rA] if kc == 0 else [], accs=[rA] if kc else [])
                    c.op('tensor', lambda e, kc=kc, s=s: e.matmul(psB[:], G['ones_bf'][:], s[:, 512:1024],
                                                                  start=(kc == 0), stop=(kc == 15)),
                         reads=[rs, R['const']], writes=[rB] if kc == 0 else [], accs=[rB] if kc else [])
                rstd_from(c, psA[:], rA, b['rstd'][:, 0:512], rrstd, D)
                rstd_from(c, psB[:], rB, b['rstd'][:, 512:1024], rrstd, D, acc=True)
                for kc in range(16):
                    c.op('vector', lambda e, kc=kc: e.scalar_tensor_tensor(
                        out=act[:, kc, :], in0=xacc[:, kc, :], scalar=vec(norm_name, kc), in1=b['rstd'][:],
                        op0=ALU.mult, op1=ALU.mult),
                        reads=[rx[kc], rrstd, R['const']], writes=[ract] if kc == 0 else [], accs=[ract] if kc else [])

                def load_f(F, slot):
                    c.dma('gpsimd', b['w'][slot][:], W_up[:, F * 512:(F + 1) * 512].rearrange("(kc p) n -> p kc n", p=128),
                          f'w{slot}', writes=[rw[slot]])
                    c.dma('gpsimd', b['dw'][slot][:], W_down[F * 512:(F + 1) * 512, :].rearrange("(fc p) n -> p fc n", p=128),
                          f'dw{slot}', writes=[rdw[slot]])

                def comp_f(F, slot):
                    w, dw, aT = b['w'][slot], b['dw'][slot], b['aT'][slot]
                    firsta = True
                    for fi in range(4):
                        for tl in range(2):
                            ps, rp = pb.next()
                            for kc in range(16):
                                c.op('tensor', lambda e, ps=ps, kc=kc, fi=fi, tl=tl: e.matmul(
                                    ps[:], w[:, kc, fi * 128:(fi + 1) * 128], act[:, kc, tl * 512:(tl + 1) * 512],
                                    start=(kc == 0), stop=(kc == 15)),
                                    reads=[rw[slot], ract], writes=[rp] if kc == 0 else [], accs=[rp] if kc == 15 else [],
                                    sig=(kc in (0, 15)))
                            r_, rr_ = rl.next()
                            c.op('scalar', lambda e, ps=ps, r_=r_: e.activation(out=r_[:], in_=ps[:], func=AF.Relu),
                                 reads=[rp], writes=[rr_])
                            c.op('vector', lambda e, r_=r_, fi=fi, tl=tl: e.tensor_tensor(
                                out=aT[:, fi, tl * 512:(tl + 1) * 512], in0=r_[:], in1=r_[:], op=ALU.mult),
                                reads=[rr_], writes=[raT[slot]] if firsta else [], accs=[] if firsta else [raT[slot]])
                            firsta = False
                    for o in range(16):
                        for tl in range(2):
                            ps, rp = pb.next()
                            for fi in range(4):
                                c.op('tensor', lambda e, ps=ps, fi=fi, o=o, tl=tl: e.matmul(
                                    ps[:], dw[:, fi, o * 128:(o + 1) * 128], aT[:, fi, tl * 512:(tl + 1) * 512],
                                    start=(fi == 0), stop=(fi == 3)),
                                    reads=[rdw[slot], raT[slot]], writes=[rp] if fi == 0 else [], accs=[rp] if fi == 3 else [],
                                    sig=(fi in (0, 3)))
                            c.op('vector', lambda e, ps=ps, o=o, tl=tl: e.tensor_tensor(
                                out=xacc[:, o, tl * 512:(tl + 1) * 512], in0=xacc[:, o, tl * 512:(tl + 1) * 512], in1=ps[:], op=ALU.add),
                                reads=[rp, rx[o]], accs=[rx[o]])

                stream(list(range(16)), 2, load_f, comp_f)

                if not final:
                    for kc in range(16):
                        c.dma('sync', G['x1T'][kc * 128:(kc + 1) * 128, t0:t0 + 1024], xacc[:, kc, :], 'stc',
                              reads=[rx[kc]], accs=[R['x1T']])
                    for kc in range(16):
                        rx[kc].r['stc'] = c.count['stc']
                else:
                    psA, rA = pb.next()
                    psB, rB = pb.next()
                    for kc in range(16):
                        s, rs = sq.next()
                        c.op('scalar', lambda e, kc=kc, s=s: e.activation(out=s[:], in_=xacc[:, kc, :], func=AF.Square),
                             reads=[rx[kc]], writes=[rs])
                        c.op('tensor', lambda e, kc=kc, s=s: e.matmul(psA[:], G['ones_bf'][:], s[:, 0:512],
                                                                      start=(kc == 0), stop=(kc == 15)),
                             reads=[rs, R['const']], writes=[rA] if kc == 0 else [], accs=[rA] if kc else [])
                        c.op('tensor', lambda e, kc=kc, s=s: e.matmul(psB[:], G['ones_bf'][:], s[:, 512:1024],
                                                                      start=(kc == 0), stop=(kc == 15)),
                             reads=[rs, R['const']], writes=[rB] if kc == 0 else [], accs=[rB] if kc else [])
                    rstd_from(c, psA[:], rA, b['rstd'][:, 0:512], rrstd, D)
                    rstd_from(c, psB[:], rB, b['rstd'][:, 512:1024], rrstd, D, acc=True)
                    for kc in range(16):
                        c.op('vector', lambda e, kc=kc: e.scalar_tensor_tensor(
                            out=xacc[:, kc, :], in0=xacc[:, kc, :], scalar=vec('final_norm', kc), in1=b['rstd'][:],
                            op0=ALU.mult, op1=ALU.mult),
                            reads=[rrstd, R['const']], writes=[rx[kc]])
                        c.dma('sync', G['outT'][kc * 128:(kc + 1) * 128, t0:t0 + 1024], xacc[:, kc, :], 'stc',
                              reads=[rx[kc]], accs=[R['outT']])
                    for kc in range(16):
                        rx[kc].r['stc'] = c.count['stc']

        return alloc, prog

    if LAST_PHASE >= 3:
        a, p = make_mixmlp(0, G['w_out_e'], G['w_up0'], G['w_down0'], G['xT'], 16, None, 'mlp_norm0', False)
        c.run_phase(a, p)

    def pd_alloc(nc, st):
        sb = lambda name, shape, dt: st.enter_context(nc.sbuf_tensor(name, shape, dt))
        b = {}
        b['xc'] = [sb(f'd_xc{i}', [128, 1024], F32) for i in range(3)]
        b['act'] = sb('d_act', [128, 16, 1024], BF16)
        b['w'] = [sb(f'd_w{i}', [128, 16, 512], BF16) for i in range(3)]
        b['csb'] = sb('d_csb', [128, 4, 1024], F32)
        b['cq'] = sb('d_cq', [128, 4, 1024], F32)
        b['ckv'] = sb('d_ckv', [128, 2, 1024], F32)
        b['qn'] = sb('d_qn', [128, 4, 1024], BF16)
        b['kvn'] = sb('d_kvn', [128, 2, 1024], BF16)
        b['s32'] = [sb(f'd_s32{i}', [128, 512], F32) for i in range(3)]
        b['s16'] = [sb(f'd_s16{i}', [128, 512], BF16) for i in range(3)]
        b['sq'] = [sb(f'd_sq{i}', [128, 1024], BF16) for i in range(2)]
        b['rstd'] = sb('d_rstd', [128, 1024], F32)
        b['rope'] = sb('d_rope', [64, 2, 2048], F32)
        b['t1'] = [sb(f'd_t1{i}', [64, 512], F32) for i in range(2)]
        b['t2'] = [sb(f'd_t2{i}', [64, 512], F32) for i in range(2)]
        b['brow'] = sb('d_brow', [1, 2048], F32)
        return b

    def pd_prog(c, b):
        pb = Rot(c, G['ps'])
        xcr = Rot(c, b['xc'])
        sq = Rot(c, b['sq'])
        s32 = Rot(c, b['s32'])
        s16 = Rot(c, b['s16'])
        t1r = Rot(c, b['t1'])
        t2r = Rot(c, b['t2'])
        act = b['act']
        ract, rrstd, rcsb, rcq, rckv, rqn, rkvn, rrope, rbrow = [c.reg() for _ in range(9)]
        rw = [c.reg() for _ in range(3)]
        c.dma('sync', b['rope'][:], G['rope'].rearrange("p (a n) -> p a n", a=2), 'ldd9', writes=[rrope])

        def small_norm(src, rsrc, nch, gname, dst, rdst):
            psA, rA = pb.next()
            psB, rB = pb.next()
            for kc in range(nch):
                s, rs = sq.next()
                c.op('scalar', lambda e, kc=kc, s=s: e.activation(out=s[:], in_=src[:, kc, :], func=AF.Square),
                     reads=[rsrc], writes=[rs])
                c.op('tensor', lambda e, kc=kc, s=s: e.matmul(psA[:], G['ones_bf'][:], s[:, 0:512],
                                                              start=(kc == 0), stop=(kc == nch - 1)),
                     reads=[rs, R['const']], writes=[rA] if kc == 0 else [], accs=[rA] if kc else [])
                c.op('tensor', lambda e, kc=kc, s=s: e.matmul(psB[:], G['ones_bf'][:], s[:, 512:1024],
                                                              start=(kc == 0), stop=(kc == nch - 1)),
                     reads=[rs, R['const']], writes=[rB] if kc == 0 else [], accs=[rB] if kc else [])
            rstd_from(c, psA[:], rA, b['rstd'][:, 0:512], rrstd, nch * 128)
            rstd_from(c, psB[:], rB, b['rstd'][:, 512:1024], rrstd, nch * 128, acc=True)
            for kc in range(nch):
                c.op('vector', lambda e, kc=kc: e.scalar_tensor_tensor(
                    out=dst[:, kc, :], in0=src[:, kc, :], scalar=vec(gname, kc), in1=b['rstd'][:],
                    op0=ALU.mult, op1=ALU.mult),
                    reads=[rsrc, rrstd, R['const']], writes=[rdst] if kc == 0 else [], accs=[rdst] if kc else [])

        def rope_out(psx, rpx, psy, rpy, tok0, dst_ap, dst_reg):
            t1, rt1 = t1r.next()
            t2, rt2 = t2r.next()
            c.op('vector', lambda e: e.tensor_tensor(out=t1[:], in0=psx[0:64, :], in1=b['rope'][:, 0, tok0:tok0 + 512], op=ALU.mult),
                 reads=[rpx, rrope], writes=[rt1])
            c.op('vector', lambda e: e.tensor_tensor(out=t2[:], in0=psy[0:64, :], in1=b['rope'][:, 1, tok0:tok0 + 512], op=ALU.mult),
                 reads=[rpy, rrope], writes=[rt2])
            si, s, rs = s16.nexti()
            c.op('vector', lambda e: e.tensor_tensor(out=s[0:64, :], in0=t1[:], in1=t2[:], op=ALU.add),
                 reads=[rt1, rt2], writes=[rs])
            c.dma('sync', dst_ap, s[0:64, :], f'sd16_{si}', reads=[rs], accs=[dst_reg])

        for half in range(2):
            t0 = half * 1024
            psA, rA = pb.next()
            psB, rB = pb.next()
            for kc in range(16):
                xi, xc, rxc = xcr.nexti()
                c.dma('sync', xc[:], G['x1T'][kc * 128:(kc + 1) * 128, t0:t0 + 1024], f'ldd0{xi}', reads=[R['x1T']], writes=[rxc])
                s, rs = sq.next()
                c.op('scalar', lambda e, xc=xc, s=s: e.activation(out=s[:], in_=xc[:], func=AF.Square), reads=[rxc], writes=[rs])
                c.op('tensor', lambda e, kc=kc, s=s: e.matmul(psA[:], G['ones_bf'][:], s[:, 0:512], start=(kc == 0), stop=(kc == 15)),
                     reads=[rs, R['const']], writes=[rA] if kc == 0 else [], accs=[rA] if kc else [])
                c.op('tensor', lambda e, kc=kc, s=s: e.matmul(psB[:], G['ones_bf'][:], s[:, 512:1024], start=(kc == 0), stop=(kc == 15)),
                     reads=[rs, R['const']], writes=[rB] if kc == 0 else [], accs=[rB] if kc else [])
            rstd_from(c, psA[:], rA, b['rstd'][:, 0:512], rrstd, D)
            rstd_from(c, psB[:], rB, b['rstd'][:, 512:1024], rrstd, D, acc=True)
            for kc in range(16):
                xi, xc, rxc = xcr.nexti()
                c.dma('sync', xc[:], G['x1T'][kc * 128:(kc + 1) * 128, t0:t0 + 1024], f'ldd0{xi}', reads=[R['x1T']], writes=[rxc])
                c.op('vector', lambda e, kc=kc, xc=xc: e.scalar_tensor_tensor(
                    out=act[:, kc, :], in0=xc[:], scalar=vec('mix_norm_o', kc), in1=b['rstd'][:], op0=ALU.mult, op1=ALU.mult),
                    reads=[rxc, rrstd, R['const']], writes=[ract] if kc == 0 else [], accs=[ract] if kc else [])

            groups = [(0, 'b', 0), (512, 'b', 1), (1024, 'c', 0), (2048, 'hc', 0), (1536, 'c', 1), (2560, 'hc', 1),
                      (3072, 'cq', 0), (3584, 'kv', 0)]

            def load_w(item, slot):
                col0, kind, gi = item
                n = 384 if kind == 'kv' else 512
                c.dma('gpsimd', b['w'][slot][:, :, 0:n], G['w_in_o'][:, col0:col0 + n].rearrange("(kc p) n -> p kc n", p=128),
                      f'w{slot}', writes=[rw[slot]])

            def mm16(ps, rp, w, slot, c0, m, tl):
                for kc in range(16):
                    c.op('tensor', lambda e, kc=kc: e.matmul(
                        ps[0:m, :], w[:, kc, c0:c0 + m], act[:, kc, tl * 512:(tl + 1) * 512],
                        start=(kc == 0), stop=(kc == 15)),
                        reads=[rw[slot], ract], writes=[rp] if kc == 0 else [], accs=[rp] if kc == 15 else [],
                        sig=(kc in (0, 15)))

            def compute(item, slot):
                col0, kind, gi = item
                w = b['w'][slot]
                if kind == 'kv':
                    for mi in range(2):
                        for tl in range(2):
                            ps, rp = pb.next()
                            mm16(ps, rp, w, slot, mi * 128, 128, tl)
                            c.op('scalar', lambda e, ps=ps, mi=mi, tl=tl: e.activation(
                                out=b['ckv'][:, mi, tl * 512:(tl + 1) * 512], in_=ps[:], func=AF.Copy),
                                reads=[rp], writes=[rckv] if (mi == 0 and tl == 0) else [], accs=[] if (mi == 0 and tl == 0) else [rckv])
                    for tl in range(2):
                        psx, rpx = pb.next()
                        mm16(psx, rpx, w, slot, 256, 64, tl)
                        psy, rpy = pb.next()
                        mm16(psy, rpy, w, slot, 320, 64, tl)
                        tok0 = t0 + tl * 512
                        rope_out(psx, rpx, psy, rpy, tok0, G['ag2in'][1024:1088, tok0:tok0 + 512], R['ag2in'])
                    return
                for mi in range(4):
                    ch = gi * 4 + mi
                    for tl in range(2):
                        tok0 = t0 + tl * 512
                        ps, rp = pb.next()
                        mm16(ps, rp, w, slot, mi * 128, 128, tl)
                        if kind == 'b':
                            si, s, rs = s32.nexti()
                            c.op('scalar', lambda e, ps=ps, s=s: e.activation(out=s[:], in_=ps[:], func=AF.Copy),
                                 reads=[rp], writes=[rs])
                            c.dma('sync', G['bT'][ch * 128:(ch + 1) * 128, tok0:tok0 + 512], s[:], f'sd32_{si}', reads=[rs], accs=[R['bT']])
                        elif kind == 'c':
                            fw = (mi == 0 and tl == 0)
                            c.op('scalar', lambda e, ps=ps, mi=mi, tl=tl: e.activation(
                                out=b['csb'][:, mi, tl * 512:(tl + 1) * 512], in_=ps[:], func=AF.Copy),
                                reads=[rp], writes=[rcsb] if fw else [], accs=[] if fw else [rcsb])
                        elif kind == 'hc':
                            si, s, rs = s32.nexti()
                            c.op('vector', lambda e, ps=ps, s=s, mi=mi, tl=tl: e.tensor_tensor(
                                out=s[:], in0=b['csb'][:, mi, tl * 512:(tl + 1) * 512], in1=ps[:], op=ALU.mult),
                                reads=[rp, rcsb], writes=[rs])
                            c.dma('sync', G['vT'][ch * 128:(ch + 1) * 128, tok0:tok0 + 512], s[:], f'sd32_{si}', reads=[rs], accs=[R['vT']])
                            for (cond, col, off) in ((half == 0 and tl == 0, 0, 0), (half == 1 and tl == 1, 511, 1024)):
                                if cond:
                                    psb, rpb = pb.next()
                                    c.op('tensor', lambda e, psb=psb, s=s, col=col: e.matmul(
                                        psb[0:1, 0:128], s[:, col:col + 1], G['ident32'][:], start=True, stop=True),
                                        reads=[rs, R['const']], writes=[rpb])
                                    c.op('scalar', lambda e, psb=psb, off=off, ch=ch: e.activation(
                                        out=b['brow'][0:1, off + ch * 128:off + (ch + 1) * 128], in_=psb[0:1, 0:128], func=AF.Copy),
                                        reads=[rpb], accs=[rbrow])
                        elif kind == 'cq':
                            fw = (mi == 0 and tl == 0)
                            c.op('scalar', lambda e, ps=ps, mi=mi, tl=tl: e.activation(
                                out=b['cq'][:, mi, tl * 512:(tl + 1) * 512], in_=ps[:], func=AF.Copy),
                                reads=[rp], writes=[rcq] if fw else [], accs=[] if fw else [rcq])

            stream(groups, 3, load_w, compute)

            small_norm(b['cq'], rcq, 4, 'q_norm_g', b['qn'], rqn)
            small_norm(b['ckv'], rckv, 2, 'kv_norm_g', b['kvn'], rkvn)
            wq = b['w'][0][:].rearrange("p a n -> p (a n)").rearrange("p (kc n) -> p kc n", kc=4)
            wkv = b['w'][1][:].rearrange("p a n -> p (a n)")[:, 0:4096].rearrange("p (kc n) -> p kc n", kc=2)
            c.dma('gpsimd', wq, G['w_uq3'].rearrange("(kc p) n -> p kc n", p=128), 'w0', writes=[rw[0]])
            c.dma('gpsimd', wkv, G['w_ukv2'].rearrange("(kc p) n -> p kc n", p=128), 'w1', writes=[rw[1]])

            def mmq(ps, rp, c0, m, tl):
                for kc in range(4):
                    c.op('tensor', lambda e, kc=kc: e.matmul(
                        ps[0:m, :], wq[:, kc, c0:c0 + m], b['qn'][:, kc, tl * 512:(tl + 1) * 512], start=(kc == 0), stop=(kc == 3)),
                        reads=[rw[0], rqn], writes=[rp] if kc == 0 else [], accs=[rp] if kc == 3 else [], sig=(kc in (0, 3)))

            for h in range(8):
                for tl in range(2):
                    tok0 = t0 + tl * 512
                    ps, rp = pb.next()
                    mmq(ps, rp, h * 128, 128, tl)
                    si, s, rs = s16.nexti()
                    c.op('scalar', lambda e, ps=ps, s=s: e.activation(out=s[:], in_=ps[:], func=AF.Copy), reads=[rp], writes=[rs])
                    c.dma('sync', G['qT'][h * 128:(h + 1) * 128, tok0:tok0 + 512], s[:], f'sd16_{si}', reads=[rs], accs=[R['qT']])
                    psx, rpx = pb.next()
                    mmq(psx, rpx, 1024 + h * 64, 64, tl)
                    psy, rpy = pb.next()
                    mmq(psy, rpy, 1536 + h * 64, 64, tl)
                    rope_out(psx, rpx, psy, rpy, tok0, G['qT'][1024 + h * 64:1024 + (h + 1) * 64, tok0:tok0 + 512], R['qT'])
            for h in range(8):
                for tl in range(2):
                    tok0 = t0 + tl * 512
                    ps, rp = pb.next()
                    for kc in range(2):
                        c.op('tensor', lambda e, kc=kc, ps=ps, h=h, tl=tl: e.matmul(
                            ps[:], wkv[:, kc, h * 128:(h + 1) * 128], b['kvn'][:, kc, tl * 512:(tl + 1) * 512], start=(kc == 0), stop=(kc == 1)),
                            reads=[rw[1], rkvn], writes=[rp] if kc == 0 else [], accs=[rp] if kc else [])
                    si, s, rs = s16.nexti()
                    c.op('scalar', lambda e, ps=ps, s=s: e.activation(out=s[:], in_=ps[:], func=AF.Copy), reads=[rp], writes=[rs])
                    c.dma('sync', G['ag2in'][h * 128:(h + 1) * 128, tok0:tok0 + 512], s[:], f'sd16_{si}', reads=[rs], accs=[R['ag2in']])
            for tb in range(8):
                blk = half * 8 + tb
                for nh in range(2):
                    ps, rp = pb.next()
                    for kc in range(2):
                        c.op('tensor', lambda e, kc=kc, ps=ps, tb=tb, nh=nh: e.matmul(
                            ps[:], b['kvn'][:, kc, tb * 128:(tb + 1) * 128], wkv[:, kc, 1024 + nh * 512:1024 + (nh + 1) * 512],
                            start=(kc == 0), stop=(kc == 1)),
                            reads=[rw[1], rkvn], writes=[rp] if kc == 0 else [], accs=[rp] if kc else [])
                    si, s, rs = s16.nexti()
                    c.op('vector', lambda e, ps=ps, s=s: e.tensor_copy(out=s[:], in_=ps[:]), reads=[rp], writes=[rs])
                    r0 = 1088 + nh * 512
                    dst = G['ag2in'][r0:r0 + 512, blk * 128:(blk + 1) * 128].rearrange("(hh p) d -> p hh d", p=128)
                    c.dma('sync', dst, s[:].rearrange("p (hh d) -> p hh d", hh=4), f'sd16_{si}', reads=[rs], accs=[R['ag2in']])
        c.dma('sync', G['ag3in'][:], b['brow'][:], 'stbrow', reads=[rbrow], accs=[R['ag3in']])

    def pd_prog_all(c, b):
        pd_prog(c, b)

    if LAST_PHASE >= 4:
        c.run_phase(pd_alloc, pd_prog_all)
    if LAST_PHASE >= 5:
        ag_phase('ag2in', 'ag2out', 'cc2')
        ag_phase('ag3in', 'ag3out', 'cc3')

    def pe_alloc(nc, st):
        sb = lambda name, shape, dt: st.enter_context(nc.sbuf_tensor(name, shape, dt))
        b = {}
        b['KT'] = [sb(f'e_KT{i}', [128, 16384], BF16) for i in range(2)]
        b['V'] = [sb(f'e_V{i}', [128, 8, 2048], BF16) for i in range(2)]
        b['kr'] = sb('e_kr', [64, 16384], BF16)
        b['qn'] = [sb(f'e_qn{i}', [128, 2048], BF16) for i in range(2)]
        b['qr'] = [sb(f'e_qr{i}', [64, 2048], BF16) for i in range(2)]
        b['P'] = [sb(f'e_P{i}', [128, 512], BF16) for i in range(4)]
        b['rd'] = sb('e_rd', [128, 512], F32)
        b['o'] = [sb(f'e_o{i}', [128, 512], BF16) for i in range(2)]
        return b

    def pe0_alloc(nc, st):
        sb = lambda name, shape, dt: st.enter_context(nc.sbuf_tensor(name, shape, dt))
        b = {}
        b['bt'] = sb('e_bt', [8, 2048], F32)
        b['sel'] = sb('e_sel', [8, 2], F32)
        b['vt'] = sb('e_vt', [128, 2050], F32)
        b['bb'] = sb('e_bb', [128, 2048], F32)
        b['ca'] = sb('e_ca', [128, 2048], F32)
        b['yc'] = sb('e_yc', [128, 2048], BF16)
        return b

    def pe0_prog(c, b):
        rbt, rsel, rvt, rbb, rca, ryc = [c.reg() for _ in range(6)]
        sb_rot = Rot(c, G['ps'][0:4])
        pbs = sb_rot
        c.dma('sync', b['bt'][:], G['ag3out'][:], 'lde0', reads=[R['ag3out']], writes=[rbt])
        c.dma('sync', b['sel'][:], G['sel'][:], 'lde8', writes=[rsel])
        for j in range(8):
            c.dma('sync', b['vt'][:, 1:2049], G['vT'][j * 128:(j + 1) * 128, :], 'lde1', reads=[R['vT']], writes=[rvt])
            c.dma('sync', b['bb'][:], G['bT'][j * 128:(j + 1) * 128, :], 'lde2', reads=[R['bT']], writes=[rbb])
            ps, rp = pbs.next()
            c.op('tensor', lambda e, ps=ps, j=j: e.matmul(ps[:, 0:1], b['bt'][0:8, 1024 + j * 128:1024 + (j + 1) * 128], b['sel'][0:8, 0:1],
                                                          start=True, stop=True), reads=[rbt, rsel], writes=[rp])
            c.op('tensor', lambda e, ps=ps, j=j: e.matmul(ps[:, 1:2], b['bt'][0:8, j * 128:(j + 1) * 128], b['sel'][0:8, 1:2],
                                                          start=True, stop=True), reads=[rbt, rsel], accs=[rp])
            c.op('scalar', lambda e, ps=ps: e.activation(out=b['vt'][:, 0:1], in_=ps[:, 0:1], func=AF.Copy), reads=[rp], accs=[rvt])
            c.op('scalar', lambda e, ps=ps: e.activation(out=b['vt'][:, 2049:2050], in_=ps[:, 1:2], func=AF.Copy), reads=[rp], accs=[rvt])
            c.op('vector', lambda e, j=j: e.tensor_scalar(out=b['ca'][:], in0=b['vt'][:, 0:2048], scalar1=vec('conv_c_w', 0 * 8 + j),
                                                          scalar2=None, op0=ALU.mult), reads=[rvt, R['const']], writes=[rca])
            for k in (1, 2):
                c.op('vector', lambda e, j=j, k=k: e.scalar_tensor_tensor(
                    out=b['ca'][:], in0=b['vt'][:, k:k + 2048], scalar=vec('conv_c_w', k * 8 + j), in1=b['ca'][:],
                    op0=ALU.mult, op1=ALU.add), reads=[rvt, rca], accs=[rca])
            c.op('vector', lambda e: e.tensor_tensor(out=b['yc'][:], in0=b['ca'][:], in1=b['bb'][:], op=ALU.mult),
                 reads=[rca, rbb], writes=[ryc])
            c.dma('sync', G['mixT'][j * 128:(j + 1) * 128, :], b['yc'][:], 'ste0', reads=[ryc], accs=[R['mixT']])

    def pe_prog(c, b):
        sb_rot = Rot(c, G['ps'][0:4])
        SCALE = 192.0 ** -0.5
        rKT = [c.reg() for _ in range(2)]
        rV = [c.reg() for _ in range(2)]
        rqn = [c.reg() for _ in range(2)]
        rqr = [c.reg() for _ in range(2)]
        rkr, rrd = c.reg(), c.reg()
        Pr = Rot(c, b['P'])
        orot = Rot(c, b['o'])
        obanks = Rot(c, G['ps'][4:6])
        dbanks = Rot(c, G['ps'][6:8])
        for r in range(8):
            c.dma('sync', b['kr'][:, r * 2048:(r + 1) * 2048], G['ag2out'][r * 2112 + 1024:r * 2112 + 1088, :], 'lde3',
                  reads=[R['ag2out']], writes=[rkr] if r == 0 else [], accs=[rkr] if r else [])

        def load(h, slot):
            for r in range(8):
                c.dma('sync', b['KT'][slot][:, r * 2048:(r + 1) * 2048], G['ag2out'][r * 2112 + h * 128:r * 2112 + (h + 1) * 128, :],
                      f'lde4{slot}', reads=[R['ag2out']], writes=[rKT[slot]] if r == 0 else [], accs=[rKT[slot]] if r else [])
                c.dma('sync', b['V'][slot][:, r, :], G['ag2out'][r * 2112 + 1088 + h * 128:r * 2112 + 1088 + (h + 1) * 128, :],
                      f'lde5{slot}', reads=[R['ag2out']], writes=[rV[slot]] if r == 0 else [], accs=[rV[slot]] if r else [])
            c.dma('sync', b['qn'][slot][:], G['qT'][h * 128:(h + 1) * 128, :], f'lde6{slot}', reads=[R['qT']], writes=[rqn[slot]])
            c.dma('sync', b['qr'][slot][:], G['qT'][1024 + h * 64:1024 + (h + 1) * 64, :], f'lde7{slot}', reads=[R['qT']], writes=[rqr[slot]])

        def compute(h, slot):
            KT, V, qn, qr = b['KT'][slot], b['V'][slot], b['qn'][slot], b['qr'][slot]
            for qt in range(4):
                q0 = qt * 512
                ops_, rO = obanks.next()
                dps_, rD = dbanks.next()

                def s_mm(kb):
                    ps, rp = sb_rot.next()
                    c.op('tensor', lambda e: e.matmul(ps[:], KT[:, kb * 128:(kb + 1) * 128], qn[:, q0:q0 + 512], start=True, stop=False),
                         reads=[rKT[slot], rqn[slot]], writes=[rp])
                    c.op('tensor', lambda e: e.matmul(ps[:], b['kr'][0:64, kb * 128:(kb + 1) * 128], qr[0:64, q0:q0 + 512], start=False, stop=True),
                         reads=[rkr, rqr[slot]], accs=[rp])
                    return ps, rp

                pend = [s_mm(kb) for kb in range(3)]
                for kb in range(128):
                    if kb + 3 < 128:
                        pend.append(s_mm(kb + 3))
                    ps, rp = pend.pop(0)
                    P, rP = Pr.next()
                    c.op('scalar', lambda e, ps=ps, P=P: e.activation(out=P[:], in_=ps[:], func=AF.Exp, scale=SCALE),
                         reads=[rp], writes=[rP])
                    r_, bb = kb // 16, kb % 16
                    c.op('tensor', lambda e, P=P, r_=r_, bb=bb, kb=kb: e.matmul(
                        ops_[:], V[:, r_, bb * 128:(bb + 1) * 128], P[:], start=(kb == 0), stop=(kb == 127)),
                        reads=[rV[slot], rP], writes=[rO] if kb == 0 else [], accs=[rO] if kb == 127 else [],
                        sig=(kb in (0, 127)))
                    c.op('tensor', lambda e, P=P, kb=kb: e.matmul(
                        dps_[:], G['ones_bf'][:], P[:], start=(kb == 0), stop=(kb == 127)),
                        reads=[rP, R['const']], writes=[rD] if kb == 0 else [], accs=[rD] if kb else [])
                c.op('vector', lambda e: e.reciprocal(out=b['rd'][:], in_=dps_[:]), reads=[rD], writes=[rrd])
                oi, o, ro = orot.nexti()
                c.op('vector', lambda e, o=o: e.tensor_tensor(out=o[:], in0=ops_[:], in1=b['rd'][:], op=ALU.mult),
                     reads=[rO, rrd], writes=[ro])
                c.dma('sync', G['mixT'][1024 + h * 128:1024 + (h + 1) * 128, q0:q0 + 512], o[:], f'ste1{oi}', reads=[ro], accs=[R['mixT']])

        stream(list(range(8)), 2, load, compute)

    if LAST_PHASE >= 6:
        c.run_phase(pe0_alloc, pe0_prog)
        c.run_phase(pe_alloc, pe_prog)

    if LAST_PHASE >= 7:
        a, p = make_mixmlp(1, G['w_out_o'], G['w_up1'], G['w_down1'], G['x1T'], 0, R['x1T'], 'mlp_norm1', True)
        c.run_phase(a, p)

    if DEBUG:
        def dbg_alloc(nc, st):
            return {}

        def dbg_prog(c, b):
            for kc in range(16):
                c.dma('sync', G['dbg'][kc * 128:(kc + 1) * 128, :], G['mixT'][kc * 128:(kc + 1) * 128, :], 'dbg1', reads=[R['mixT']], accs=[R['dbg']])
                c.dma('sync', G['dbg2'][kc * 128:(kc + 1) * 128, :], G['x1T'][kc * 128:(kc + 1) * 128, :], 'dbg1', reads=[R['x1T']], accs=[R['dbg']])
        c.run_phase(dbg_alloc, dbg_prog)

    stack.close()
    return nc


_NC_CACHE = {}


def _chunked(v, nch):
    return np.ascontiguousarray(np.asarray(v, np.float32).reshape(nch, 128).T)


def _host_consts():
    n = np.arange(128, dtype=np.float64)
    ang = 2.0 * np.pi * np.outer(n, n) / 128.0
    C1, S1 = np.cos(ang), np.sin(ang)
    dft = np.concatenate([C1, -S1, S1, C1], axis=1).astype(np.float32)
    return dft


def _prep_inputs(inp):
    f32 = np.float32
    x = np.asarray(inp['x'], f32)[0]
    xT = np.zeros((D, S + 32), f32)
    xT[:, 16:16 + S] = x.T
    w_in_o = np.asarray(inp['w_in_o'], f32)[0]
    kr = w_in_o[:, 3840:3904]
    w_in_o2 = np.concatenate([w_in_o, kr[:, 32:64], kr[:, 0:32]], axis=1)
    w_uq = np.asarray(inp['w_uq'], f32)[0].reshape(512, 8, 192)
    qn = w_uq[:, :, 0:128].reshape(512, 1024)
    qr = w_uq[:, :, 128:192]
    qrs = np.concatenate([qr[:, :, 32:64], qr[:, :, 0:32]], axis=2)
    w_uq3 = np.ascontiguousarray(np.concatenate([qn, qr.reshape(512, 512), qrs.reshape(512, 512)], axis=1))
    w_ukv = np.asarray(inp['w_ukv'], f32)[0].reshape(256, 8, 256)
    w_ukv2 = np.ascontiguousarray(np.concatenate([w_ukv[:, :, 0:128].reshape(256, 1024),
                                                  w_ukv[:, :, 128:256].reshape(256, 1024)], axis=1))
    vecs = np.zeros((128, NV), f32)

    def put(name, arr, nch):
        vecs[:, VOFF[name]:VOFF[name] + nch] = _chunked(arr, nch)
    put('mix_norm_e', inp['mix_norm_e'][0], 16)
    put('mlp_norm0', inp['mlp_norm'][0], 16)
    put('mix_norm_o', inp['mix_norm_o'][0], 16)
    put('mlp_norm1', inp['mlp_norm'][1], 16)
    put('final_norm', inp['final_norm'], 16)
    put('conv_a_b', inp['conv_a_b'][0], 8)
    put('ln_a_g', inp['ln_a_g'][0], 8)
    put('ln_a_b', inp['ln_a_b'][0], 8)
    caw = np.asarray(inp['conv_a_w'], f32)[0]
    for k in range(31):
        vecs[:, VOFF['conv_a_w'] + k * 8:VOFF['conv_a_w'] + (k + 1) * 8] = _chunked(caw[k], 8)
    ccw = np.asarray(inp['conv_c_w'], f32)[0]
    for k in range(3):
        vecs[:, VOFF['conv_c_w'] + k * 8:VOFF['conv_c_w'] + (k + 1) * 8] = _chunked(ccw[k], 8)
    put('q_norm_g', inp['q_norm_g'][0], 4)
    put('kv_norm_g', inp['kv_norm_g'][0], 2)
    dft = _host_consts()
    common = {
        'w_in_e': np.ascontiguousarray(np.asarray(inp['w_in_e'], f32)[0]),
        'w_out_e': np.ascontiguousarray(np.asarray(inp['w_out_e'], f32)[0]),
        'w_in_o': np.ascontiguousarray(w_in_o2),
        'w_uq3': w_uq3, 'w_ukv2': w_ukv2,
        'w_out_o': np.ascontiguousarray(np.asarray(inp['w_out_o'], f32)[0]),
        'w_up0': np.ascontiguousarray(np.asarray(inp['w_up'], f32)[0]),
        'w_up1': np.ascontiguousarray(np.asarray(inp['w_up'], f32)[1]),
        'w_down0': np.ascontiguousarray(np.asarray(inp['w_down'], f32)[0]),
        'w_down1': np.ascontiguousarray(np.asarray(inp['w_down'], f32)[1]),
        'vecs': vecs, 'dft': dft,
    }
    inv = 1.0 / (10000.0 ** (np.arange(0, 64, 2, dtype=np.float32) / 64.0))
    in_maps = []
    n2 = np.arange(128, dtype=np.float64)[:, None]
    nrm = 2.0 ** -10.5
    for cidx in range(NCORES):
        m = dict(common)
        m['xT'] = np.ascontiguousarray(xT[:, cidx * T:cidx * T + T + 32])
        k1 = np.arange(128)[:, None]
        k2 = np.arange(16)[None, :]
        kk = (cidx * T + 128 * k2 + k1).reshape(1, 2048).astype(np.float64)
        ang = 2.0 * np.pi * ((n2 * kk) % S) / S
        m['tcs'] = np.concatenate([np.cos(ang) * nrm, np.sin(ang) * nrm], axis=1).astype(np.float32)
        pos = np.arange(cidx * T, (cidx + 1) * T, dtype=np.float32)
        a = (pos[:, None] * inv[None, :]).T
        cs, sn = np.cos(a), np.sin(a)
        CC = np.concatenate([cs, cs], axis=0)
        SS = np.concatenate([-sn, sn], axis=0)
        m['rope'] = np.ascontiguousarray(np.concatenate([CC, SS], axis=1).astype(np.float32))
        sel = np.zeros((8, 2), np.float32)
        if cidx > 0:
            sel[cidx - 1, 0] = 1.0
        if cidx < NCORES - 1:
            sel[cidx + 1, 1] = 1.0
        m['sel'] = sel
        m['ident'] = np.eye(128, dtype=np.float32)
        in_maps.append(m)
    return in_maps


def kernel(**inputs):
    if 'nc' not in _NC_CACHE:
        _NC_CACHE['nc'] = build_program()
    nc = _NC_CACHE['nc']
    in_maps = _prep_inputs(inputs)
    res = run_bass_kernel_spmd(nc, in_maps, core_ids=list(range(NCORES)))
    out = np.concatenate([np.asarray(res.results[i]['outT']).T for i in range(NCORES)], axis=0)
    if DEBUG:
        _NC_CACHE['dbg'] = [np.asarray(res.results[i]['dbg']) for i in range(NCORES)]
        _NC_CACHE['dbg2'] = [np.asarray(res.results[i]['dbg2']) for i in range(NCORES)]
    return np.ascontiguousarray(out[None].astype(np.float32))
```

```python
import numpy as np
from contextlib import ExitStack
import concourse.bass as bass
import concourse.mybir as mybir
from concourse.bass_utils import run_bass_kernel_spmd

F32 = mybir.dt.float32
BF16 = mybir.dt.bfloat16
AF = mybir.ActivationFunctionType
ALU = mybir.AluOpType

NCORES = 8
S = 16384
T = 2048
D = 2048
EPS = 1e-6
ENG_NAMES = ['sync', 'scalar', 'vector', 'gpsimd', 'tensor']
LAST_PHASE = 99
DEBUG = False
DEBUG_SRC = ('mixT', 'mixT')


class Reg:
    __slots__ = ('w', 'r')

    def __init__(self):
        self.w = {}
        self.r = {}


class Rot:
    def __init__(self, c, bufs):
        self.bufs = bufs
        self.regs = [c.reg() for _ in bufs]
        self.i = 0

    def next(self):
        i = self.i % len(self.bufs)
        self.i += 1
        return self.bufs[i], self.regs[i]

    def nexti(self):
        i = self.i % len(self.bufs)
        self.i += 1
        return i, self.bufs[i], self.regs[i]


class Ctx:
    def __init__(self, nc, stack):
        self.nc = nc
        self.stack = stack
        self.sems = {}
        self.cur = None
        self.eng = None
        self.count = {}
        self.waited = {}
        self.waited_by = {n: {} for n in ENG_NAMES}
        self.pregs = []

    def reg(self):
        return Reg()

    def preg(self):
        r = Reg()
        self.pregs.append(r)
        return r

    def snapshot(self):
        return (dict(self.count), [(dict(r.w), dict(r.r)) for r in self.pregs])

    def restore(self, snap):
        self.count = dict(snap[0])
        for r, (w, rr) in zip(self.pregs, snap[1]):
            r.w = dict(w)
            r.r = dict(rr)

    def _deps(self, reads, writes, accs, extra):
        need = {}
        for r in reads:
            for s, v in r.w.items():
                if need.get(s, 0) < v:
                    need[s] = v
        for r in writes:
            for d in (r.w, r.r):
                for s, v in d.items():
                    if need.get(s, 0) < v:
                        need[s] = v
        for r in accs:
            for s, v in r.r.items():
                if need.get(s, 0) < v:
                    need[s] = v
        for t in extra:
            if t is not None and need.get(t[0], 0) < t[1]:
                need[t[0]] = t[1]
        return need

    def _emit_waits(self, need, skip_sem=None):
        for s, v in need.items():
            if s == skip_sem:
                continue
            if self.waited.get(s, 0) >= v:
                continue
            self.eng.wait_ge(self.sems[s], v)
            self.waited[s] = v

    def _update(self, tok, reads, writes, accs):
        s, v = tok
        for r in reads:
            if r.r.get(s, 0) < v:
                r.r[s] = v
        for r in writes:
            r.w = {s: v}
            r.r = {}
        for r in accs:
            if r.w.get(s, 0) < v:
                r.w[s] = v

    def op(self, engname, fn, reads=(), writes=(), accs=(), extra=(), sig=True):
        sname = 'm_' + engname
        if not sig:
            if engname == self.cur:
                need = self._deps(reads, writes, accs, extra)
                self._emit_waits(need, skip_sem=sname if engname == 'tensor' else None)
                fn(self.eng)
            return None
        need = self._deps(reads, writes, accs, extra)
        for r in accs:
            for s_, v_ in r.w.items():
                if s_ != sname and need.get(s_, 0) < v_:
                    need[s_] = v_
        self.count[sname] = self.count.get(sname, 0) + 1
        tok = (sname, self.count[sname])
        if engname == self.cur:
            self._emit_waits(need, skip_sem=sname if engname == 'tensor' else None)
            ins = fn(self.eng)
            ins.then_inc(self.sems[sname], 1)
        self._update(tok, reads, writes, accs)
        return tok

    def dma(self, qname, out, in_, sem, reads=(), writes=(), accs=(), extra=(), fn=None, inc=16):
        need = self._deps(reads, writes, accs, extra)
        self.count[sem] = self.count.get(sem, 0) + inc
        tok = (sem, self.count[sem])
        if qname == self.cur:
            self._emit_waits(need)
            if fn is not None:
                ins = fn(self.eng)
            else:
                ins = self.eng.dma_start(out=out, in_=in_)
            ins.then_inc(self.sems[sem], inc)
        self._update(tok, reads, writes, accs)
        return tok

    def barrier(self):
        if self.cur is not None:
            self._emit_waits(dict(self.count))

    def run_phase(self, alloc_fn, prog_fn):
        nc = self.nc
        with ExitStack() as st:
            bufs = alloc_fn(nc, st)
            snap = self.snapshot()
            self.cur = None
            self.eng = None
            prog_fn(self, bufs)
            end = self.snapshot()
            for name in sorted(self.count.keys()):
                if name not in self.sems:
                    self.sems[name] = self.stack.enter_context(nc.semaphore(name))
            with nc.Block() as block:
                for name in ENG_NAMES:
                    def body(eng, name=name):
                        self.restore(snap)
                        self.cur = name
                        self.eng = eng
                        self.waited = self.waited_by[name]
                        prog_fn(self, bufs)
                        self.barrier()
                    getattr(block, name)(body)
            self.cur = None
            self.eng = None
            self.restore(end)


def stream(items, nslots, load_fn, compute_fn):
    n = len(items)
    for i in range(min(nslots - 1, n)):
        load_fn(items[i], i % nslots)
    for i in range(n):
        j = i + nslots - 1
        if j < n:
            load_fn(items[j], j % nslots)
        compute_fn(items[i], i % nslots)


VOFF = {}
_o = 0
for _name, _n in [('mix_norm_e', 16), ('mlp_norm0', 16), ('mix_norm_o', 16), ('mlp_norm1', 16),
                  ('final_norm', 16), ('conv_a_b', 8), ('ln_a_g', 8), ('ln_a_b', 8),
                  ('conv_a_w', 31 * 8), ('conv_c_w', 3 * 8), ('q_norm_g', 4), ('kv_norm_g', 2)]:
    VOFF[_name] = _o
    _o += _n
NV = _o


def build_program():
    nc = bass.Bass("TRN2", target_bir_lowering=False)
    stack = ExitStack()
    G = {}

    def din(name, shape, dt=F32):
        G[name] = nc.dram_tensor(name, list(shape), dt, kind="ExternalInput").ap()

    din('xT', [D, T + 32])
    din('w_in_e', [D, 3072])
    din('w_out_e', [D, D])
    din('w_in_o', [D, 3968])
    din('w_uq3', [512, 2048])
    din('w_ukv2', [256, 2048])
    din('w_out_o', [D, D])
    din('w_up0', [D, 8192])
    din('w_up1', [D, 8192])
    din('w_down0', [8192, D])
    din('w_down1', [8192, D])
    din('vecs', [128, NV])
    din('dft', [128, 512])
    din('tcs', [128, 2 * 2048])
    din('rope', [64, 2 * 2048])
    din('sel', [8, 2])
    din('ident', [128, 128])
    G['outT'] = nc.dram_tensor('outT', [D, T], F32, kind="ExternalOutput").ap()
    if DEBUG:
        G['dbg'] = nc.dram_tensor('dbg', [D, T], BF16, kind="ExternalOutput").ap()
        G['dbg2'] = nc.dram_tensor('dbg2', [D, T], F32, kind="ExternalOutput").ap()
    G['mixT'] = nc.dram_tensor('mixT', [D, T], BF16).ap()
    G['agin1'] = nc.dram_tensor('agin1', [16, 262144], BF16).ap()
    G['agout1'] = nc.dram_tensor('agout1', [128, 262144], BF16).ap()
    G['x1T'] = nc.dram_tensor('x1T', [D, T], F32).ap()
    G['ag2in'] = nc.dram_tensor('ag2in', [2112, T], BF16).ap()
    G['ag2out'] = nc.dram_tensor('ag2out', [8 * 2112, T], BF16).ap()
    G['ag3in'] = nc.dram_tensor('ag3in', [1, 2048], F32).ap()
    G['ag3out'] = nc.dram_tensor('ag3out', [8, 2048], F32).ap()
    G['qT'] = nc.dram_tensor('qT', [1536, T], BF16).ap()
    G['vT'] = nc.dram_tensor('vT', [1024, T], F32).ap()
    G['bT'] = nc.dram_tensor('bT', [1024, T], F32).ap()

    def gsb(name, shape, dt):
        return stack.enter_context(nc.sbuf_tensor(name, shape, dt))

    G['vecs_sb'] = gsb('vecs_sb', [128, NV], F32)
    G['ones_bf'] = gsb('ones_bf', [128, 128], BF16)
    G['ones32'] = gsb('ones32', [128, 128], F32)
    G['dft_sb'] = gsb('dft_sb', [128, 512], BF16)
    G['ident32'] = gsb('ident32', [128, 128], F32)
    G['ps'] = [stack.enter_context(nc.psum_tensor(f'ps{i}', [128, 512], F32)) for i in range(8)]

    c = Ctx(nc, stack)
    R = {k: c.preg() for k in ['mixT', 'agin1', 'agout1', 'x1T', 'ag2in', 'ag2out', 'ag3in', 'ag3out',
                               'qT', 'vT', 'bT', 'outT', 'const', 'dbg']}

    def vec(name, i):
        o = VOFF[name] + i
        return G['vecs_sb'][:, o:o + 1]

    def p0_alloc(nc, st):
        return {}

    def p0_prog(c, b):
        c.dma('sync', G['vecs_sb'][:], G['vecs'][:], 'ld0', accs=[R['const']])
        c.dma('gpsimd', G['dft_sb'][:], G['dft'][:], 'ld1', accs=[R['const']])
        c.dma('sync', G['ident32'][:], G['ident'][:], 'ld2', accs=[R['const']])
        c.op('vector', lambda e: e.memset(G['ones_bf'][:], 1.0), accs=[R['const']])
        c.op('vector', lambda e: e.memset(G['ones32'][:], 1.0), accs=[R['const']])

    c.run_phase(p0_alloc, p0_prog)

    def rstd_from(c, ps_ap, rps, out_ap, rout, n, acc=False):
        c.op('scalar', lambda e: e.activation(out=out_ap, in_=ps_ap, func=AF.Sqrt, bias=EPS, scale=1.0 / n),
             reads=[rps], accs=[rout] if acc else [], writes=[] if acc else [rout])
        c.op('vector', lambda e: e.reciprocal(out=out_ap, in_=out_ap), reads=[rout], accs=[rout])

    def pa_alloc(nc, st):
        sb = lambda name, shape, dt: st.enter_context(nc.sbuf_tensor(name, shape, dt))
        b = {}
        b['xs'] = sb('a_xs', [128, 16, 544], F32)
        b['hT'] = sb('a_hT', [128, 16, 544], BF16)
        b['w'] = [sb(f'a_w{i}', [128, 16, 512], BF16) for i in range(3)]
        b['val'] = sb('a_val', [128, 4, 544], F32)
        b['gT'] = sb('a_gT', [128, 8, 544], F32)
        b['acc'] = sb('a_acc', [128, 8, 512], F32)
        b['sq'] = [sb(f'a_sq{i}', [128, 544], BF16) for i in range(2)]
        b['sig'] = [sb(f'a_sig{i}', [128, 512], F32) for i in range(2)]
        b['ub'] = [sb(f'a_ub{i}', [128, 512], BF16) for i in range(2)]
        b['stg'] = [sb(f'a_stg{i}', [128, 2, 512], BF16) for i in range(2)]
        b['rstd'] = sb('a_rstd', [128, 544], F32)
        b['ln'] = sb('a_ln', [128, 5, 512], F32)
        b['t'] = [sb(f'a_t{i}', [128, 512], F32) for i in range(2)]
        b['ya'] = [sb(f'a_ya{i}', [128, 512], BF16) for i in range(2)]
        return b

    def pa_prog(c, b):
        pb = Rot(c, G['ps'])
        wrot_regs = [c.reg() for _ in range(3)]
        sq = Rot(c, b['sq'])
        sig = Rot(c, b['sig'])
        ubr = Rot(c, b['ub'])
        stg = Rot(c, b['stg'])
        tt = Rot(c, b['t'])
        yar = Rot(c, b['ya'])
        rxs, rhT, rval, rgT, racc, rrstd, rln = [c.reg() for _ in range(7)]
        agin1 = G['agin1'].rearrange("a (ri g kc n) -> a ri g kc n", ri=2, g=8, kc=128)
        xs, hT = b['xs'], b['hT']
        for q in range(4):
            c.dma('sync', xs[:], G['xT'][:, 512 * q:512 * q + 544].rearrange("(kc p) t -> p kc t", p=128),
                  'lda', writes=[rxs])
            psA, rA = pb.next()
            psB, rB = pb.next()
            for kc in range(16):
                s, rs = sq.next()
                c.op('scalar', lambda e, kc=kc, s=s: e.activation(out=s[:], in_=xs[:, kc, :], func=AF.Square),
                     reads=[rxs], writes=[rs])
                c.op('tensor', lambda e, kc=kc, s=s: e.matmul(psA[:], G['ones_bf'][:], s[:, 0:512],
                                                              start=(kc == 0), stop=(kc == 15)),
                     reads=[rs, R['const']], writes=[rA] if kc == 0 else [], accs=[rA] if kc else [])
                c.op('tensor', lambda e, kc=kc, s=s: e.matmul(psB[:, 0:32], G['ones_bf'][:], s[:, 512:544],
                                                              start=(kc == 0), stop=(kc == 15)),
                     reads=[rs, R['const']], writes=[rB] if kc == 0 else [], accs=[rB] if kc else [])
            rstd_from(c, psA[:], rA, b['rstd'][:, 0:512], rrstd, D)
            rstd_from(c, psB[:, 0:32], rB, b['rstd'][:, 512:544], rrstd, D, acc=True)
            for kc in range(16):
                c.op('vector', lambda e, kc=kc: e.scalar_tensor_tensor(
                    out=hT[:, kc, :], in0=xs[:, kc, :], scalar=vec('mix_norm_e', kc), in1=b['rstd'][:],
                    op0=ALU.mult, op1=ALU.mult),
                    reads=[rxs, rrstd, R['const']], writes=[rhT] if kc == 0 else [], accs=[rhT] if kc else [])

            groups = [(0, 'val', 0), (1024, 'gate', 0), (512, 'val', 1), (1536, 'gate', 1),
                      (2048, 'ub', 0), (2560, 'ub', 1)]

            def load_w(item, slot):
                col0 = item[0]
                c.dma('gpsimd', b['w'][slot][:], G['w_in_e'][:, col0:col0 + 512].rearrange("(kc p) n -> p kc n", p=128),
                      f'w{slot}', writes=[wrot_regs[slot]])

            def compute(item, slot):
                col0, kind, gi = item
                w = b['w'][slot]
                rw = wrot_regs[slot]
                for mi in range(4):
                    parts = [(16, 512)] if kind == 'ub' else [(0, 512), (512, 32)]
                    for (c0, n) in parts:
                        ps, rp = pb.next()
                        for kc in range(16):
                            last = kc == 15
                            c.op('tensor', lambda e, kc=kc, ps=ps, c0=c0, n=n, w=w, mi=mi: e.matmul(
                                ps[:, 0:n], w[:, kc, mi * 128:(mi + 1) * 128], hT[:, kc, c0:c0 + n],
                                start=(kc == 0), stop=(kc == 15)),
                                reads=[rw, rhT], writes=[rp] if kc == 0 else [], accs=[rp] if last and kc else [],
                                sig=(last or kc == 0))
                        if kind == 'val':
                            c.op('scalar', lambda e, ps=ps, c0=c0, n=n, mi=mi: e.activation(
                                out=b['val'][:, mi, c0:c0 + n], in_=ps[:, 0:n], func=AF.Copy),
                                reads=[rp], accs=[rval])
                        elif kind == 'gate':
                            sg, rsg = sig.next()
                            c.op('scalar', lambda e, ps=ps, n=n, sg=sg: e.activation(
                                out=sg[:, 0:n], in_=ps[:, 0:n], func=AF.Sigmoid),
                                reads=[rp], writes=[rsg])
                            ch = gi * 4 + mi
                            c.op('vector', lambda e, sg=sg, c0=c0, n=n, mi=mi, ch=ch: e.tensor_tensor(
                                out=b['gT'][:, ch, c0:c0 + n], in0=b['val'][:, mi, c0:c0 + n], in1=sg[:, 0:n], op=ALU.mult),
                                reads=[rsg, rval], accs=[rgT])
                        else:
                            u, ru = ubr.next()
                            c.op('scalar', lambda e, ps=ps, u=u: e.activation(out=u[:], in_=ps[:], func=AF.Copy),
                                 reads=[rp], writes=[ru])
                            g = gi * 4 + mi
                            sgi, sgb, rsgb = stg.nexti()
                            for ri in range(2):
                                ps2, rp2 = pb.next()
                                c.op('tensor', lambda e, ps2=ps2, u=u, ri=ri: e.matmul(
                                    ps2[:], G['dft_sb'][:, ri * 128:(ri + 1) * 128], u[:], start=True, stop=True),
                                    reads=[ru, R['const']], writes=[rp2])
                                if ri == 0:
                                    c.op('scalar', lambda e, ps2=ps2, sgb=sgb: e.activation(
                                        out=sgb[:, 0, :], in_=ps2[:], func=AF.Copy), reads=[rp2], writes=[rsgb])
                                else:
                                    c.op('vector', lambda e, ps2=ps2, sgb=sgb: e.tensor_copy(
                                        out=sgb[:, 1, :], in_=ps2[:]), reads=[rp2], accs=[rsgb])
                            for ri in range(2):
                                c.dma('sync', agin1[4 * q:4 * q + 4, ri, g, :, :].rearrange("a k n -> k a n"),
                                      sgb[:, ri, :].rearrange("k (a n) -> k a n", a=4), f'sta1{sgi}',
                                      reads=[rsgb], accs=[R['agin1']])
                if kind == 'val':
                    pass

            stream(groups, 3, load_w, compute)

            for ch in range(8):
                c.op('vector', lambda e, ch=ch: e.tensor_scalar(
                    out=b['acc'][:, ch, :], in0=b['gT'][:, ch, 1:513], scalar1=vec('conv_a_w', 0 * 8 + ch),
                    scalar2=vec('conv_a_b', ch), op0=ALU.mult, op1=ALU.add),
                    reads=[rgT, R['const']], writes=[racc] if ch == 0 else [], accs=[racc] if ch else [])
                for k in range(1, 31):
                    c.op('vector', lambda e, ch=ch, k=k: e.scalar_tensor_tensor(
                        out=b['acc'][:, ch, :], in0=b['gT'][:, ch, 1 + k:513 + k], scalar=vec('conv_a_w', k * 8 + ch),
                        in1=b['acc'][:, ch, :], op0=ALU.mult, op1=ALU.add),
                        reads=[rgT, racc], accs=[racc])
            psM, rM = pb.next()
            psQ, rQ = pb.next()
            for ch in range(8):
                c.op('tensor', lambda e, ch=ch: e.matmul(psM[:], G['ones32'][:], b['acc'][:, ch, :],
                                                         start=(ch == 0), stop=(ch == 7)),
                     reads=[racc, R['const']], writes=[rM] if ch == 0 else [], accs=[rM] if ch else [])
                s, rs = sq.next()
                c.op('scalar', lambda e, ch=ch, s=s: e.activation(out=s[:, 0:512], in_=b['acc'][:, ch, :], func=AF.Square),
                     reads=[racc], writes=[rs])
                c.op('tensor', lambda e, ch=ch, s=s: e.matmul(psQ[:], G['ones_bf'][:], s[:, 0:512],
                                                              start=(ch == 0), stop=(ch == 7)),
                     reads=[rs], writes=[rQ] if ch == 0 else [], accs=[rQ] if ch else [])
            ln = b['ln']
            mean, msq, var, rs2, nmr = [ln[:, i, :] for i in range(5)]
            c.op('vector', lambda e: e.tensor_scalar(out=mean, in0=psM[:], scalar1=1.0 / 1024, scalar2=None, op0=ALU.mult),
                 reads=[rM], writes=[rln])
            c.op('vector', lambda e: e.tensor_tensor(out=msq, in0=mean, in1=mean, op=ALU.mult), reads=[rln], accs=[rln])
            c.op('vector', lambda e: e.scalar_tensor_tensor(out=var, in0=psQ[:], scalar=1.0 / 1024, in1=msq,
                                                           op0=ALU.mult, op1=ALU.subtract),
                 reads=[rQ, rln], accs=[rln])
            c.op('scalar', lambda e: e.activation(out=rs2, in_=var, func=AF.Sqrt, bias=EPS, scale=1.0), reads=[rln], accs=[rln])
            c.op('vector', lambda e: e.reciprocal(out=rs2, in_=rs2), reads=[rln], accs=[rln])
            c.op('vector', lambda e: e.scalar_tensor_tensor(out=nmr, in0=mean, scalar=-1.0, in1=rs2,
                                                           op0=ALU.mult, op1=ALU.mult), reads=[rln], accs=[rln])
            for ch in range(8):
                t, rt = tt.next()
                c.op('vector', lambda e, ch=ch, t=t: e.tensor_tensor(out=t[:], in0=b['acc'][:, ch, :], in1=rs2, op=ALU.mult),
                     reads=[racc, rln], writes=[rt])
                c.op('vector', lambda e, t=t: e.tensor_tensor(out=t[:], in0=t[:], in1=nmr, op=ALU.add),
                     reads=[rt, rln], accs=[rt])
                yi, ya, rya = yar.nexti()
                c.op('scalar', lambda e, ch=ch, t=t, ya=ya: e.activation(
                    out=ya[:], in_=t[:], func=AF.Silu, bias=vec('ln_a_b', ch), scale=vec('ln_a_g', ch)),
                    reads=[rt, R['const']], writes=[rya])
                c.dma('sync', G['mixT'][ch * 128:(ch + 1) * 128, 512 * q:512 * q + 512], ya[:], f'sta2{yi}',
                      reads=[rya], accs=[R['mixT']])

    c.run_phase(pa_alloc, pa_prog)

    def ag_phase(kind_in, kind_out, semname):
        def alloc(nc, st):
            return {}

        def prog(c, b):
            c.dma('gpsimd', None, None, semname, reads=[R[kind_in]], writes=[R[kind_out]], inc=1,
                  fn=lambda e: e.collective_compute("AllGather", ALU.bypass, replica_groups=[list(range(NCORES))],
                                                    ins=[G[kind_in][:]], outs=[G[kind_out][:]]))
        c.run_phase(alloc, prog)

    if LAST_PHASE >= 1:
        ag_phase('agin1', 'agout1', 'cc1')

    def pb_alloc(nc, st):
        sb = lambda name, shape, dt: st.enter_context(nc.sbuf_tensor(name, shape, dt))
        b = {}
        b['At'] = [sb(f'b_At{i}', [128, 2, 64, 128], BF16) for i in range(2)]
        b['Bt'] = [sb(f'b_Bt{i}', [128, 2, 64, 128], BF16) for i in range(2)]
        b['tcs'] = sb('b_tcs', [128, 2, 2048], BF16)
        b['yb'] = [sb(f'b_yb{i}', [128, 2048], BF16) for i in range(2)]
        return b

    def pb_prog(c, b):
        pbk = Rot(c, G['ps'][0:4])
        ybanks = G['ps'][4:8]
        rY = [c.reg() for _ in range(4)]
        rAt = [c.reg() for _ in range(2)]
        Btr = Rot(c, b['Bt'])
        ybr = Rot(c, b['yb'])
        rtcs = c.reg()
        c.dma('gpsimd', b['tcs'][:], G['tcs'].rearrange("p (a n) -> p a n", a=2), 'ldb0', writes=[rtcs])
        agv = G['agout1'].rearrange("a (ri g kc n) -> a ri g kc n", ri=2, g=8, kc=128)
        items = [(g, kh) for g in range(8) for kh in range(2)]

        def load(item, slot):
            g, kh = item
            for ri in range(2):
                c.dma('sync', b['At'][slot][:, ri, :, :], agv[:, ri, g, kh * 64:(kh + 1) * 64, :], f'ldb{slot + 1}',
                      reads=[R['agout1']], writes=[rAt[slot]] if ri == 0 else [], accs=[rAt[slot]] if ri else [])

        def compute(item, slot):
            g, kh = item
            At = b['At'][slot]
            Bt, rBt = Btr.next()
            first = True
            for kcp in range(32):
                ps, rp = pbk.next()
                for j in range(2):
                    kc = 2 * kcp + j
                    c.op('tensor', lambda e, ps=ps, j=j, kc=kc: e.matmul(
                        ps[:, j * 256:(j + 1) * 256], At[:, 0, kc, :], G['dft_sb'][:, 0:256], start=True, stop=False),
                        reads=[rAt[slot], R['const']], writes=[rp] if j == 0 else [], sig=(j == 0))
                    c.op('tensor', lambda e, ps=ps, j=j, kc=kc: e.matmul(
                        ps[:, j * 256:(j + 1) * 256], At[:, 1, kc, :], G['dft_sb'][:, 256:512], start=False, stop=True),
                        reads=[rAt[slot]], accs=[rp] if j == 1 else [], sig=(j == 1))
                src = ps[:].rearrange("p (j ri k) -> p ri j k", j=2, ri=2)
                dst = Bt[:, :, 2 * kcp:2 * kcp + 2, :]
                if kcp % 2 == 0:
                    c.op('scalar', lambda e, src=src, dst=dst: e.activation(out=dst, in_=src, func=AF.Copy),
                         reads=[rp], writes=[rBt] if first else [], accs=[] if first else [rBt])
                else:
                    c.op('vector', lambda e, src=src, dst=dst: e.tensor_copy(out=dst, in_=src),
                         reads=[rp], writes=[rBt] if first else [], accs=[] if first else [rBt])
                first = False
            for k1 in range(128):
                bk = k1 // 32
                o = (k1 % 32) * 16
                yb_ps = ybanks[bk]
                fst = (k1 % 32 == 0)
                lst = (k1 % 32 == 31)
                c.op('tensor', lambda e, yb_ps=yb_ps, o=o, k1=k1: e.matmul(
                    yb_ps[0:64, o:o + 16], Bt[:, 0, :, k1], b['tcs'][:, 0, k1 * 16:(k1 + 1) * 16], start=True, stop=False),
                    reads=[rBt, rtcs], writes=[rY[bk]] if fst else [], sig=fst)
                c.op('tensor', lambda e, yb_ps=yb_ps, o=o, k1=k1: e.matmul(
                    yb_ps[0:64, o:o + 16], Bt[:, 1, :, k1], b['tcs'][:, 1, k1 * 16:(k1 + 1) * 16], start=False, stop=True),
                    reads=[rBt], accs=[rY[bk]] if lst else [], sig=lst)
            ybi, yb, ryb = ybr.nexti()
            for bk in range(4):
                src = ybanks[bk][0:64, :].rearrange("p (k1 k2) -> p k2 k1", k2=16)
                dst = yb[0:64, :].rearrange("p (k2 k1) -> p k2 k1", k2=16)[:, :, bk * 32:(bk + 1) * 32]
                c.op('scalar', lambda e, src=src, dst=dst: e.activation(out=dst, in_=src, func=AF.Copy),
                     reads=[rY[bk]], writes=[ryb] if bk == 0 else [], accs=[ryb] if bk else [])
            r0 = 1024 + g * 128 + kh * 64
            c.dma('sync', G['mixT'][r0:r0 + 64, :], yb[0:64, :], f'stb{ybi}', reads=[ryb], accs=[R['mixT']])

        stream(items, 2, load, compute)

    if LAST_PHASE >= 2:
        c.run_phase(pb_alloc, pb_prog)

    def make_mixmlp(layer, W_out, W_up, W_down, xin, xin_off, xin_reg, norm_name, final):
        def alloc(nc, st):
            sb = lambda name, shape, dt: st.enter_context(nc.sbuf_tensor(name, shape, dt))
            b = {}
            b['xacc'] = sb(f'c{layer}_xacc', [128, 16, 1024], F32)
            b['act'] = sb(f'c{layer}_act', [128, 16, 1024], BF16)
            b['w'] = [sb(f'c{layer}_w{i}', [128, 16, 512], BF16) for i in range(2)]
            b['dw'] = [sb(f'c{layer}_dw{i}', [128, 4, 2048], BF16) for i in range(2)]
            b['aT'] = [sb(f'c{layer}_aT{i}', [128, 4, 1024], BF16) for i in range(2)]
            b['rl'] = [sb(f'c{layer}_rl{i}', [128, 512], F32) for i in range(2)]
            b['sq'] = [sb(f'c{layer}_sq{i}', [128, 1024], BF16) for i in range(2)]
            b['rstd'] = sb(f'c{layer}_rstd', [128, 1024], F32)
            return b

        def rmsnorm_acc(c, b, pb, sq, rx, rrstd):
            xacc = b['xacc']
            psA, rA = pb.next()
            psB, rB = pb.next()
            for kc in range(16):
                s, rs = sq.next()
                c.op('scalar', lambda e, kc=kc, s=s: e.activation(out=s[:], in_=xacc[:, kc, :], func=AF.Square),
                     reads=[rx], writes=[rs])
                c.op('tensor', lambda e, kc=kc, s=s: e.matmul(psA[:], G['ones_bf'][:], s[:, 0:512],
                                                              start=(kc == 0), stop=(kc == 15)),
                     reads=[rs, R['const']], writes=[rA] if kc == 0 else [], accs=[rA] if kc else [])
                c.op('tensor', lambda e, kc=kc, s=s: e.matmul(psB[:], G['ones_bf'][:], s[:, 512:1024],
                                                              start=(kc == 0), stop=(kc == 15)),
                     reads=[rs, R['const']], writes=[rB] if kc == 0 else [], accs=[rB] if kc else [])
            rstd_from(c, psA[:], rA, b['rstd'][:, 0:512], rrstd, D)
            rstd_from(c, psB[:], rB, b['rstd'][:, 512:1024], rrstd, D, acc=True)

        def prog(c, b):
            pb = Rot(c, G['ps'])
            sq = Rot(c, b['sq'])
            rl = Rot(c, b['rl'])
            xacc, act = b['xacc'], b['act']
            rx = [c.reg() for _ in range(16)]
            ract, rrstd = c.reg(), c.reg()
            rw = [c.reg() for _ in range(2)]
            rdw = [c.reg() for _ in range(2)]
            raT = [c.reg() for _ in range(2)]
            for half in range(2):
                t0 = half * 1024
                for kc in range(16):
                    c.dma('sync', xacc[:, kc, :], xin[kc * 128:(kc + 1) * 128, xin_off + t0:xin_off + t0 + 1024], 'ldc0',
                          reads=[xin_reg] if xin_reg is not None else [], writes=[rx[kc]])
                for kc in range(16):
                    rx[kc].w = {'ldc0': c.count['ldc0']}
                c.dma('sync', act[:], G['mixT'][:, t0:t0 + 1024].rearrange("(kc p) t -> p kc t", p=128), 'ldc1',
                      reads=[R['mixT']], writes=[ract])

                def load_o(item, slot):
                    c.dma('gpsimd', b['w'][slot][:], W_out[:, item * 512:(item + 1) * 512].rearrange("(kc p) n -> p kc n", p=128),
                          f'w{slot}', writes=[rw[slot]])

                def comp_o(item, slot):
                    w = b['w'][slot]
                    for mi in range(4):
                        m = item * 4 + mi
                        for tl in range(2):
                            ps, rp = pb.next()
                            for kc in range(16):
                                c.op('tensor', lambda e, ps=ps, kc=kc, w=w, mi=mi, tl=tl: e.matmul(
                                    ps[:], w[:, kc, mi * 128:(mi + 1) * 128], act[:, kc, tl * 512:(tl + 1) * 512],
                                    start=(kc == 0), stop=(kc == 15)),
                                    reads=[rw[slot], ract], writes=[rp] if kc == 0 else [], accs=[rp] if kc == 15 else [],
                                    sig=(kc in (0, 15)))
                            c.op('vector', lambda e, ps=ps, m=m, tl=tl: e.tensor_tensor(
                                out=xacc[:, m, tl * 512:(tl + 1) * 512], in0=xacc[:, m, tl * 512:(tl + 1) * 512], in1=ps[:], op=ALU.add),
                                reads=[rp, rx[m]], accs=[rx[m]])

                stream(list(range(4)), 2, load_o, comp_o)
                psA, rA = pb.next()
                psB, rB = pb.next()
                for kc in range(16):
                    s, rs = sq.next()
                    c.op('scalar', lambda e, kc=kc, s=s: e.activation(out=s[:], in_=xacc[:, kc, :], func=AF.Square),
                         reads=[rx[kc]], writes=[rs])
                    c.op('tensor', lambda e, kc=kc, s=s: e.matmul(psA[:], G['ones_bf'][:], s[:, 0:512],
                                                                  start=(kc == 0), stop=(kc == 15)),
                         reads=[rs, R['const']], writes=[rA] if kc == 0 else [], accs=[rA] if kc else [])
                    c.op('tensor', lambda e, kc=kc, s=s: e.matmul(psB[:], G['ones_bf'][:], s[:, 512:1024],
                                                                  start=(kc == 0), stop=(kc == 15)),
                         reads=[rs, R['const']], writes=[rB] if kc == 0 else [], accs=[rB] if kc else [])
                rstd_from(c, psA[:], rA, b['rstd'][:, 0:512], rrstd, D)
                rstd_from(c, psB[:], rB, b['rstd'][:, 512:1024], rrstd, D, acc=True)
                for kc in range(16):
                    c.op('vector', lambda e, kc=kc: e.scalar_tensor_tensor(
                        out=act[:, kc, :], in0=xacc[:, kc, :], scalar=vec(norm_name, kc), in1=b['rstd'][:],
                        op0=ALU.mult, op1=ALU.mult),
                        reads=[rx[kc], rrstd, R['const']], writes=[ract] if kc == 0 else [], accs=[ract] if kc else [])

                def load_f(F, slot):
                    c.dma('gpsimd', b['w'][slot][:], W_up[:, F * 512:(F + 1) * 512].rearrange("(kc p) n -> p kc n", p=128),
                          f'w{slot}', writes=[rw[slot]])
                    c.dma('gpsimd', b['dw'][slot][:], W_down[F * 512:(F + 1) * 512, :].rearrange("(fc p) n -> p fc n", p=128),
                          f'dw{slot}', writes=[rdw[slot]])

                def comp_f(F, slot):
                    w, dw, aT = b['w'][slot], b['dw'][slot], b['aT'][slot]
                    firsta = True
                    for fi in range(4):
                        for tl in range(2):
                            ps, rp = pb.next()
                            for kc in range(16):
                                c.op('tensor', lambda e, ps=ps, kc=kc, fi=fi, tl=tl: e.matmul(
                                    ps[:], w[:, kc, fi * 128:(fi + 1) * 128], act[:, kc, tl * 512:(tl + 1) * 512],
                                    start=(kc == 0), stop=(kc == 15)),
                                    reads=[rw[slot], ract], writes=[rp] if kc == 0 else [], accs=[rp] if kc == 15 else [],
                                    sig=(kc in (0, 15)))
                            r_, rr_ = rl.next()
                            c.op('scalar', lambda e, ps=ps, r_=r_: e.activation(out=r_[:], in_=ps[:], func=AF.Relu),
                                 reads=[rp], writes=[rr_])
                            c.op('vector', lambda e, r_=r_, fi=fi, tl=tl: e.tensor_tensor(
                                out=aT[:, fi, tl * 512:(tl + 1) * 512], in0=r_[:], in1=r_[:], op=ALU.mult),
                                reads=[rr_], writes=[raT[slot]] if firsta else [], accs=[] if firsta else [raT[slot]])
                            firsta = False
                    for o in range(16):
                        for tl in range(2):
                            ps, rp = pb.next()
                            for fi in range(4):
                                c.op('tensor', lambda e, ps=ps, fi=fi, o=o, tl=tl: e.matmul(
                                    ps[:], dw[:, fi, o * 128:(o + 1) * 128], aT[:, fi, tl * 512:(tl + 1) * 512],
                                    start=(fi == 0), stop=(fi == 3)),
                                    reads=[rdw[slot], raT[slot]], writes=[rp] if fi == 0 else [], accs=[rp] if fi == 3 else [],
                                    sig=(fi in (0, 3)))
                            c.op('vector', lambda e, ps=ps, o=o, tl=tl: e.tensor_tensor(
                                out=xacc[:, o, tl * 512:(tl + 1) * 512], in0=xacc[:, o, tl * 512:(tl + 1) * 512], in1=ps[:], op=ALU.add),
                                reads=[rp, rx[o]], accs=[rx[o]])

                stream(list(range(16)), 2, load_f, comp_f)

                if not final:
                    for kc in range(16):
                        c.dma('sync', G['x1T'][kc * 128:(kc + 1) * 128, t0:t0 + 1024], xacc[:, kc, :], 'stc',
                              reads=[rx[kc]], accs=[R['x1T']])
                    for kc in range(16):
                        rx[kc].r['stc'] = c.count['stc']
                else:
                    psA, rA = pb.next()
                    psB, rB = pb.next()
                    for kc in range(16):
                        s, rs = sq.next()
                        c.op('scalar', lambda e, kc=kc, s=s: e.activation(out=s[:], in_=xacc[:, kc, :], func=AF.Square),
                             reads=[rx[kc]], writes=[rs])
                        c.op('tensor', lambda e, kc=kc, s=s: e.matmul(psA[:], G['ones_bf'][:], s[:, 0:512],
                                                                      start=(kc == 0), stop=(kc == 15)),
                             reads=[rs, R['const']], writes=[rA] if kc == 0 else [], accs=[rA] if kc else [])
                        c.op('tensor', lambda e, kc=kc, s=s: e.matmul(psB[:], G['ones_bf'][:], s[:, 512:1024],
                                                                      start=(kc == 0), stop=(kc == 15)),
                             reads=[rs, R['const']], writes=[rB] if kc == 0 else [], accs=[rB] if kc else [])
                    rstd_from(c, psA[:], rA, b['rstd'][:, 0:512], rrstd, D)
                    rstd_from(c, psB[:], rB, b['rstd'][:, 512:1024], rrstd, D, acc=True)
                    for kc in range(16):
                        c.op('vector', lambda e, kc=kc: e.scalar_tensor_tensor(
                            out=xacc[:, kc, :], in0=xacc[:, kc, :], scalar=vec('final_norm', kc), in1=b['rstd'][:],
                            op0=ALU.mult, op1=ALU.mult),
                            reads=[rrstd, R['const']], writes=[rx[kc]])
                        c.dma('sync', G['outT'][kc * 128:(kc + 1) * 128, t0:t0 + 1024], xacc[:, kc, :], 'stc',
                              reads=[rx[kc]], accs=[R['outT']])
                    for kc in range(16):
                        rx[kc].r['stc'] = c.count['stc']

        return alloc, prog

    if LAST_PHASE >= 3:
        a, p = make_mixmlp(0, G['w_out_e'], G['w_up0'], G['w_down0'], G['xT'], 16, None, 'mlp_norm0', False)
        c.run_phase(a, p)

    def pd_alloc(nc, st):
        sb = lambda name, shape, dt: st.enter_context(nc.sbuf_tensor(name, shape, dt))
        b = {}
        b['xc'] = [sb(f'd_xc{i}', [128, 1024], F32) for i in range(3)]
        b['act'] = sb('d_act', [128, 16, 1024], BF16)
        b['w'] = [sb(f'd_w{i}', [128, 16, 512], BF16) for i in range(3)]
        b['csb'] = sb('d_csb', [128, 4, 1024], F32)
        b['cq'] = sb('d_cq', [128, 4, 1024], F32)
        b['ckv'] = sb('d_ckv', [128, 2, 1024], F32)
        b['qn'] = sb('d_qn', [128, 4, 1024], BF16)
        b['kvn'] = sb('d_kvn', [128, 2, 1024], BF16)
        b['s32'] = [sb(f'd_s32{i}', [128, 512], F32) for i in range(3)]
        b['s16'] = [sb(f'd_s16{i}', [128, 512], BF16) for i in range(3)]
        b['sq'] = [sb(f'd_sq{i}', [128, 1024], BF16) for i in range(2)]
        b['rstd'] = sb('d_rstd', [128, 1024], F32)
        b['rope'] = sb('d_rope', [64, 2, 2048], F32)
        b['t1'] = [sb(f'd_t1{i}', [64, 512], F32) for i in range(2)]
        b['t2'] = [sb(f'd_t2{i}', [64, 512], F32) for i in range(2)]
        b['brow'] = sb('d_brow', [1, 2048], F32)
        return b

    def pd_prog(c, b):
        pb = Rot(c, G['ps'])
        xcr = Rot(c, b['xc'])
        sq = Rot(c, b['sq'])
        s32 = Rot(c, b['s32'])
        s16 = Rot(c, b['s16'])
        t1r = Rot(c, b['t1'])
        t2r = Rot(c, b['t2'])
        act = b['act']
        ract, rrstd, rcsb, rcq, rckv, rqn, rkvn, rrope, rbrow = [c.reg() for _ in range(9)]
        rw = [c.reg() for _ in range(3)]
        c.dma('sync', b['rope'][:], G['rope'].rearrange("p (a n) -> p a n", a=2), 'ldd9', writes=[rrope])

        def small_norm(src, rsrc, nch, gname, dst, rdst):
            psA, rA = pb.next()
            psB, rB = pb.next()
            for kc in range(nch):
                s, rs = sq.next()
                c.op('scalar', lambda e, kc=kc, s=s: e.activation(out=s[:], in_=src[:, kc, :], func=AF.Square),
                     reads=[rsrc], writes=[rs])
                c.op('tensor', lambda e, kc=kc, s=s: e.matmul(psA[:], G['ones_bf'][:], s[:, 0:512],
                                                              start=(kc == 0), stop=(kc == nch - 1)),
                     reads=[rs, R['const']], writes=[rA] if kc == 0 else [], accs=[rA] if kc else [])
                c.op('tensor', lambda e, kc=kc, s=s: e.matmul(psB[:], G['ones_bf'][:], s[:, 512:1024],
                                                              start=(kc == 0), stop=(kc == nch - 1)),
                     reads=[rs, R['const']], writes=[rB] if kc == 0 else [], accs=[rB] if kc else [])
            rstd_from(c, psA[:], rA, b['rstd'][:, 0:512], rrstd, nch * 128)
            rstd_from(c, psB[:], rB, b['rstd'][:, 512:1024], rrstd, nch * 128, acc=True)
            for kc in range(nch):
                c.op('vector', lambda e, kc=kc: e.scalar_tensor_tensor(
                    out=dst[:, kc, :], in0=src[:, kc, :], scalar=vec(gname, kc), in1=b['rstd'][:],
                    op0=ALU.mult, op1=ALU.mult),
                    reads=[rsrc, rrstd, R['const']], writes=[rdst] if kc == 0 else [], accs=[rdst] if kc else [])

        def rope_out(psx, rpx, psy, rpy, tok0, dst_ap, dst_reg):
            t1, rt1 = t1r.next()
            t2, rt2 = t2r.next()
            c.op('vector', lambda e: e.tensor_tensor(out=t1[:], in0=psx[0:64, :], in1=b['rope'][:, 0, tok0:tok0 + 512], op=ALU.mult),
                 reads=[rpx, rrope], writes=[rt1])
            c.op('vector', lambda e: e.tensor_tensor(out=t2[:], in0=psy[0:64, :], in1=b['rope'][:, 1, tok0:tok0 + 512], op=ALU.mult),
                 reads=[rpy, rrope], writes=[rt2])
            si, s, rs = s16.nexti()
            c.op('vector', lambda e: e.tensor_tensor(out=s[0:64, :], in0=t1[:], in1=t2[:], op=ALU.add),
                 reads=[rt1, rt2], writes=[rs])
            c.dma('sync', dst_ap, s[0:64, :], f'sd16_{si}', reads=[rs], accs=[dst_reg])

        for half in range(2):
            t0 = half * 1024
            psA, rA = pb.next()
            psB, rB = pb.next()
            for kc in range(16):
                xi, xc, rxc = xcr.nexti()
                c.dma('sync', xc[:], G['x1T'][kc * 128:(kc + 1) * 128, t0:t0 + 1024], f'ldd0{xi}', reads=[R['x1T']], writes=[rxc])
                s, rs = sq.next()
                c.op('scalar', lambda e, xc=xc, s=s: e.activation(out=s[:], in_=xc[:], func=AF.Square), reads=[rxc], writes=[rs])
                c.op('tensor', lambda e, kc=kc, s=s: e.matmul(psA[:], G['ones_bf'][:], s[:, 0:512], start=(kc == 0), stop=(kc == 15)),
                     reads=[rs, R['const']], writes=[rA] if kc == 0 else [], accs=[rA] if kc else [])
                c.op('tensor', lambda e, kc=kc, s=s: e.matmul(psB[:], G['ones_bf'][:], s[:, 512:1024], start=(kc == 0), stop=(kc == 15)),
                     reads=[rs, R['const']], writes=[rB] if kc == 0 else [], accs=[rB] if kc else [])
            rstd_from(c, psA[:], rA, b['rstd'][:, 0:512], rrstd, D)
            rstd_from(c, psB[:], rB, b['rstd'][:, 512:1024], rrstd, D, acc=True)
            for kc in range(16):
                xi, xc, rxc = xcr.nexti()
                c.dma('sync', xc[:], G['x1T'][kc * 128:(kc + 1) * 128, t0:t0 + 1024], f'ldd0{xi}', reads=[R['x1T']], writes=[rxc])
                c.op('vector', lambda e, kc=kc, xc=xc: e.scalar_tensor_tensor(
                    out=act[:, kc, :], in0=xc[:], scalar=vec('mix_norm_o', kc), in1=b['rstd'][:], op0=ALU.mult, op1=ALU.mult),
                    reads=[rxc, rrstd, R['const']], writes=[ract] if kc == 0 else [], accs=[ract] if kc else [])

            groups = [(0, 'b', 0), (512, 'b', 1), (1024, 'c', 0), (2048, 'hc', 0), (1536, 'c', 1), (2560, 'hc', 1),
                      (3072, 'cq', 0), (3584, 'kv', 0)]

            def load_w(item, slot):
                col0, kind, gi = item
                n = 384 if kind == 'kv' else 512
                c.dma('gpsimd', b['w'][slot][:, :, 0:n], G['w_in_o'][:, col0:col0 + n].rearrange("(kc p) n -> p kc n", p=128),
                      f'w{slot}', writes=[rw[slot]])

            def mm16(ps, rp, w, slot, c0, m, tl):
                for kc in range(16):
                    c.op('tensor', lambda e, kc=kc: e.matmul(
                        ps[0:m, :], w[:, kc, c0:c0 + m], act[:, kc, tl * 512:(tl + 1) * 512],
                        start=(kc == 0), stop=(kc == 15)),
                        reads=[rw[slot], ract], writes=[rp] if kc == 0 else [], accs=[rp] if kc == 15 else [],
                        sig=(kc in (0, 15)))

            def compute(item, slot):
                col0, kind, gi = item
                w = b['w'][slot]
                if kind == 'kv':
                    for mi in range(2):
                        for tl in range(2):
                            ps, rp = pb.next()
                            mm16(ps, rp, w, slot, mi * 128, 128, tl)
                            c.op('scalar', lambda e, ps=ps, mi=mi, tl=tl: e.activation(
                                out=b['ckv'][:, mi, tl * 512:(tl + 1) * 512], in_=ps[:], func=AF.Copy),
                                reads=[rp], writes=[rckv] if (mi == 0 and tl == 0) else [], accs=[] if (mi == 0 and tl == 0) else [rckv])
                    for tl in range(2):
                        psx, rpx = pb.next()
                        mm16(psx, rpx, w, slot, 256, 64, tl)
                        psy, rpy = pb.next()
                        mm16(psy, rpy, w, slot, 320, 64, tl)
                        tok0 = t0 + tl * 512
                        rope_out(psx, rpx, psy, rpy, tok0, G['ag2in'][1024:1088, tok0:tok0 + 512], R['ag2in'])
                    return
                for mi in range(4):
                    ch = gi * 4 + mi
                    for tl in range(2):
                        tok0 = t0 + tl * 512
                        ps, rp = pb.next()
                        mm16(ps, rp, w, slot, mi * 128, 128, tl)
                        if kind == 'b':
                            si, s, rs = s32.nexti()
                            c.op('scalar', lambda e, ps=ps, s=s: e.activation(out=s[:], in_=ps[:], func=AF.Copy),
                                 reads=[rp], writes=[rs])
                            c.dma('sync', G['bT'][ch * 128:(ch + 1) * 128, tok0:tok0 + 512], s[:], f'sd32_{si}', reads=[rs], accs=[R['bT']])
                        elif kind == 'c':
                            fw = (mi == 0 and tl == 0)
                            c.op('scalar', lambda e, ps=ps, mi=mi, tl=tl: e.activation(
                                out=b['csb'][:, mi, tl * 512:(tl + 1) * 512], in_=ps[:], func=AF.Copy),
                                reads=[rp], writes=[rcsb] if fw else [], accs=[] if fw else [rcsb])
                        elif kind == 'hc':
                            si, s, rs = s32.nexti()
                            c.op('vector', lambda e, ps=ps, s=s, mi=mi, tl=tl: e.tensor_tensor(
                                out=s[:], in0=b['csb'][:, mi, tl * 512:(tl + 1) * 512], in1=ps[:], op=ALU.mult),
                                reads=[rp, rcsb], writes=[rs])
                            c.dma('sync', G['vT'][ch * 128:(ch + 1) * 128, tok0:tok0 + 512], s[:], f'sd32_{si}', reads=[rs], accs=[R['vT']])
                            for (cond, col, off) in ((half == 0 and tl == 0, 0, 0), (half == 1 and tl == 1, 511, 1024)):
                                if cond:
                                    psb, rpb = pb.next()
                                    c.op('tensor', lambda e, psb=psb, s=s, col=col: e.matmul(
                                        psb[0:1, 0:128], s[:, col:col + 1], G['ident32'][:], start=True, stop=True),
                                        reads=[rs, R['const']], writes=[rpb])
                                    c.op('scalar', lambda e, psb=psb, off=off, ch=ch: e.activation(
                                        out=b['brow'][0:1, off + ch * 128:off + (ch + 1) * 128], in_=psb[0:1, 0:128], func=AF.Copy),
                                        reads=[rpb], accs=[rbrow])
                        elif kind == 'cq':
                            fw = (mi == 0 and tl == 0)
                            c.op('scalar', lambda e, ps=ps, mi=mi, tl=tl: e.activation(
                                out=b['cq'][:, mi, tl * 512:(tl + 1) * 512], in_=ps[:], func=AF.Copy),
                                reads=[rp], writes=[rcq] if fw else [], accs=[] if fw else [rcq])

            stream(groups, 3, load_w, compute)

            small_norm(b['cq'], rcq, 4, 'q_norm_g', b['qn'], rqn)
            small_norm(b['ckv'], rckv, 2, 'kv_norm_g', b['kvn'], rkvn)
            wq = b['w'][0][:].rearrange("p a n -> p (a n)").rearrange("p (kc n) -> p kc n", kc=4)
            wkv = b['w'][1][:].rearrange("p a n -> p (a n)")[:, 0:4096].rearrange("p (kc n) -> p kc n", kc=2)
            c.dma('gpsimd', wq, G['w_uq3'].rearrange("(kc p) n -> p kc n", p=128), 'w0', writes=[rw[0]])
            c.dma('gpsimd', wkv, G['w_ukv2'].rearrange("(kc p) n -> p kc n", p=128), 'w1', writes=[rw[1]])

            def mmq(ps, rp, c0, m, tl):
                for kc in range(4):
                    c.op('tensor', lambda e, kc=kc: e.matmul(
                        ps[0:m, :], wq[:, kc, c0:c0 + m], b['qn'][:, kc, tl * 512:(tl + 1) * 512], start=(kc == 0), stop=(kc == 3)),
                        reads=[rw[0], rqn], writes=[rp] if kc == 0 else [], accs=[rp] if kc == 3 else [], sig=(kc in (0, 3)))

            for h in range(8):
                for tl in range(2):
                    tok0 = t0 + tl * 512
                    ps, rp = pb.next()
                    mmq(ps, rp, h * 128, 128, tl)
                    si, s, rs = s16.nexti()
                    c.op('scalar', lambda e, ps=ps, s=s: e.activation(out=s[:], in_=ps[:], func=AF.Copy), reads=[rp], writes=[rs])
                    c.dma('sync', G['qT'][h * 128:(h + 1) * 128, tok0:tok0 + 512], s[:], f'sd16_{si}', reads=[rs], accs=[R['qT']])
                    psx, rpx = pb.next()
                    mmq(psx, rpx, 1024 + h * 64, 64, tl)
                    psy, rpy = pb.next()
                    mmq(psy, rpy, 1536 + h * 64, 64, tl)
                    rope_out(psx, rpx, psy, rpy, tok0, G['qT'][1024 + h * 64:1024 + (h + 1) * 64, tok0:tok0 + 512], R['qT'])
            for h in range(8):
                for tl in range(2):
                    tok0 = t0 + tl * 512
                    ps, rp = pb.next()
                    for kc in range(2):
                        c.op('tensor', lambda e, kc=kc, ps=ps, h=h, tl=tl: e.matmul(
                            ps[:], wkv[:, kc, h * 128:(h + 1) * 128], b['kvn'][:, kc, tl * 512:(tl + 1) * 512], start=(kc == 0), stop=(kc == 1)),
                            reads=[rw[1], rkvn], writes=[rp] if kc == 0 else [], accs=[rp] if kc else [])
                    si, s, rs = s16.nexti()
                    c.op('scalar', lambda e, ps=ps, s=s: e.activation(out=s[:], in_=ps[:], func=AF.Copy), reads=[rp], writes=[rs])
                    c.dma('sync', G['ag2in'][h * 128:(h + 1) * 128, tok0:tok0 + 512], s[:], f'sd16_{si}', reads=[rs], accs=[R['ag2in']])
            for tb in range(8):
                blk = half * 8 + tb
                for nh in range(2):
                    ps, rp = pb.next()
                    for kc in range(2):
                        c.op('tensor', lambda e, kc=kc, ps=ps, tb=tb, nh=nh: e.matmul(
                            ps[:], b['kvn'][:, kc, tb * 128:(tb + 1) * 128], wkv[:, kc, 1024 + nh * 512:1024 + (nh + 1) * 512],
                            start=(kc == 0), stop=(kc == 1)),
                            reads=[rw[1], rkvn], writes=[rp] if kc == 0 else [], accs=[rp] if kc else [])
                    si, s, rs = s16.nexti()
                    c.op('vector', lambda e, ps=ps, s=s: e.tensor_copy(out=s[:], in_=ps[:]), reads=[rp], writes=[rs])
                    r0 = 1088 + nh * 512
                    dst = G['ag2in'][r0:r0 + 512, blk * 128:(blk + 1) * 128].rearrange("(hh p) d -> p hh d", p=128)
                    c.dma('sync', dst, s[:].rearrange("p (hh d) -> p hh d", hh=4), f'sd16_{si}', reads=[rs], accs=[R['ag2in']])
        c.dma('sync', G['ag3in'][:], b['brow'][:], 'stbrow', reads=[rbrow], accs=[R['ag3in']])

    def pd_prog_all(c, b):
        pd_prog(c, b)

    if LAST_PHASE >= 4:
        c.run_phase(pd_alloc, pd_prog_all)
    if LAST_PHASE >= 5:
        ag_phase('ag2in', 'ag2out', 'cc2')
        ag_phase('ag3in', 'ag3out', 'cc3')

    def pe_alloc(nc, st):
        sb = lambda name, shape, dt: st.enter_context(nc.sbuf_tensor(name, shape, dt))
        b = {}
        b['KT'] = [sb(f'e_KT{i}', [128, 16384], BF16) for i in range(2)]
        b['V'] = [sb(f'e_V{i}', [128, 8, 2048], BF16) for i in range(2)]
        b['kr'] = sb('e_kr', [128, 16384], BF16)
        b['qn'] = [sb(f'e_qn{i}', [128, 2048], BF16) for i in range(2)]
        b['qr'] = [sb(f'e_qr{i}', [128, 2048], BF16) for i in range(2)]
        b['P'] = [sb(f'e_P{i}', [128, 512], BF16) for i in range(6)]
        b['dacc'] = [sb(f'e_dacc{i}', [128, 512], F32) for i in range(2)]
        b['rd'] = sb('e_rd', [128, 512], F32)
        b['o'] = [sb(f'e_o{i}', [128, 512], BF16) for i in range(2)]
        return b

    def pe0_alloc(nc, st):
        sb = lambda name, shape, dt: st.enter_context(nc.sbuf_tensor(name, shape, dt))
        b = {}
        b['bt'] = sb('e_bt', [8, 2048], F32)
        b['sel'] = sb('e_sel', [8, 2], F32)
        b['vt'] = sb('e_vt', [128, 2050], F32)
        b['bb'] = sb('e_bb', [128, 2048], F32)
        b['ca'] = sb('e_ca', [128, 2048], F32)
        b['yc'] = sb('e_yc', [128, 2048], BF16)
        return b

    def pe0_prog(c, b):
        rbt, rsel, rvt, rbb, rca, ryc = [c.reg() for _ in range(6)]
        sb_rot = Rot(c, G['ps'][0:4])
        pbs = sb_rot
        c.dma('sync', b['bt'][:], G['ag3out'][:], 'lde0', reads=[R['ag3out']], writes=[rbt])
        c.dma('sync', b['sel'][:], G['sel'][:], 'lde8', writes=[rsel])
        for j in range(8):
            c.dma('sync', b['vt'][:, 1:2049], G['vT'][j * 128:(j + 1) * 128, :], 'lde1', reads=[R['vT']], writes=[rvt])
            c.dma('sync', b['bb'][:], G['bT'][j * 128:(j + 1) * 128, :], 'lde2', reads=[R['bT']], writes=[rbb])
            ps, rp = pbs.next()
            c.op('tensor', lambda e, ps=ps, j=j: e.matmul(ps[:, 0:1], b['bt'][0:8, 1024 + j * 128:1024 + (j + 1) * 128], b['sel'][0:8, 0:1],
                                                          start=True, stop=True), reads=[rbt, rsel], writes=[rp])
            c.op('tensor', lambda e, ps=ps, j=j: e.matmul(ps[:, 1:2], b['bt'][0:8, j * 128:(j + 1) * 128], b['sel'][0:8, 1:2],
                                                          start=True, stop=True), reads=[rbt, rsel], accs=[rp])
            c.op('scalar', lambda e, ps=ps: e.activation(out=b['vt'][:, 0:1], in_=ps[:, 0:1], func=AF.Copy), reads=[rp], accs=[rvt])
            c.op('scalar', lambda e, ps=ps: e.activation(out=b['vt'][:, 2049:2050], in_=ps[:, 1:2], func=AF.Copy), reads=[rp], accs=[rvt])
            c.op('vector', lambda e, j=j: e.tensor_scalar(out=b['ca'][:], in0=b['vt'][:, 0:2048], scalar1=vec('conv_c_w', 0 * 8 + j),
                                                          scalar2=None, op0=ALU.mult), reads=[rvt, R['const']], writes=[rca])
            for k in (1, 2):
                c.op('vector', lambda e, j=j, k=k: e.scalar_tensor_tensor(
                    out=b['ca'][:], in0=b['vt'][:, k:k + 2048], scalar=vec('conv_c_w', k * 8 + j), in1=b['ca'][:],
                    op0=ALU.mult, op1=ALU.add), reads=[rvt, rca], accs=[rca])
            c.op('vector', lambda e: e.tensor_tensor(out=b['yc'][:], in0=b['ca'][:], in1=b['bb'][:], op=ALU.mult),
                 reads=[rca, rbb], writes=[ryc])
            c.dma('sync', G['mixT'][j * 128:(j + 1) * 128, :], b['yc'][:], 'ste0', reads=[ryc], accs=[R['mixT']])

    def pe_prog(c, b):
        sb_rot = Rot(c, G['ps'][0:4])
        SCALE = 192.0 ** -0.5
        rKT = [c.reg() for _ in range(2)]
        rV = [c.reg() for _ in range(2)]
        rqn = [c.reg() for _ in range(2)]
        rqr = [c.reg() for _ in range(2)]
        rkr, rrd = c.reg(), c.reg()
        Pr = Rot(c, b['P'])
        orot = Rot(c, b['o'])
        obanks = Rot(c, G['ps'][4:6])
        dbanks = Rot(c, G['ps'][6:8])
        daccr = Rot(c, b['dacc'])
        c.op('vector', lambda e: e.memset(b['kr'][64:128, :], 0.0), accs=[rkr])
        for i in range(2):
            c.op('vector', lambda e, i=i: e.memset(b['qr'][i][64:128, :], 0.0), accs=[rqr[i]])
        for r in range(8):
            c.dma('sync', b['kr'][0:64, r * 2048:(r + 1) * 2048], G['ag2out'][r * 2112 + 1024:r * 2112 + 1088, :], 'lde3',
                  reads=[R['ag2out']], accs=[rkr])

        def load(h, slot):
            for r in range(8):
                c.dma('sync', b['KT'][slot][:, r * 2048:(r + 1) * 2048], G['ag2out'][r * 2112 + h * 128:r * 2112 + (h + 1) * 128, :],
                      f'lde4{slot}', reads=[R['ag2out']], writes=[rKT[slot]] if r == 0 else [], accs=[rKT[slot]] if r else [])
                c.dma('sync', b['V'][slot][:, r, :], G['ag2out'][r * 2112 + 1088 + h * 128:r * 2112 + 1088 + (h + 1) * 128, :],
                      f'lde5{slot}', reads=[R['ag2out']], writes=[rV[slot]] if r == 0 else [], accs=[rV[slot]] if r else [])
            c.dma('sync', b['qn'][slot][:], G['qT'][h * 128:(h + 1) * 128, :], f'lde6{slot}', reads=[R['qT']], writes=[rqn[slot]])
            c.dma('sync', b['qr'][slot][0:64, :], G['qT'][1024 + h * 64:1024 + (h + 1) * 64, :], f'lde7{slot}', reads=[R['qT']],
                  accs=[rqr[slot]], extra=list(rqr[slot].w.items()))

        def compute(h, slot):
            KT, V, qn, qr = b['KT'][slot], b['V'][slot], b['qn'][slot], b['qr'][slot]
            for qt in range(4):
                q0 = qt * 512
                ops_, rO = obanks.next()
                dps_, rD = dbanks.next()
                dacc, rda = daccr.next()

                def s_mm(kb):
                    ps, rp = sb_rot.next()
                    c.op('tensor', lambda e: e.matmul(ps[:], KT[:, kb * 128:(kb + 1) * 128], qn[:, q0:q0 + 512], start=True, stop=False),
                         reads=[rKT[slot], rqn[slot]], writes=[rp])
                    c.op('tensor', lambda e: e.matmul(ps[:], b['kr'][:, kb * 128:(kb + 1) * 128], qr[:, q0:q0 + 512], start=False, stop=True),
                         reads=[rkr, rqr[slot]], accs=[rp])
                    return ps, rp

                pend = [s_mm(kb) for kb in range(3)]
                for kb in range(128):
                    if kb + 3 < 128:
                        pend.append(s_mm(kb + 3))
                    ps, rp = pend.pop(0)
                    P, rP = Pr.next()
                    c.op('scalar', lambda e, ps=ps, P=P: e.activation(out=P[:], in_=ps[:], func=AF.Exp, scale=SCALE),
                         reads=[rp], writes=[rP])
                    r_, bb = kb // 16, kb % 16
                    c.op('tensor', lambda e, P=P, r_=r_, bb=bb, kb=kb: e.matmul(
                        ops_[:], V[:, r_, bb * 128:(bb + 1) * 128], P[:], start=(kb == 0), stop=(kb == 127)),
                        reads=[rV[slot], rP], writes=[rO] if kb == 0 else [], accs=[rO] if kb == 127 else [],
                        sig=(kb in (0, 127)))
                    if kb == 0:
                        c.op('vector', lambda e, P=P: e.tensor_copy(out=dacc[:], in_=P[:]), reads=[rP], writes=[rda])
                    else:
                        c.op('vector', lambda e, P=P: e.tensor_tensor(out=dacc[:], in0=dacc[:], in1=P[:], op=ALU.add),
                             reads=[rP, rda], accs=[rda])
                c.op('tensor', lambda e: e.matmul(dps_[:], G['ones32'][:], dacc[:], start=True, stop=True),
                     reads=[rda, R['const']], writes=[rD])
                c.op('vector', lambda e: e.reciprocal(out=b['rd'][:], in_=dps_[:]), reads=[rD], writes=[rrd])
                oi, o, ro = orot.nexti()
                c.op('vector', lambda e, o=o: e.tensor_tensor(out=o[:], in0=ops_[:], in1=b['rd'][:], op=ALU.mult),
                     reads=[rO, rrd], writes=[ro])
                c.dma('sync', G['mixT'][1024 + h * 128:1024 + (h + 1) * 128, q0:q0 + 512], o[:], f'ste1{oi}', reads=[ro], accs=[R['mixT']])

        stream(list(range(8)), 2, load, compute)

    if LAST_PHASE >= 6:
        c.run_phase(pe0_alloc, pe0_prog)
        c.run_phase(pe_alloc, pe_prog)

    if LAST_PHASE >= 7:
        a, p = make_mixmlp(1, G['w_out_o'], G['w_up1'], G['w_down1'], G['x1T'], 0, R['x1T'], 'mlp_norm1', True)
        c.run_phase(a, p)

    if DEBUG:
        def dbg_alloc(nc, st):
            return {}

        def dbg_prog(c, b):
            for kc in range(16):
                c.dma('sync', G['dbg'][kc * 128:(kc + 1) * 128, :], G['mixT'][kc * 128:(kc + 1) * 128, :], 'dbg1', reads=[R['mixT']], accs=[R['dbg']])
                c.dma('sync', G['dbg2'][kc * 128:(kc + 1) * 128, :], G['x1T'][kc * 128:(kc + 1) * 128, :], 'dbg1', reads=[R['x1T']], accs=[R['dbg']])
        c.run_phase(dbg_alloc, dbg_prog)

    stack.close()
    return nc


_NC_CACHE = {}


def _chunked(v, nch):
    return np.ascontiguousarray(np.asarray(v, np.float32).reshape(nch, 128).T)


def _host_consts():
    n = np.arange(128, dtype=np.float64)
    ang = 2.0 * np.pi * np.outer(n, n) / 128.0
    C1, S1 = np.cos(ang), np.sin(ang)
    dft = np.concatenate([C1, -S1, S1, C1], axis=1).astype(np.float32)
    return dft


def _prep_inputs(inp):
    f32 = np.float32
    x = np.asarray(inp['x'], f32)[0]
    xT = np.zeros((D, S + 32), f32)
    xT[:, 16:16 + S] = x.T
    w_in_o = np.asarray(inp['w_in_o'], f32)[0]
    kr = w_in_o[:, 3840:3904]
    w_in_o2 = np.concatenate([w_in_o, kr[:, 32:64], kr[:, 0:32]], axis=1)
    w_uq = np.asarray(inp['w_uq'], f32)[0].reshape(512, 8, 192)
    qn = w_uq[:, :, 0:128].reshape(512, 1024)
    qr = w_uq[:, :, 128:192]
    qrs = np.concatenate([qr[:, :, 32:64], qr[:, :, 0:32]], axis=2)
    w_uq3 = np.ascontiguousarray(np.concatenate([qn, qr.reshape(512, 512), qrs.reshape(512, 512)], axis=1))
    w_ukv = np.asarray(inp['w_ukv'], f32)[0].reshape(256, 8, 256)
    w_ukv2 = np.ascontiguousarray(np.concatenate([w_ukv[:, :, 0:128].reshape(256, 1024),
                                                  w_ukv[:, :, 128:256].reshape(256, 1024)], axis=1))
    vecs = np.zeros((128, NV), f32)

    def put(name, arr, nch):
        vecs[:, VOFF[name]:VOFF[name] + nch] = _chunked(arr, nch)
    put('mix_norm_e', inp['mix_norm_e'][0], 16)
    put('mlp_norm0', inp['mlp_norm'][0], 16)
    put('mix_norm_o', inp['mix_norm_o'][0], 16)
    put('mlp_norm1', inp['mlp_norm'][1], 16)
    put('final_norm', inp['final_norm'], 16)
    put('conv_a_b', inp['conv_a_b'][0], 8)
    put('ln_a_g', inp['ln_a_g'][0], 8)
    put('ln_a_b', inp['ln_a_b'][0], 8)
    caw = np.asarray(inp['conv_a_w'], f32)[0]
    for k in range(31):
        vecs[:, VOFF['conv_a_w'] + k * 8:VOFF['conv_a_w'] + (k + 1) * 8] = _chunked(caw[k], 8)
    ccw = np.asarray(inp['conv_c_w'], f32)[0]
    for k in range(3):
        vecs[:, VOFF['conv_c_w'] + k * 8:VOFF['conv_c_w'] + (k + 1) * 8] = _chunked(ccw[k], 8)
    put('q_norm_g', inp['q_norm_g'][0], 4)
    put('kv_norm_g', inp['kv_norm_g'][0], 2)
    dft = _host_consts()
    common = {
        'w_in_e': np.ascontiguousarray(np.asarray(inp['w_in_e'], f32)[0]),
        'w_out_e': np.ascontiguousarray(np.asarray(inp['w_out_e'], f32)[0]),
        'w_in_o': np.ascontiguousarray(w_in_o2),
        'w_uq3': w_uq3, 'w_ukv2': w_ukv2,
        'w_out_o': np.ascontiguousarray(np.asarray(inp['w_out_o'], f32)[0]),
        'w_up0': np.ascontiguousarray(np.asarray(inp['w_up'], f32)[0]),
        'w_up1': np.ascontiguousarray(np.asarray(inp['w_up'], f32)[1]),
        'w_down0': np.ascontiguousarray(np.asarray(inp['w_down'], f32)[0]),
        'w_down1': np.ascontiguousarray(np.asarray(inp['w_down'], f32)[1]),
        'vecs': vecs, 'dft': dft,
    }
    inv = 1.0 / (10000.0 ** (np.arange(0, 64, 2, dtype=np.float32) / 64.0))
    in_maps = []
    n2 = np.arange(128, dtype=np.float64)[:, None]
    nrm = 2.0 ** -10.5
    for cidx in range(NCORES):
        m = dict(common)
        m['xT'] = np.ascontiguousarray(xT[:, cidx * T:cidx * T + T + 32])
        k1 = np.arange(128)[:, None]
        k2 = np.arange(16)[None, :]
        kk = (cidx * T + 128 * k2 + k1).reshape(1, 2048).astype(np.float64)
        ang = 2.0 * np.pi * ((n2 * kk) % S) / S
        m['tcs'] = np.concatenate([np.cos(ang) * nrm, np.sin(ang) * nrm], axis=1).astype(np.float32)
        pos = np.arange(cidx * T, (cidx + 1) * T, dtype=np.float32)
        a = (pos[:, None] * inv[None, :]).T
        cs, sn = np.cos(a), np.sin(a)
        CC = np.concatenate([cs, cs], axis=0)
        SS = np.concatenate([-sn, sn], axis=0)
        m['rope'] = np.ascontiguousarray(np.concatenate([CC, SS], axis=1).astype(np.float32))
        sel = np.zeros((8, 2), np.float32)
        if cidx > 0:
            sel[cidx - 1, 0] = 1.0
        if cidx < NCORES - 1:
            sel[cidx + 1, 1] = 1.0
        m['sel'] = sel
        m['ident'] = np.eye(128, dtype=np.float32)
        in_maps.append(m)
    return in_maps


def kernel(**inputs):
    if 'nc' not in _NC_CACHE:
        _NC_CACHE['nc'] = build_program()
    nc = _NC_CACHE['nc']
    in_maps = _prep_inputs(inputs)
    res = run_bass_kernel_spmd(nc, in_maps, core_ids=list(range(NCORES)))
    out = np.concatenate([np.asarray(res.results[i]['outT']).T for i in range(NCORES)], axis=0)
    if DEBUG:
        _NC_CACHE['dbg'] = [np.asarray(res.results[i]['dbg']) for i in range(NCORES)]
        _NC_CACHE['dbg2'] = [np.asarray(res.results[i]['dbg2']) for i in range(NCORES)]
    return np.ascontiguousarray(out[None].astype(np.float32))
```

```python
import numpy as np
from contextlib import ExitStack
import concourse.bass as bass
import concourse.mybir as mybir
from concourse.bass_utils import run_bass_kernel_spmd

F32 = mybir.dt.float32
BF16 = mybir.dt.bfloat16
AF = mybir.ActivationFunctionType
ALU = mybir.AluOpType

NCORES = 8
S = 16384
T = 2048
D = 2048
EPS = 1e-6
ENG_NAMES = ['sync', 'scalar', 'vector', 'gpsimd', 'tensor']
LAST_PHASE = 99
DEBUG = False
DEBUG_SRC = ('mixT', 'mixT')


class Reg:
    __slots__ = ('w', 'r')

    def __init__(self):
        self.w = {}
        self.r = {}


class Rot:
    def __init__(self, c, bufs):
        self.bufs = bufs
        self.regs = [c.reg() for _ in bufs]
        self.i = 0

    def next(self):
        i = self.i % len(self.bufs)
        self.i += 1
        return self.bufs[i], self.regs[i]

    def nexti(self):
        i = self.i % len(self.bufs)
        self.i += 1
        return i, self.bufs[i], self.regs[i]


class Ctx:
    def __init__(self, nc, stack):
        self.nc = nc
        self.stack = stack
        self.sems = {}
        self.cur = None
        self.eng = None
        self.count = {}
        self.waited = {}
        self.waited_by = {n: {} for n in ENG_NAMES}
        self.pregs = []

    def reg(self):
        return Reg()

    def preg(self):
        r = Reg()
        self.pregs.append(r)
        return r

    def snapshot(self):
        return (dict(self.count), [(dict(r.w), dict(r.r)) for r in self.pregs])

    def restore(self, snap):
        self.count = dict(snap[0])
        for r, (w, rr) in zip(self.pregs, snap[1]):
            r.w = dict(w)
            r.r = dict(rr)

    def _deps(self, reads, writes, accs, extra):
        need = {}
        for r in reads:
            for s, v in r.w.items():
                if need.get(s, 0) < v:
                    need[s] = v
        for r in writes:
            for d in (r.w, r.r):
                for s, v in d.items():
                    if need.get(s, 0) < v:
                        need[s] = v
        for r in accs:
            for s, v in r.r.items():
                if need.get(s, 0) < v:
                    need[s] = v
        for t in extra:
            if t is not None and need.get(t[0], 0) < t[1]:
                need[t[0]] = t[1]
        return need

    def _emit_waits(self, need, skip_sem=None):
        for s, v in need.items():
            if s == skip_sem:
                continue
            if self.waited.get(s, 0) >= v:
                continue
            self.eng.wait_ge(self.sems[s], v)
            self.waited[s] = v

    def _update(self, tok, reads, writes, accs):
        s, v = tok
        for r in reads:
            if r.r.get(s, 0) < v:
                r.r[s] = v
        for r in writes:
            r.w = {s: v}
            r.r = {}
        for r in accs:
            if r.w.get(s, 0) < v:
                r.w[s] = v

    def op(self, engname, fn, reads=(), writes=(), accs=(), extra=(), sig=True):
        sname = 'm_' + engname
        if not sig:
            if engname == self.cur:
                need = self._deps(reads, writes, accs, extra)
                self._emit_waits(need, skip_sem=sname if engname == 'tensor' else None)
                fn(self.eng)
            return None
        need = self._deps(reads, writes, accs, extra)
        for r in accs:
            for s_, v_ in r.w.items():
                if s_ != sname and need.get(s_, 0) < v_:
                    need[s_] = v_
        self.count[sname] = self.count.get(sname, 0) + 1
        tok = (sname, self.count[sname])
        if engname == self.cur:
            self._emit_waits(need, skip_sem=sname if engname == 'tensor' else None)
            ins = fn(self.eng)
            ins.then_inc(self.sems[sname], 1)
        self._update(tok, reads, writes, accs)
        return tok

    def dma(self, qname, out, in_, sem, reads=(), writes=(), accs=(), extra=(), fn=None, inc=16):
        need = self._deps(reads, writes, accs, extra)
        self.count[sem] = self.count.get(sem, 0) + inc
        tok = (sem, self.count[sem])
        if qname == self.cur:
            self._emit_waits(need)
            if fn is not None:
                ins = fn(self.eng)
            else:
                ins = self.eng.dma_start(out=out, in_=in_)
            ins.then_inc(self.sems[sem], inc)
        self._update(tok, reads, writes, accs)
        return tok

    def barrier(self, include_cc=False):
        if self.cur is not None:
            self._emit_waits({k: v for k, v in self.count.items() if include_cc or not k.startswith('cc')})

    def run_phase(self, alloc_fn, prog_fn):
        nc = self.nc
        with ExitStack() as st:
            bufs = alloc_fn(nc, st)
            snap = self.snapshot()
            self.cur = None
            self.eng = None
            prog_fn(self, bufs)
            end = self.snapshot()
            for name in sorted(self.count.keys()):
                if name not in self.sems:
                    self.sems[name] = self.stack.enter_context(nc.semaphore(name))
            with nc.Block() as block:
                for name in ENG_NAMES:
                    def body(eng, name=name):
                        self.restore(snap)
                        self.cur = name
                        self.eng = eng
                        self.waited = self.waited_by[name]
                        prog_fn(self, bufs)
                        self.barrier()
                    getattr(block, name)(body)
            self.cur = None
            self.eng = None
            self.restore(end)


def stream(items, nslots, load_fn, compute_fn):
    n = len(items)
    for i in range(min(nslots - 1, n)):
        load_fn(items[i], i % nslots)
    for i in range(n):
        j = i + nslots - 1
        if j < n:
            load_fn(items[j], j % nslots)
        compute_fn(items[i], i % nslots)


VOFF = {}
_o = 0
for _name, _n in [('mix_norm_e', 16), ('mlp_norm0', 16), ('mix_norm_o', 16), ('mlp_norm1', 16),
                  ('final_norm', 16), ('conv_a_b', 8), ('ln_a_g', 8), ('ln_a_b', 8),
                  ('conv_a_w', 31 * 8), ('conv_c_w', 3 * 8), ('q_norm_g', 4), ('kv_norm_g', 2)]:
    VOFF[_name] = _o
    _o += _n
NV = _o


def build_program():
    nc = bass.Bass("TRN2", target_bir_lowering=False)
    stack = ExitStack()
    G = {}

    def din(name, shape, dt=F32):
        G[name] = nc.dram_tensor(name, list(shape), dt, kind="ExternalInput").ap()

    din('xT', [D, T + 32])
    din('w_in_e', [D, 3072])
    din('w_out_e', [D, D])
    din('w_in_o', [D, 3968])
    din('w_uq3', [512, 2048])
    din('w_ukv2', [256, 2048])
    din('w_out_o', [D, D])
    din('w_up0', [D, 8192])
    din('w_up1', [D, 8192])
    din('w_down0', [8192, D])
    din('w_down1', [8192, D])
    din('vecs', [128, NV])
    din('dft', [128, 512])
    din('tcs', [128, 2 * 2048])
    din('rope', [64, 2 * 2048])
    din('sel', [8, 2])
    din('ident', [128, 128])
    G['outT'] = nc.dram_tensor('outT', [D, T], F32, kind="ExternalOutput").ap()
    if DEBUG:
        G['dbg'] = nc.dram_tensor('dbg', [D, T], BF16, kind="ExternalOutput").ap()
        G['dbg2'] = nc.dram_tensor('dbg2', [D, T], F32, kind="ExternalOutput").ap()
    G['mixT'] = nc.dram_tensor('mixT', [D, T], BF16).ap()
    G['agin1'] = nc.dram_tensor('agin1', [16, 262144], BF16).ap()
    G['agout1'] = nc.dram_tensor('agout1', [128, 262144], BF16).ap()
    G['x1T'] = nc.dram_tensor('x1T', [D, T], F32).ap()
    for hf in range(2):
        G[f'ag2in{hf}'] = nc.dram_tensor(f'ag2in{hf}', [2112, 1024], BF16).ap()
        G[f'ag2out{hf}'] = nc.dram_tensor(f'ag2out{hf}', [8 * 2112, 1024], BF16).ap()
    G['ag3in'] = nc.dram_tensor('ag3in', [1, 2048], F32).ap()
    G['ag3out'] = nc.dram_tensor('ag3out', [8, 2048], F32).ap()
    G['qT'] = nc.dram_tensor('qT', [1536, T], BF16).ap()
    G['vT'] = nc.dram_tensor('vT', [1024, T], F32).ap()
    G['bT'] = nc.dram_tensor('bT', [1024, T], F32).ap()

    def gsb(name, shape, dt):
        return stack.enter_context(nc.sbuf_tensor(name, shape, dt))

    G['vecs_sb'] = gsb('vecs_sb', [128, NV], F32)
    G['ones_bf'] = gsb('ones_bf', [128, 128], BF16)
    G['ones32'] = gsb('ones32', [128, 128], F32)
    G['dft_sb'] = gsb('dft_sb', [128, 512], BF16)
    G['ident32'] = gsb('ident32', [128, 128], F32)
    G['ps'] = [stack.enter_context(nc.psum_tensor(f'ps{i}', [128, 512], F32)) for i in range(8)]

    c = Ctx(nc, stack)
    R = {k: c.preg() for k in ['mixT', 'agin1', 'agout1', 'x1T', 'ag2in0', 'ag2out0', 'ag2in1', 'ag2out1', 'ag3in', 'ag3out',
                               'qT', 'vT', 'bT', 'outT', 'const', 'dbg']}

    def vec(name, i):
        o = VOFF[name] + i
        return G['vecs_sb'][:, o:o + 1]

    def p0_alloc(nc, st):
        return {}

    def p0_prog(c, b):
        c.dma('sync', G['vecs_sb'][:], G['vecs'][:], 'ld0', accs=[R['const']])
        c.dma('gpsimd', G['dft_sb'][:], G['dft'][:], 'ld1', accs=[R['const']])
        c.dma('sync', G['ident32'][:], G['ident'][:], 'ld2', accs=[R['const']])
        c.op('vector', lambda e: e.memset(G['ones_bf'][:], 1.0), accs=[R['const']])
        c.op('vector', lambda e: e.memset(G['ones32'][:], 1.0), accs=[R['const']])

    c.run_phase(p0_alloc, p0_prog)

    def rstd_from(c, ps_ap, rps, out_ap, rout, n, acc=False):
        c.op('scalar', lambda e: e.activation(out=out_ap, in_=ps_ap, func=AF.Sqrt, bias=EPS, scale=1.0 / n),
             reads=[rps], accs=[rout] if acc else [], writes=[] if acc else [rout])
        c.op('vector', lambda e: e.reciprocal(out=out_ap, in_=out_ap), reads=[rout], accs=[rout])

    def pa_alloc(nc, st):
        sb = lambda name, shape, dt: st.enter_context(nc.sbuf_tensor(name, shape, dt))
        b = {}
        b['xs'] = sb('a_xs', [128, 16, 544], F32)
        b['hT'] = sb('a_hT', [128, 16, 544], BF16)
        b['w'] = [sb(f'a_w{i}', [128, 16, 512], BF16) for i in range(2)]
        b['val'] = sb('a_val', [128, 8, 544], F32)
        b['sigT'] = sb('a_sigT', [128, 8, 544], F32)
        b['gT'] = sb('a_gT', [128, 8, 544], F32)
        b['acc'] = sb('a_acc', [128, 8, 512], F32)
        b['sq'] = [sb(f'a_sq{i}', [128, 544], BF16) for i in range(2)]
        b['ub'] = [sb(f'a_ub{i}', [128, 512], BF16) for i in range(2)]
        b['stg'] = [sb(f'a_stg{i}', [128, 2, 512], BF16) for i in range(2)]
        b['rstd'] = sb('a_rstd', [128, 544], F32)
        b['ln'] = sb('a_ln', [128, 5, 512], F32)
        b['t'] = [sb(f'a_t{i}', [128, 512], F32) for i in range(2)]
        b['ya'] = [sb(f'a_ya{i}', [128, 512], BF16) for i in range(2)]
        return b

    def pa_prog(c, b):
        pb = Rot(c, G['ps'])
        wrot_regs = [c.reg() for _ in range(2)]
        sq = Rot(c, b['sq'])
        ubr = Rot(c, b['ub'])
        stg = Rot(c, b['stg'])
        tt = Rot(c, b['t'])
        yar = Rot(c, b['ya'])
        rxs, rhT, rval, rsig, rgT, racc, rrstd, rln = [c.reg() for _ in range(8)]
        agin1 = G['agin1'].rearrange("a (ri g kc n) -> a ri g kc n", ri=2, g=8, kc=128)
        xs, hT = b['xs'], b['hT']
        wstate = {'i': 0}

        def S1(q):
            c.dma('sync', xs[:], G['xT'][:, 512 * q:512 * q + 544].rearrange("(kc p) t -> p kc t", p=128),
                  'lda', writes=[rxs])
            psA, rA = pb.next()
            psB, rB = pb.next()
            for kc in range(16):
                s, rs = sq.next()
                c.op('scalar', lambda e: e.activation(out=s[:], in_=xs[:, kc, :], func=AF.Square),
                     reads=[rxs], writes=[rs])
                c.op('tensor', lambda e: e.matmul(psA[:], G['ones_bf'][:], s[:, 0:512], start=(kc == 0), stop=(kc == 15)),
                     reads=[rs, R['const']], writes=[rA] if kc == 0 else [], accs=[rA] if kc else [])
                c.op('tensor', lambda e: e.matmul(psB[:, 0:32], G['ones_bf'][:], s[:, 512:544], start=(kc == 0), stop=(kc == 15)),
                     reads=[rs, R['const']], writes=[rB] if kc == 0 else [], accs=[rB] if kc else [])
            rstd_from(c, psA[:], rA, b['rstd'][:, 0:512], rrstd, D)
            rstd_from(c, psB[:, 0:32], rB, b['rstd'][:, 512:544], rrstd, D, acc=True)
            for kc in range(16):
                c.op('vector', lambda e: e.scalar_tensor_tensor(
                    out=hT[:, kc, :], in0=xs[:, kc, :], scalar=vec('mix_norm_e', kc), in1=b['rstd'][:],
                    op0=ALU.mult, op1=ALU.mult),
                    reads=[rxs, rrstd, R['const']], writes=[rhT] if kc == 0 else [], accs=[rhT] if kc else [])

        groups = [(2048, 'ub', 0), (2560, 'ub', 1), (0, 'val', 0), (512, 'val', 1), (1024, 'gate', 0), (1536, 'gate', 1)]

        def INPROJ(q):
            def load_w(item, slot_unused):
                slot = wstate['i'] % 2
                wstate['i'] += 1
                wstate[(q, item[0])] = slot
                col0 = item[0]
                c.dma('gpsimd', b['w'][slot][:], G['w_in_e'][:, col0:col0 + 512].rearrange("(kc p) n -> p kc n", p=128),
                      f'w{slot}', writes=[wrot_regs[slot]])

            def compute(item, slot_unused):
                col0, kind, gi = item
                slot = wstate[(q, col0)]
                w = b['w'][slot]
                rw = wrot_regs[slot]
                for mi in range(4):
                    ch = gi * 4 + mi
                    parts = [(16, 512)] if kind == 'ub' else [(0, 512), (512, 32)]
                    for (c0, n) in parts:
                        ps, rp = pb.next()
                        for kc in range(16):
                            c.op('tensor', lambda e: e.matmul(
                                ps[:, 0:n], w[:, kc, mi * 128:(mi + 1) * 128], hT[:, kc, c0:c0 + n],
                                start=(kc == 0), stop=(kc == 15)),
                                reads=[rw, rhT], writes=[rp] if kc == 0 else [], accs=[rp] if kc == 15 else [],
                                sig=(kc in (0, 15)))
                        if kind == 'val':
                            c.op('scalar', lambda e: e.activation(out=b['val'][:, ch, c0:c0 + n], in_=ps[:, 0:n], func=AF.Copy),
                                 reads=[rp], accs=[rval])
                        elif kind == 'gate':
                            c.op('scalar', lambda e: e.activation(out=b['sigT'][:, ch, c0:c0 + n], in_=ps[:, 0:n], func=AF.Sigmoid),
                                 reads=[rp], accs=[rsig])
                        else:
                            u, ru = ubr.next()
                            c.op('scalar', lambda e: e.activation(out=u[:], in_=ps[:], func=AF.Copy), reads=[rp], writes=[ru])
                            g = ch
                            sgi, sgb, rsgb = stg.nexti()
                            for ri in range(2):
                                ps2, rp2 = pb.next()
                                c.op('tensor', lambda e: e.matmul(ps2[:], G['dft_sb'][:, ri * 128:(ri + 1) * 128], u[:], start=True, stop=True),
                                     reads=[ru, R['const']], writes=[rp2])
                                c.op('scalar', lambda e: e.activation(out=sgb[:, ri, :], in_=ps2[:], func=AF.Copy),
                                     reads=[rp2], writes=[rsgb] if ri == 0 else [], accs=[rsgb] if ri else [])
                            for ri in range(2):
                                c.dma('sync', agin1[4 * q:4 * q + 4, ri, g, :, :].rearrange("a k n -> k a n"),
                                      sgb[:, ri, :].rearrange("k (a n) -> k a n", a=4), f'sta1{sgi}',
                                      reads=[rsgb], accs=[R['agin1']])

            n = len(groups)
            load_w(groups[0], None)
            for i in range(n):
                if i + 1 < n:
                    load_w(groups[i + 1], None)
                compute(groups[i], None)

        def GLU(q):
            for ch in range(8):
                c.op('vector', lambda e: e.tensor_tensor(out=b['gT'][:, ch, :], in0=b['val'][:, ch, :], in1=b['sigT'][:, ch, :], op=ALU.mult),
                     reads=[rval, rsig], writes=[rgT] if ch == 0 else [], accs=[rgT] if ch else [])

        def CONV_LN(q):
            for ch in range(8):
                c.op('vector', lambda e: e.tensor_scalar(
                    out=b['acc'][:, ch, :], in0=b['gT'][:, ch, 1:513], scalar1=vec('conv_a_w', 0 * 8 + ch),
                    scalar2=vec('conv_a_b', ch), op0=ALU.mult, op1=ALU.add),
                    reads=[rgT, R['const']], writes=[racc] if ch == 0 else [], accs=[racc] if ch else [])
                for k in range(1, 31):
                    c.op('vector', lambda e: e.scalar_tensor_tensor(
                        out=b['acc'][:, ch, :], in0=b['gT'][:, ch, 1 + k:513 + k], scalar=vec('conv_a_w', k * 8 + ch),
                        in1=b['acc'][:, ch, :], op0=ALU.mult, op1=ALU.add),
                        reads=[rgT, racc], accs=[racc])
            psM, rM = pb.next()
            psQ, rQ = pb.next()
            for ch in range(8):
                c.op('tensor', lambda e: e.matmul(psM[:], G['ones32'][:], b['acc'][:, ch, :], start=(ch == 0), stop=(ch == 7)),
                     reads=[racc, R['const']], writes=[rM] if ch == 0 else [], accs=[rM] if ch else [])
                s, rs = sq.next()
                c.op('scalar', lambda e: e.activation(out=s[:, 0:512], in_=b['acc'][:, ch, :], func=AF.Square),
                     reads=[racc], writes=[rs])
                c.op('tensor', lambda e: e.matmul(psQ[:], G['ones_bf'][:], s[:, 0:512], start=(ch == 0), stop=(ch == 7)),
                     reads=[rs], writes=[rQ] if ch == 0 else [], accs=[rQ] if ch else [])
            ln = b['ln']
            mean, msq, var, rs2, nmr = [ln[:, i, :] for i in range(5)]
            c.op('vector', lambda e: e.tensor_scalar(out=mean, in0=psM[:], scalar1=1.0 / 1024, scalar2=None, op0=ALU.mult),
                 reads=[rM], writes=[rln])
            c.op('vector', lambda e: e.tensor_tensor(out=msq, in0=mean, in1=mean, op=ALU.mult), reads=[rln], accs=[rln])
            c.op('vector', lambda e: e.scalar_tensor_tensor(out=var, in0=psQ[:], scalar=1.0 / 1024, in1=msq,
                                                           op0=ALU.mult, op1=ALU.subtract),
                 reads=[rQ, rln], accs=[rln])
            c.op('scalar', lambda e: e.activation(out=rs2, in_=var, func=AF.Sqrt, bias=EPS, scale=1.0), reads=[rln], accs=[rln])
            c.op('vector', lambda e: e.reciprocal(out=rs2, in_=rs2), reads=[rln], accs=[rln])
            c.op('vector', lambda e: e.scalar_tensor_tensor(out=nmr, in0=mean, scalar=-1.0, in1=rs2,
                                                           op0=ALU.mult, op1=ALU.mult), reads=[rln], accs=[rln])
            for ch in range(8):
                t, rt = tt.next()
                c.op('vector', lambda e: e.tensor_tensor(out=t[:], in0=b['acc'][:, ch, :], in1=rs2, op=ALU.mult),
                     reads=[racc, rln], writes=[rt])
                c.op('vector', lambda e: e.tensor_tensor(out=t[:], in0=t[:], in1=nmr, op=ALU.add),
                     reads=[rt, rln], accs=[rt])
                yi, ya, rya = yar.nexti()
                c.op('scalar', lambda e: e.activation(out=ya[:], in_=t[:], func=AF.Silu, bias=vec('ln_a_b', ch), scale=vec('ln_a_g', ch)),
                     reads=[rt, R['const']], writes=[rya])
                c.dma('sync', G['mixT'][ch * 128:(ch + 1) * 128, 512 * q:512 * q + 512], ya[:], f'sta2{yi}',
                      reads=[rya], accs=[R['mixT']])

        S1(0)
        INPROJ(0)
        for q in range(4):
            if q + 1 < 4:
                S1(q + 1)
            GLU(q)
            if q + 1 < 4:
                INPROJ(q + 1)
            CONV_LN(q)

    c.run_phase(pa_alloc, pa_prog)

    def ag_phase(kind_in, kind_out, semname):
        def alloc(nc, st):
            return {}

        def prog(c, b):
            c.dma('gpsimd', None, None, semname, reads=[R[kind_in]], writes=[R[kind_out]], inc=1,
                  fn=lambda e: e.collective_compute("AllGather", ALU.bypass, replica_groups=[list(range(NCORES))],
                                                    ins=[G[kind_in][:]], outs=[G[kind_out][:]]))
        c.run_phase(alloc, prog)

    if LAST_PHASE >= 1:
        ag_phase('agin1', 'agout1', 'cc1')

    def pb_alloc(nc, st):
        sb = lambda name, shape, dt: st.enter_context(nc.sbuf_tensor(name, shape, dt))
        b = {}
        b['At'] = [sb(f'b_At{i}', [128, 2, 64, 128], BF16) for i in range(2)]
        b['Bt'] = [sb(f'b_Bt{i}', [128, 2, 64, 128], BF16) for i in range(2)]
        b['tcs'] = sb('b_tcs', [128, 2, 2048], BF16)
        b['yb'] = [sb(f'b_yb{i}', [128, 2048], BF16) for i in range(2)]
        return b

    def pb_prog(c, b):
        pbk = Rot(c, G['ps'][0:4])
        ybanks = G['ps'][4:8]
        rY = [c.reg() for _ in range(4)]
        rAt = [c.reg() for _ in range(2)]
        Btr = Rot(c, b['Bt'])
        ybr = Rot(c, b['yb'])
        rtcs = c.reg()
        c.dma('gpsimd', b['tcs'][:], G['tcs'].rearrange("p (a n) -> p a n", a=2), 'ldb0', writes=[rtcs])
        agv = G['agout1'].rearrange("a (ri g kc n) -> a ri g kc n", ri=2, g=8, kc=128)
        items = [(g, kh) for g in range(8) for kh in range(2)]

        def load(item, slot):
            g, kh = item
            for ri in range(2):
                c.dma('sync', b['At'][slot][:, ri, :, :], agv[:, ri, g, kh * 64:(kh + 1) * 64, :], f'ldb{slot + 1}',
                      reads=[R['agout1']], writes=[rAt[slot]] if ri == 0 else [], accs=[rAt[slot]] if ri else [])

        def compute(item, slot):
            g, kh = item
            At = b['At'][slot]
            Bt, rBt = Btr.next()
            first = True
            for kcp in range(32):
                ps, rp = pbk.next()
                for j in range(2):
                    kc = 2 * kcp + j
                    c.op('tensor', lambda e, ps=ps, j=j, kc=kc: e.matmul(
                        ps[:, j * 256:(j + 1) * 256], At[:, 0, kc, :], G['dft_sb'][:, 0:256], start=True, stop=False),
                        reads=[rAt[slot], R['const']], writes=[rp] if j == 0 else [], sig=(j == 0))
                    c.op('tensor', lambda e, ps=ps, j=j, kc=kc: e.matmul(
                        ps[:, j * 256:(j + 1) * 256], At[:, 1, kc, :], G['dft_sb'][:, 256:512], start=False, stop=True),
                        reads=[rAt[slot]], accs=[rp] if j == 1 else [], sig=(j == 1))
                src = ps[:].rearrange("p (j ri k) -> p ri j k", j=2, ri=2)
                dst = Bt[:, :, 2 * kcp:2 * kcp + 2, :]
                if kcp % 2 == 0:
                    c.op('scalar', lambda e, src=src, dst=dst: e.activation(out=dst, in_=src, func=AF.Copy),
                         reads=[rp], writes=[rBt] if first else [], accs=[] if first else [rBt])
                else:
                    c.op('vector', lambda e, src=src, dst=dst: e.tensor_copy(out=dst, in_=src),
                         reads=[rp], writes=[rBt] if first else [], accs=[] if first else [rBt])
                first = False
            for k1 in range(128):
                bk = k1 // 32
                o = (k1 % 32) * 16
                yb_ps = ybanks[bk]
                fst = (k1 % 32 == 0)
                lst = (k1 % 32 == 31)
                c.op('tensor', lambda e, yb_ps=yb_ps, o=o, k1=k1: e.matmul(
                    yb_ps[0:64, o:o + 16], Bt[:, 0, :, k1], b['tcs'][:, 0, k1 * 16:(k1 + 1) * 16], start=True, stop=False),
                    reads=[rBt, rtcs], writes=[rY[bk]] if fst else [], sig=fst)
                c.op('tensor', lambda e, yb_ps=yb_ps, o=o, k1=k1: e.matmul(
                    yb_ps[0:64, o:o + 16], Bt[:, 1, :, k1], b['tcs'][:, 1, k1 * 16:(k1 + 1) * 16], start=False, stop=True),
                    reads=[rBt], accs=[rY[bk]] if lst else [], sig=lst)
            ybi, yb, ryb = ybr.nexti()
            for bk in range(4):
                src = ybanks[bk][0:64, :].rearrange("p (k1 k2) -> p k2 k1", k2=16)
                dst = yb[0:64, :].rearrange("p (k2 k1) -> p k2 k1", k2=16)[:, :, bk * 32:(bk + 1) * 32]
                c.op('scalar', lambda e, src=src, dst=dst: e.activation(out=dst, in_=src, func=AF.Copy),
                     reads=[rY[bk]], writes=[ryb] if bk == 0 else [], accs=[ryb] if bk else [])
            r0 = 1024 + g * 128 + kh * 64
            c.dma('sync', G['mixT'][r0:r0 + 64, :], yb[0:64, :], f'stb{ybi}', reads=[ryb], accs=[R['mixT']])

        stream(items, 2, load, compute)

    if LAST_PHASE >= 2:
        c.run_phase(pb_alloc, pb_prog)

    def make_mixmlp(layer, W_out, W_up, W_down, xin, xin_off, xin_reg, norm_name, final):
        def alloc(nc, st):
            sb = lambda name, shape, dt: st.enter_context(nc.sbuf_tensor(name, shape, dt))
            b = {}
            b['xacc'] = sb(f'c{layer}_xacc', [128, 16, 1024], F32)
            b['act'] = sb(f'c{layer}_act', [128, 16, 1024], BF16)
            b['w'] = [sb(f'c{layer}_w{i}', [128, 16, 512], BF16) for i in range(2)]
            b['dw'] = [sb(f'c{layer}_dw{i}', [128, 4, 2048], BF16) for i in range(2)]
            b['aT'] = [sb(f'c{layer}_aT{i}', [128, 4, 1024], BF16) for i in range(2)]
            b['rl'] = [sb(f'c{layer}_rl{i}', [128, 512], F32) for i in range(2)]
            b['sq'] = [sb(f'c{layer}_sq{i}', [128, 1024], BF16) for i in range(2)]
            b['rstd'] = sb(f'c{layer}_rstd', [128, 1024], F32)
            return b

        def rmsnorm_acc(c, b, pb, sq, rx, rrstd):
            xacc = b['xacc']
            psA, rA = pb.next()
            psB, rB = pb.next()
            for kc in range(16):
                s, rs = sq.next()
                c.op('scalar', lambda e, kc=kc, s=s: e.activation(out=s[:], in_=xacc[:, kc, :], func=AF.Square),
                     reads=[rx], writes=[rs])
                c.op('tensor', lambda e, kc=kc, s=s: e.matmul(psA[:], G['ones_bf'][:], s[:, 0:512],
                                                              start=(kc == 0), stop=(kc == 15)),
                     reads=[rs, R['const']], writes=[rA] if kc == 0 else [], accs=[rA] if kc else [])
                c.op('tensor', lambda e, kc=kc, s=s: e.matmul(psB[:], G['ones_bf'][:], s[:, 512:1024],
                                                              start=(kc == 0), stop=(kc == 15)),
                     reads=[rs, R['const']], writes=[rB] if kc == 0 else [], accs=[rB] if kc else [])
            rstd_from(c, psA[:], rA, b['rstd'][:, 0:512], rrstd, D)
            rstd_from(c, psB[:], rB, b['rstd'][:, 512:1024], rrstd, D, acc=True)

        def prog(c, b):
            pb = Rot(c, G['ps'])
            sq = Rot(c, b['sq'])
            rl = Rot(c, b['rl'])
            xacc, act = b['xacc'], b['act']
            rx = [c.reg() for _ in range(16)]
            ract, rrstd = c.reg(), c.reg()
            rw = [c.reg() for _ in range(2)]
            rdw = [c.reg() for _ in range(2)]
            raT = [c.reg() for _ in range(2)]
            for half in range(2):
                t0 = half * 1024
                for kc in range(16):
                    c.dma('sync', xacc[:, kc, :], xin[kc * 128:(kc + 1) * 128, xin_off + t0:xin_off + t0 + 1024], 'ldc0',
                          reads=[xin_reg] if xin_reg is not None else [], writes=[rx[kc]])
                for kc in range(16):
                    rx[kc].w = {'ldc0': c.count['ldc0']}
                c.dma('sync', act[:], G['mixT'][:, t0:t0 + 1024].rearrange("(kc p) t -> p kc t", p=128), 'ldc1',
                      reads=[R['mixT']], writes=[ract])

                def load_o(item, slot):
                    c.dma('gpsimd', b['w'][slot][:], W_out[:, item * 512:(item + 1) * 512].rearrange("(kc p) n -> p kc n", p=128),
                          f'w{slot}', writes=[rw[slot]])

                def comp_o(item, slot):
                    w = b['w'][slot]
                    for mi in range(4):
                        m = item * 4 + mi
                        for tl in range(2):
                            ps, rp = pb.next()
                            for kc in range(16):
                                c.op('tensor', lambda e, ps=ps, kc=kc, w=w, mi=mi, tl=tl: e.matmul(
                                    ps[:], w[:, kc, mi * 128:(mi + 1) * 128], act[:, kc, tl * 512:(tl + 1) * 512],
                                    start=(kc == 0), stop=(kc == 15)),
                                    reads=[rw[slot], ract], writes=[rp] if kc == 0 else [], accs=[rp] if kc == 15 else [],
                                    sig=(kc in (0, 15)))
                            c.op('vector', lambda e, ps=ps, m=m, tl=tl: e.tensor_tensor(
                                out=xacc[:, m, tl * 512:(tl + 1) * 512], in0=xacc[:, m, tl * 512:(tl + 1) * 512], in1=ps[:], op=ALU.add),
                                reads=[rp, rx[m]], accs=[rx[m]])

                stream(list(range(4)), 2, load_o, comp_o)
                psA, rA = pb.next()
                psB, rB = pb.next()
                for kc in range(16):
                    s, rs = sq.next()
                    c.op('scalar', lambda e, kc=kc, s=s: e.activation(out=s[:], in_=xacc[:, kc, :], func=AF.Square),
                         reads=[rx[kc]], writes=[rs])
                    c.op('tensor', lambda e, kc=kc, s=s: e.matmul(psA[:], G['ones_bf'][:], s[:, 0:512],
                                                                  start=(kc == 0), stop=(kc == 15)),
                         reads=[rs, R['const']], writes=[rA] if kc == 0 else [], accs=[rA] if kc else [])
                    c.op('tensor', lambda e, kc=kc, s=s: e.matmul(psB[:], G['ones_bf'][:], s[:, 512:1024],
                                                                  start=(kc == 0), stop=(kc == 15)),
                         reads=[rs, R['const']], writes=[rB] if kc == 0 else [], accs=[rB] if kc else [])
                rstd_from(c, psA[:], rA, b['rstd'][:, 0:512], rrstd, D)
                rstd_from(c, psB[:], rB, b['rstd'][:, 512:1024], rrstd, D, acc=True)
                for kc in range(16):
                    c.op('vector', lambda e, kc=kc: e.scalar_tensor_tensor(
                        out=act[:, kc, :], in0=xacc[:, kc, :], scalar=vec(norm_name, kc), in1=b['rstd'][:],
                        op0=ALU.mult, op1=ALU.mult),
                        reads=[rx[kc], rrstd, R['const']], writes=[ract] if kc == 0 else [], accs=[ract] if kc else [])

                def load_f(F, slot):
                    c.dma('gpsimd', b['w'][slot][:], W_up[:, F * 512:(F + 1) * 512].rearrange("(kc p) n -> p kc n", p=128),
                          f'w{slot}', writes=[rw[slot]])
                    c.dma('gpsimd', b['dw'][slot][:], W_down[F * 512:(F + 1) * 512, :].rearrange("(fc p) n -> p fc n", p=128),
                          f'dw{slot}', writes=[rdw[slot]])

                def comp_f(F, slot):
                    w, dw, aT = b['w'][slot], b['dw'][slot], b['aT'][slot]
                    firsta = True
                    for fi in range(4):
                        for tl in range(2):
                            ps, rp = pb.next()
                            for kc in range(16):
                                c.op('tensor', lambda e, ps=ps, kc=kc, fi=fi, tl=tl: e.matmul(
                                    ps[:], w[:, kc, fi * 128:(fi + 1) * 128], act[:, kc, tl * 512:(tl + 1) * 512],
                                    start=(kc == 0), stop=(kc == 15)),
                                    reads=[rw[slot], ract], writes=[rp] if kc == 0 else [], accs=[rp] if kc == 15 else [],
                                    sig=(kc in (0, 15)))
                            r_, rr_ = rl.next()
                            c.op('scalar', lambda e, ps=ps, r_=r_: e.activation(out=r_[:], in_=ps[:], func=AF.Relu),
                                 reads=[rp], writes=[rr_])
                            c.op('vector', lambda e, r_=r_, fi=fi, tl=tl: e.tensor_tensor(
                                out=aT[:, fi, tl * 512:(tl + 1) * 512], in0=r_[:], in1=r_[:], op=ALU.mult),
                                reads=[rr_], writes=[raT[slot]] if firsta else [], accs=[] if firsta else [raT[slot]])
                            firsta = False
                    for o in range(16):
                        for tl in range(2):
                            ps, rp = pb.next()
                            for fi in range(4):
                                c.op('tensor', lambda e, ps=ps, fi=fi, o=o, tl=tl: e.matmul(
                                    ps[:], dw[:, fi, o * 128:(o + 1) * 128], aT[:, fi, tl * 512:(tl + 1) * 512],
                                    start=(fi == 0), stop=(fi == 3)),
                                    reads=[rdw[slot], raT[slot]], writes=[rp] if fi == 0 else [], accs=[rp] if fi == 3 else [],
                                    sig=(fi in (0, 3)))
                            c.op('vector', lambda e, ps=ps, o=o, tl=tl: e.tensor_tensor(
                                out=xacc[:, o, tl * 512:(tl + 1) * 512], in0=xacc[:, o, tl * 512:(tl + 1) * 512], in1=ps[:], op=ALU.add),
                                reads=[rp, rx[o]], accs=[rx[o]])

                stream(list(range(16)), 2, load_f, comp_f)

                if not final:
                    for kc in range(16):
                        c.dma('sync', G['x1T'][kc * 128:(kc + 1) * 128, t0:t0 + 1024], xacc[:, kc, :], 'stc',
                              reads=[rx[kc]], accs=[R['x1T']])
                    for kc in range(16):
                        rx[kc].r['stc'] = c.count['stc']
                else:
                    psA, rA = pb.next()
                    psB, rB = pb.next()
                    for kc in range(16):
                        s, rs = sq.next()
                        c.op('scalar', lambda e, kc=kc, s=s: e.activation(out=s[:], in_=xacc[:, kc, :], func=AF.Square),
                             reads=[rx[kc]], writes=[rs])
                        c.op('tensor', lambda e, kc=kc, s=s: e.matmul(psA[:], G['ones_bf'][:], s[:, 0:512],
                                                                      start=(kc == 0), stop=(kc == 15)),
                             reads=[rs, R['const']], writes=[rA] if kc == 0 else [], accs=[rA] if kc else [])
                        c.op('tensor', lambda e, kc=kc, s=s: e.matmul(psB[:], G['ones_bf'][:], s[:, 512:1024],
                                                                      start=(kc == 0), stop=(kc == 15)),
                             reads=[rs, R['const']], writes=[rB] if kc == 0 else [], accs=[rB] if kc else [])
                    rstd_from(c, psA[:], rA, b['rstd'][:, 0:512], rrstd, D)
                    rstd_from(c, psB[:], rB, b['rstd'][:, 512:1024], rrstd, D, acc=True)
                    for kc in range(16):
                        c.op('vector', lambda e, kc=kc: e.scalar_tensor_tensor(
                            out=xacc[:, kc, :], in0=xacc[:, kc, :], scalar=vec('final_norm', kc), in1=b['rstd'][:],
                            op0=ALU.mult, op1=ALU.mult),
                            reads=[rrstd, R['const']], writes=[rx[kc]])
                        c.dma('sync', G['outT'][kc * 128:(kc + 1) * 128, t0:t0 + 1024], xacc[:, kc, :], 'stc',
                              reads=[rx[kc]], accs=[R['outT']])
                    for kc in range(16):
                        rx[kc].r['stc'] = c.count['stc']

        return alloc, prog

    if LAST_PHASE >= 3:
        a, p = make_mixmlp(0, G['w_out_e'], G['w_up0'], G['w_down0'], G['xT'], 16, None, 'mlp_norm0', False)
        c.run_phase(a, p)

    def pd_alloc(nc, st):
        sb = lambda name, shape, dt: st.enter_context(nc.sbuf_tensor(name, shape, dt))
        b = {}
        b['xc'] = [sb(f'd_xc{i}', [128, 1024], F32) for i in range(3)]
        b['act'] = sb('d_act', [128, 16, 1024], BF16)
        b['w'] = [sb(f'd_w{i}', [128, 16, 512], BF16) for i in range(3)]
        b['csb'] = sb('d_csb', [128, 4, 1024], F32)
        b['cq'] = sb('d_cq', [128, 4, 1024], F32)
        b['ckv'] = sb('d_ckv', [128, 2, 1024], F32)
        b['qn'] = sb('d_qn', [128, 4, 1024], BF16)
        b['kvn'] = sb('d_kvn', [128, 2, 1024], BF16)
        b['s32'] = [sb(f'd_s32{i}', [128, 512], F32) for i in range(3)]
        b['s16'] = [sb(f'd_s16{i}', [128, 512], BF16) for i in range(3)]
        b['sq'] = [sb(f'd_sq{i}', [128, 1024], BF16) for i in range(2)]
        b['rstd'] = sb('d_rstd', [128, 1024], F32)
        b['rope'] = sb('d_rope', [64, 2, 2048], F32)
        b['t1'] = [sb(f'd_t1{i}', [64, 512], F32) for i in range(2)]
        b['t2'] = [sb(f'd_t2{i}', [64, 512], F32) for i in range(2)]
        b['brow'] = sb('d_brow', [1, 2048], F32)
        return b

    def pd_prog(c, b):
        pb = Rot(c, G['ps'])
        xcr = Rot(c, b['xc'])
        sq = Rot(c, b['sq'])
        s32 = Rot(c, b['s32'])
        s16 = Rot(c, b['s16'])
        t1r = Rot(c, b['t1'])
        t2r = Rot(c, b['t2'])
        act = b['act']
        ract, rrstd, rcsb, rcq, rckv, rqn, rkvn, rrope, rbrow = [c.reg() for _ in range(9)]
        rw = [c.reg() for _ in range(3)]
        c.dma('sync', b['rope'][:], G['rope'].rearrange("p (a n) -> p a n", a=2), 'ldd9', writes=[rrope])

        def small_norm(src, rsrc, nch, gname, dst, rdst):
            psA, rA = pb.next()
            psB, rB = pb.next()
            for kc in range(nch):
                s, rs = sq.next()
                c.op('scalar', lambda e, kc=kc, s=s: e.activation(out=s[:], in_=src[:, kc, :], func=AF.Square),
                     reads=[rsrc], writes=[rs])
                c.op('tensor', lambda e, kc=kc, s=s: e.matmul(psA[:], G['ones_bf'][:], s[:, 0:512],
                                                              start=(kc == 0), stop=(kc == nch - 1)),
                     reads=[rs, R['const']], writes=[rA] if kc == 0 else [], accs=[rA] if kc else [])
                c.op('tensor', lambda e, kc=kc, s=s: e.matmul(psB[:], G['ones_bf'][:], s[:, 512:1024],
                                                              start=(kc == 0), stop=(kc == nch - 1)),
                     reads=[rs, R['const']], writes=[rB] if kc == 0 else [], accs=[rB] if kc else [])
            rstd_from(c, psA[:], rA, b['rstd'][:, 0:512], rrstd, nch * 128)
            rstd_from(c, psB[:], rB, b['rstd'][:, 512:1024], rrstd, nch * 128, acc=True)
            for kc in range(nch):
                c.op('vector', lambda e, kc=kc: e.scalar_tensor_tensor(
                    out=dst[:, kc, :], in0=src[:, kc, :], scalar=vec(gname, kc), in1=b['rstd'][:],
                    op0=ALU.mult, op1=ALU.mult),
                    reads=[rsrc, rrstd, R['const']], writes=[rdst] if kc == 0 else [], accs=[rdst] if kc else [])

        def rope_out(psx, rpx, psy, rpy, tok0, dst_ap, dst_reg):
            t1, rt1 = t1r.next()
            t2, rt2 = t2r.next()
            c.op('vector', lambda e: e.tensor_tensor(out=t1[:], in0=psx[0:64, :], in1=b['rope'][:, 0, tok0:tok0 + 512], op=ALU.mult),
                 reads=[rpx, rrope], writes=[rt1])
            c.op('vector', lambda e: e.tensor_tensor(out=t2[:], in0=psy[0:64, :], in1=b['rope'][:, 1, tok0:tok0 + 512], op=ALU.mult),
                 reads=[rpy, rrope], writes=[rt2])
            si, s, rs = s16.nexti()
            c.op('vector', lambda e: e.tensor_tensor(out=s[0:64, :], in0=t1[:], in1=t2[:], op=ALU.add),
                 reads=[rt1, rt2], writes=[rs])
            c.dma('sync', dst_ap, s[0:64, :], f'sd16_{si}', reads=[rs], accs=[dst_reg])

        pending_ag = []

        def issue_ag2(hf):
            c.dma('gpsimd', None, None, f'cc2{hf}', reads=[R[f'ag2in{hf}']], writes=[R[f'ag2out{hf}']], inc=1,
                  fn=lambda e: e.collective_compute("AllGather", ALU.bypass, replica_groups=[list(range(NCORES))],
                                                    ins=[G[f'ag2in{hf}'][:]], outs=[G[f'ag2out{hf}'][:]]))

        for half in range(2):
            t0 = half * 1024
            psA, rA = pb.next()
            psB, rB = pb.next()
            for kc in range(16):
                xi, xc, rxc = xcr.nexti()
                c.dma('sync', xc[:], G['x1T'][kc * 128:(kc + 1) * 128, t0:t0 + 1024], f'ldd0{xi}', reads=[R['x1T']], writes=[rxc])
                s, rs = sq.next()
                c.op('scalar', lambda e, xc=xc, s=s: e.activation(out=s[:], in_=xc[:], func=AF.Square), reads=[rxc], writes=[rs])
                c.op('tensor', lambda e, kc=kc, s=s: e.matmul(psA[:], G['ones_bf'][:], s[:, 0:512], start=(kc == 0), stop=(kc == 15)),
                     reads=[rs, R['const']], writes=[rA] if kc == 0 else [], accs=[rA] if kc else [])
                c.op('tensor', lambda e, kc=kc, s=s: e.matmul(psB[:], G['ones_bf'][:], s[:, 512:1024], start=(kc == 0), stop=(kc == 15)),
                     reads=[rs, R['const']], writes=[rB] if kc == 0 else [], accs=[rB] if kc else [])
            rstd_from(c, psA[:], rA, b['rstd'][:, 0:512], rrstd, D)
            rstd_from(c, psB[:], rB, b['rstd'][:, 512:1024], rrstd, D, acc=True)
            for kc in range(16):
                xi, xc, rxc = xcr.nexti()
                c.dma('sync', xc[:], G['x1T'][kc * 128:(kc + 1) * 128, t0:t0 + 1024], f'ldd0{xi}', reads=[R['x1T']], writes=[rxc])
                c.op('vector', lambda e, kc=kc, xc=xc: e.scalar_tensor_tensor(
                    out=act[:, kc, :], in0=xc[:], scalar=vec('mix_norm_o', kc), in1=b['rstd'][:], op0=ALU.mult, op1=ALU.mult),
                    reads=[rxc, rrstd, R['const']], writes=[ract] if kc == 0 else [], accs=[ract] if kc else [])

            groups = [(0, 'b', 0), (512, 'b', 1), (1024, 'c', 0), (2048, 'hc', 0), (1536, 'c', 1), (2560, 'hc', 1),
                      (3072, 'cq', 0), (3584, 'kv', 0)]

            def load_w(item, slot):
                col0, kind, gi = item
                if pending_ag and kind == 'hc' and gi == 0:
                    issue_ag2(pending_ag.pop())
                n = 384 if kind == 'kv' else 512
                c.dma('gpsimd', b['w'][slot][:, :, 0:n], G['w_in_o'][:, col0:col0 + n].rearrange("(kc p) n -> p kc n", p=128),
                      f'w{slot}', writes=[rw[slot]])

            def mm16(ps, rp, w, slot, c0, m, tl):
                for kc in range(16):
                    c.op('tensor', lambda e, kc=kc: e.matmul(
                        ps[0:m, :], w[:, kc, c0:c0 + m], act[:, kc, tl * 512:(tl + 1) * 512],
                        start=(kc == 0), stop=(kc == 15)),
                        reads=[rw[slot], ract], writes=[rp] if kc == 0 else [], accs=[rp] if kc == 15 else [],
                        sig=(kc in (0, 15)))

            def compute(item, slot):
                col0, kind, gi = item
                w = b['w'][slot]
                if kind == 'kv':
                    for mi in range(2):
                        for tl in range(2):
                            ps, rp = pb.next()
                            mm16(ps, rp, w, slot, mi * 128, 128, tl)
                            c.op('scalar', lambda e, ps=ps, mi=mi, tl=tl: e.activation(
                                out=b['ckv'][:, mi, tl * 512:(tl + 1) * 512], in_=ps[:], func=AF.Copy),
                                reads=[rp], writes=[rckv] if (mi == 0 and tl == 0) else [], accs=[] if (mi == 0 and tl == 0) else [rckv])
                    for tl in range(2):
                        psx, rpx = pb.next()
                        mm16(psx, rpx, w, slot, 256, 64, tl)
                        psy, rpy = pb.next()
                        mm16(psy, rpy, w, slot, 320, 64, tl)
                        tok0 = t0 + tl * 512
                        rope_out(psx, rpx, psy, rpy, tok0, G[f'ag2in{half}'][1024:1088, tl * 512:tl * 512 + 512], R[f'ag2in{half}'])
                    return
                for mi in range(4):
                    ch = gi * 4 + mi
                    for tl in range(2):
                        tok0 = t0 + tl * 512
                        ps, rp = pb.next()
                        mm16(ps, rp, w, slot, mi * 128, 128, tl)
                        if kind == 'b':
                            si, s, rs = s32.nexti()
                            c.op('scalar', lambda e, ps=ps, s=s: e.activation(out=s[:], in_=ps[:], func=AF.Copy),
                                 reads=[rp], writes=[rs])
                            c.dma('sync', G['bT'][ch * 128:(ch + 1) * 128, tok0:tok0 + 512], s[:], f'sd32_{si}', reads=[rs], accs=[R['bT']])
                        elif kind == 'c':
                            fw = (mi == 0 and tl == 0)
                            c.op('scalar', lambda e, ps=ps, mi=mi, tl=tl: e.activation(
                                out=b['csb'][:, mi, tl * 512:(tl + 1) * 512], in_=ps[:], func=AF.Copy),
                                reads=[rp], writes=[rcsb] if fw else [], accs=[] if fw else [rcsb])
                        elif kind == 'hc':
                            si, s, rs = s32.nexti()
                            c.op('vector', lambda e, ps=ps, s=s, mi=mi, tl=tl: e.tensor_tensor(
                                out=s[:], in0=b['csb'][:, mi, tl * 512:(tl + 1) * 512], in1=ps[:], op=ALU.mult),
                                reads=[rp, rcsb], writes=[rs])
                            c.dma('sync', G['vT'][ch * 128:(ch + 1) * 128, tok0:tok0 + 512], s[:], f'sd32_{si}', reads=[rs], accs=[R['vT']])
                            for (cond, col, off) in ((half == 0 and tl == 0, 0, 0), (half == 1 and tl == 1, 511, 1024)):
                                if cond:
                                    psb, rpb = pb.next()
                                    c.op('tensor', lambda e, psb=psb, s=s, col=col: e.matmul(
                                        psb[0:1, 0:128], s[:, col:col + 1], G['ident32'][:], start=True, stop=True),
                                        reads=[rs, R['const']], writes=[rpb])
                                    c.op('scalar', lambda e, psb=psb, off=off, ch=ch: e.activation(
                                        out=b['brow'][0:1, off + ch * 128:off + (ch + 1) * 128], in_=psb[0:1, 0:128], func=AF.Copy),
                                        reads=[rpb], accs=[rbrow])
                        elif kind == 'cq':
                            fw = (mi == 0 and tl == 0)
                            c.op('scalar', lambda e, ps=ps, mi=mi, tl=tl: e.activation(
                                out=b['cq'][:, mi, tl * 512:(tl + 1) * 512], in_=ps[:], func=AF.Copy),
                                reads=[rp], writes=[rcq] if fw else [], accs=[] if fw else [rcq])

            stream(groups, 3, load_w, compute)
            if half == 1:
                c.dma('sync', G['ag3in'][:], b['brow'][:], 'stbrow', reads=[rbrow], accs=[R['ag3in']])
                c.dma('gpsimd', None, None, 'cc3', reads=[R['ag3in']], writes=[R['ag3out']], inc=1,
                      fn=lambda e: e.collective_compute("AllGather", ALU.bypass, replica_groups=[list(range(NCORES))],
                                                        ins=[G['ag3in'][:]], outs=[G['ag3out'][:]]))

            small_norm(b['cq'], rcq, 4, 'q_norm_g', b['qn'], rqn)
            small_norm(b['ckv'], rckv, 2, 'kv_norm_g', b['kvn'], rkvn)
            wq = b['w'][0][:].rearrange("p a n -> p (a n)").rearrange("p (kc n) -> p kc n", kc=4)
            wkv = b['w'][1][:].rearrange("p a n -> p (a n)")[:, 0:4096].rearrange("p (kc n) -> p kc n", kc=2)
            c.dma('gpsimd', wq, G['w_uq3'].rearrange("(kc p) n -> p kc n", p=128), 'w0', writes=[rw[0]])
            c.dma('gpsimd', wkv, G['w_ukv2'].rearrange("(kc p) n -> p kc n", p=128), 'w1', writes=[rw[1]])

            def mmq(ps, rp, c0, m, tl):
                for kc in range(4):
                    c.op('tensor', lambda e, kc=kc: e.matmul(
                        ps[0:m, :], wq[:, kc, c0:c0 + m], b['qn'][:, kc, tl * 512:(tl + 1) * 512], start=(kc == 0), stop=(kc == 3)),
                        reads=[rw[0], rqn], writes=[rp] if kc == 0 else [], accs=[rp] if kc == 3 else [], sig=(kc in (0, 3)))

            for h in range(8):
                for tl in range(2):
                    tok0 = t0 + tl * 512
                    ps, rp = pb.next()
                    mmq(ps, rp, h * 128, 128, tl)
                    si, s, rs = s16.nexti()
                    c.op('scalar', lambda e, ps=ps, s=s: e.activation(out=s[:], in_=ps[:], func=AF.Copy), reads=[rp], writes=[rs])
                    c.dma('sync', G['qT'][h * 128:(h + 1) * 128, tok0:tok0 + 512], s[:], f'sd16_{si}', reads=[rs], accs=[R['qT']])
                    psx, rpx = pb.next()
                    mmq(psx, rpx, 1024 + h * 64, 64, tl)
                    psy, rpy = pb.next()
                    mmq(psy, rpy, 1536 + h * 64, 64, tl)
                    rope_out(psx, rpx, psy, rpy, tok0, G['qT'][1024 + h * 64:1024 + (h + 1) * 64, tok0:tok0 + 512], R['qT'])
            for h in range(8):
                for tl in range(2):
                    tok0 = t0 + tl * 512
                    ps, rp = pb.next()
                    for kc in range(2):
                        c.op('tensor', lambda e, kc=kc, ps=ps, h=h, tl=tl: e.matmul(
                            ps[:], wkv[:, kc, h * 128:(h + 1) * 128], b['kvn'][:, kc, tl * 512:(tl + 1) * 512], start=(kc == 0), stop=(kc == 1)),
                            reads=[rw[1], rkvn], writes=[rp] if kc == 0 else [], accs=[rp] if kc else [])
                    si, s, rs = s16.nexti()
                    c.op('scalar', lambda e, ps=ps, s=s: e.activation(out=s[:], in_=ps[:], func=AF.Copy), reads=[rp], writes=[rs])
                    c.dma('sync', G[f'ag2in{half}'][h * 128:(h + 1) * 128, tl * 512:tl * 512 + 512], s[:], f'sd16_{si}', reads=[rs], accs=[R[f'ag2in{half}']])
            for tb in range(8):
                blk = half * 8 + tb
                for nh in range(2):
                    ps, rp = pb.next()
                    for kc in range(2):
                        c.op('tensor', lambda e, kc=kc, ps=ps, tb=tb, nh=nh: e.matmul(
                            ps[:], b['kvn'][:, kc, tb * 128:(tb + 1) * 128], wkv[:, kc, 1024 + nh * 512:1024 + (nh + 1) * 512],
                            start=(kc == 0), stop=(kc == 1)),
                            reads=[rw[1], rkvn], writes=[rp] if kc == 0 else [], accs=[rp] if kc else [])
                    si, s, rs = s16.nexti()
                    c.op('vector', lambda e, ps=ps, s=s: e.tensor_copy(out=s[:], in_=ps[:]), reads=[rp], writes=[rs])
                    r0 = 1088 + nh * 512
                    dst = G[f'ag2in{half}'][r0:r0 + 512, tb * 128:(tb + 1) * 128].rearrange("(hh p) d -> p hh d", p=128)
                    c.dma('sync', dst, s[:].rearrange("p (hh d) -> p hh d", hh=4), f'sd16_{si}', reads=[rs], accs=[R[f'ag2in{half}']])
            issue_ag2(half)

    def pd_prog_all(c, b):
        pd_prog(c, b)

    if LAST_PHASE >= 4:
        c.run_phase(pd_alloc, pd_prog_all)
    if LAST_PHASE >= 5:
        pass

    def pe_alloc(nc, st):
        sb = lambda name, shape, dt: st.enter_context(nc.sbuf_tensor(name, shape, dt))
        b = {}
        b['KT'] = [sb(f'e_KT{i}', [128, 16384], BF16) for i in range(2)]
        b['V'] = [sb(f'e_V{i}', [128, 8, 2048], BF16) for i in range(2)]
        b['kr'] = sb('e_kr', [128, 16384], BF16)
        b['qn'] = [sb(f'e_qn{i}', [128, 2048], BF16) for i in range(2)]
        b['qr'] = [sb(f'e_qr{i}', [128, 2048], BF16) for i in range(2)]
        b['P'] = [sb(f'e_P{i}', [128, 512], BF16) for i in range(6)]
        b['dacc'] = [sb(f'e_dacc{i}', [128, 512], F32) for i in range(2)]
        b['rd'] = sb('e_rd', [128, 512], F32)
        b['o'] = [sb(f'e_o{i}', [128, 512], BF16) for i in range(2)]
        return b

    def pe0_alloc(nc, st):
        sb = lambda name, shape, dt: st.enter_context(nc.sbuf_tensor(name, shape, dt))
        b = {}
        b['bt'] = sb('e_bt', [8, 2048], F32)
        b['sel'] = sb('e_sel', [8, 2], F32)
        b['vt'] = sb('e_vt', [128, 2050], F32)
        b['bb'] = sb('e_bb', [128, 2048], F32)
        b['ca'] = sb('e_ca', [128, 2048], F32)
        b['yc'] = sb('e_yc', [128, 2048], BF16)
        return b

    def pe0_prog(c, b):
        rbt, rsel, rvt, rbb, rca, ryc = [c.reg() for _ in range(6)]
        sb_rot = Rot(c, G['ps'][0:4])
        pbs = sb_rot
        c.dma('sync', b['bt'][:], G['ag3out'][:], 'lde0', reads=[R['ag3out']], writes=[rbt])
        c.dma('sync', b['sel'][:], G['sel'][:], 'lde8', writes=[rsel])
        for j in range(8):
            c.dma('sync', b['vt'][:, 1:2049], G['vT'][j * 128:(j + 1) * 128, :], 'lde1', reads=[R['vT']], writes=[rvt])
            c.dma('sync', b['bb'][:], G['bT'][j * 128:(j + 1) * 128, :], 'lde2', reads=[R['bT']], writes=[rbb])
            ps, rp = pbs.next()
            c.op('tensor', lambda e, ps=ps, j=j: e.matmul(ps[:, 0:1], b['bt'][0:8, 1024 + j * 128:1024 + (j + 1) * 128], b['sel'][0:8, 0:1],
                                                          start=True, stop=True), reads=[rbt, rsel], writes=[rp])
            c.op('tensor', lambda e, ps=ps, j=j: e.matmul(ps[:, 1:2], b['bt'][0:8, j * 128:(j + 1) * 128], b['sel'][0:8, 1:2],
                                                          start=True, stop=True), reads=[rbt, rsel], accs=[rp])
            c.op('scalar', lambda e, ps=ps: e.activation(out=b['vt'][:, 0:1], in_=ps[:, 0:1], func=AF.Copy), reads=[rp], accs=[rvt])
            c.op('scalar', lambda e, ps=ps: e.activation(out=b['vt'][:, 2049:2050], in_=ps[:, 1:2], func=AF.Copy), reads=[rp], accs=[rvt])
            c.op('vector', lambda e, j=j: e.tensor_scalar(out=b['ca'][:], in0=b['vt'][:, 0:2048], scalar1=vec('conv_c_w', 0 * 8 + j),
                                                          scalar2=None, op0=ALU.mult), reads=[rvt, R['const']], writes=[rca])
            for k in (1, 2):
                c.op('vector', lambda e, j=j, k=k: e.scalar_tensor_tensor(
                    out=b['ca'][:], in0=b['vt'][:, k:k + 2048], scalar=vec('conv_c_w', k * 8 + j), in1=b['ca'][:],
                    op0=ALU.mult, op1=ALU.add), reads=[rvt, rca], accs=[rca])
            c.op('vector', lambda e: e.tensor_tensor(out=b['yc'][:], in0=b['ca'][:], in1=b['bb'][:], op=ALU.mult),
                 reads=[rca, rbb], writes=[ryc])
            c.dma('sync', G['mixT'][j * 128:(j + 1) * 128, :], b['yc'][:], 'ste0', reads=[ryc], accs=[R['mixT']])

    def pe_prog(c, b):
        sb_rot = Rot(c, G['ps'][0:4])
        SCALE = 192.0 ** -0.5
        rKT = [c.reg() for _ in range(2)]
        rV = [c.reg() for _ in range(2)]
        rqn = [c.reg() for _ in range(2)]
        rqr = [c.reg() for _ in range(2)]
        rkr, rrd = c.reg(), c.reg()
        Pr = Rot(c, b['P'])
        orot = Rot(c, b['o'])
        obanks = Rot(c, G['ps'][4:6])
        dbanks = Rot(c, G['ps'][6:8])
        daccr = Rot(c, b['dacc'])
        c.op('vector', lambda e: e.memset(b['kr'][64:128, :], 0.0), accs=[rkr])
        for i in range(2):
            c.op('vector', lambda e, i=i: e.memset(b['qr'][i][64:128, :], 0.0), accs=[rqr[i]])
        for r in range(8):
            for hf in range(2):
                c.dma('sync', b['kr'][0:64, r * 2048 + hf * 1024:r * 2048 + (hf + 1) * 1024],
                      G[f'ag2out{hf}'][r * 2112 + 1024:r * 2112 + 1088, :], 'lde3', reads=[R[f'ag2out{hf}']], accs=[rkr])

        def load(h, slot):
            for r in range(8):
                for hf in range(2):
                    fst = (r == 0 and hf == 0)
                    c.dma('sync', b['KT'][slot][:, r * 2048 + hf * 1024:r * 2048 + (hf + 1) * 1024],
                          G[f'ag2out{hf}'][r * 2112 + h * 128:r * 2112 + (h + 1) * 128, :],
                          f'lde4{slot}', reads=[R[f'ag2out{hf}']], writes=[rKT[slot]] if fst else [], accs=[] if fst else [rKT[slot]])
                    c.dma('sync', b['V'][slot][:, r, hf * 1024:(hf + 1) * 1024],
                          G[f'ag2out{hf}'][r * 2112 + 1088 + h * 128:r * 2112 + 1088 + (h + 1) * 128, :],
                          f'lde5{slot}', reads=[R[f'ag2out{hf}']], writes=[rV[slot]] if fst else [], accs=[] if fst else [rV[slot]])
            c.dma('sync', b['qn'][slot][:], G['qT'][h * 128:(h + 1) * 128, :], f'lde6{slot}', reads=[R['qT']], writes=[rqn[slot]])
            c.dma('sync', b['qr'][slot][0:64, :], G['qT'][1024 + h * 64:1024 + (h + 1) * 64, :], f'lde7{slot}', reads=[R['qT']],
                  accs=[rqr[slot]], extra=list(rqr[slot].w.items()))

        def compute(h, slot):
            KT, V, qn, qr = b['KT'][slot], b['V'][slot], b['qn'][slot], b['qr'][slot]
            for qt in range(4):
                q0 = qt * 512
                ops_, rO = obanks.next()
                dps_, rD = dbanks.next()
                dacc, rda = daccr.next()

                def s_mm(kb):
                    ps, rp = sb_rot.next()
                    c.op('tensor', lambda e: e.matmul(ps[:], KT[:, kb * 128:(kb + 1) * 128], qn[:, q0:q0 + 512], start=True, stop=False),
                         reads=[rKT[slot], rqn[slot]], writes=[rp])
                    c.op('tensor', lambda e: e.matmul(ps[:], b['kr'][:, kb * 128:(kb + 1) * 128], qr[:, q0:q0 + 512], start=False, stop=True),
                         reads=[rkr, rqr[slot]], accs=[rp])
                    return ps, rp

                pend = [s_mm(kb) for kb in range(3)]
                for kb in range(128):
                    if kb + 3 < 128:
                        pend.append(s_mm(kb + 3))
                    ps, rp = pend.pop(0)
                    P, rP = Pr.next()
                    c.op('scalar', lambda e, ps=ps, P=P: e.activation(out=P[:], in_=ps[:], func=AF.Exp, scale=SCALE),
                         reads=[rp], writes=[rP])
                    r_, bb = kb // 16, kb % 16
                    c.op('tensor', lambda e, P=P, r_=r_, bb=bb, kb=kb: e.matmul(
                        ops_[:], V[:, r_, bb * 128:(bb + 1) * 128], P[:], start=(kb == 0), stop=(kb == 127)),
                        reads=[rV[slot], rP], writes=[rO] if kb == 0 else [], accs=[rO] if kb == 127 else [],
                        sig=(kb in (0, 127)))
                    if kb == 0:
                        c.op('vector', lambda e, P=P: e.tensor_copy(out=dacc[:], in_=P[:]), reads=[rP], writes=[rda])
                    else:
                        c.op('vector', lambda e, P=P: e.tensor_tensor(out=dacc[:], in0=dacc[:], in1=P[:], op=ALU.add),
                             reads=[rP, rda], accs=[rda])
                c.op('tensor', lambda e: e.matmul(dps_[:], G['ones32'][:], dacc[:], start=True, stop=True),
                     reads=[rda, R['const']], writes=[rD])
                c.op('vector', lambda e: e.reciprocal(out=b['rd'][:], in_=dps_[:]), reads=[rD], writes=[rrd])
                oi, o, ro = orot.nexti()
                c.op('vector', lambda e, o=o: e.tensor_tensor(out=o[:], in0=ops_[:], in1=b['rd'][:], op=ALU.mult),
                     reads=[rO, rrd], writes=[ro])
                c.dma('sync', G['mixT'][1024 + h * 128:1024 + (h + 1) * 128, q0:q0 + 512], o[:], f'ste1{oi}', reads=[ro], accs=[R['mixT']])

        stream(list(range(8)), 2, load, compute)

    if LAST_PHASE >= 6:
        c.run_phase(pe0_alloc, pe0_prog)
        c.run_phase(pe_alloc, pe_prog)

    if LAST_PHASE >= 7:
        a, p = make_mixmlp(1, G['w_out_o'], G['w_up1'], G['w_down1'], G['x1T'], 0, R['x1T'], 'mlp_norm1', True)
        c.run_phase(a, p)

    if DEBUG:
        def dbg_alloc(nc, st):
            return {}

        def dbg_prog(c, b):
            for kc in range(16):
                c.dma('sync', G['dbg'][kc * 128:(kc + 1) * 128, :], G['mixT'][kc * 128:(kc + 1) * 128, :], 'dbg1', reads=[R['mixT']], accs=[R['dbg']])
                c.dma('sync', G['dbg2'][kc * 128:(kc + 1) * 128, :], G['x1T'][kc * 128:(kc + 1) * 128, :], 'dbg1', reads=[R['x1T']], accs=[R['dbg']])
        c.run_phase(dbg_alloc, dbg_prog)

    stack.close()
    return nc


_NC_CACHE = {}


def _chunked(v, nch):
    return np.ascontiguousarray(np.asarray(v, np.float32).reshape(nch, 128).T)


def _host_consts():
    n = np.arange(128, dtype=np.float64)
    ang = 2.0 * np.pi * np.outer(n, n) / 128.0
    C1, S1 = np.cos(ang), np.sin(ang)
    dft = np.concatenate([C1, -S1, S1, C1], axis=1).astype(np.float32)
    return dft


def _prep_inputs(inp):
    f32 = np.float32
    x = np.asarray(inp['x'], f32)[0]
    xT = np.zeros((D, S + 32), f32)
    xT[:, 16:16 + S] = x.T
    w_in_o = np.asarray(inp['w_in_o'], f32)[0]
    kr = w_in_o[:, 3840:3904]
    w_in_o2 = np.concatenate([w_in_o, kr[:, 32:64], kr[:, 0:32]], axis=1)
    w_uq = np.asarray(inp['w_uq'], f32)[0].reshape(512, 8, 192)
    qn = w_uq[:, :, 0:128].reshape(512, 1024)
    qr = w_uq[:, :, 128:192]
    qrs = np.concatenate([qr[:, :, 32:64], qr[:, :, 0:32]], axis=2)
    w_uq3 = np.ascontiguousarray(np.concatenate([qn, qr.reshape(512, 512), qrs.reshape(512, 512)], axis=1))
    w_ukv = np.asarray(inp['w_ukv'], f32)[0].reshape(256, 8, 256)
    w_ukv2 = np.ascontiguousarray(np.concatenate([w_ukv[:, :, 0:128].reshape(256, 1024),
                                                  w_ukv[:, :, 128:256].reshape(256, 1024)], axis=1))
    vecs = np.zeros((128, NV), f32)

    def put(name, arr, nch):
        vecs[:, VOFF[name]:VOFF[name] + nch] = _chunked(arr, nch)
    put('mix_norm_e', inp['mix_norm_e'][0], 16)
    put('mlp_norm0', inp['mlp_norm'][0], 16)
    put('mix_norm_o', inp['mix_norm_o'][0], 16)
    put('mlp_norm1', inp['mlp_norm'][1], 16)
    put('final_norm', inp['final_norm'], 16)
    put('conv_a_b', inp['conv_a_b'][0], 8)
    put('ln_a_g', inp['ln_a_g'][0], 8)
    put('ln_a_b', inp['ln_a_b'][0], 8)
    caw = np.asarray(inp['conv_a_w'], f32)[0]
    for k in range(31):
        vecs[:, VOFF['conv_a_w'] + k * 8:VOFF['conv_a_w'] + (k + 1) * 8] = _chunked(caw[k], 8)
    ccw = np.asarray(inp['conv_c_w'], f32)[0]
    for k in range(3):
        vecs[:, VOFF['conv_c_w'] + k * 8:VOFF['conv_c_w'] + (k + 1) * 8] = _chunked(ccw[k], 8)
    put('q_norm_g', inp['q_norm_g'][0], 4)
    put('kv_norm_g', inp['kv_norm_g'][0], 2)
    dft = _host_consts()
    common = {
        'w_in_e': np.ascontiguousarray(np.asarray(inp['w_in_e'], f32)[0]),
        'w_out_e': np.ascontiguousarray(np.asarray(inp['w_out_e'], f32)[0]),
        'w_in_o': np.ascontiguousarray(w_in_o2),
        'w_uq3': w_uq3, 'w_ukv2': w_ukv2,
        'w_out_o': np.ascontiguousarray(np.asarray(inp['w_out_o'], f32)[0]),
        'w_up0': np.ascontiguousarray(np.asarray(inp['w_up'], f32)[0]),
        'w_up1': np.ascontiguousarray(np.asarray(inp['w_up'], f32)[1]),
        'w_down0': np.ascontiguousarray(np.asarray(inp['w_down'], f32)[0]),
        'w_down1': np.ascontiguousarray(np.asarray(inp['w_down'], f32)[1]),
        'vecs': vecs, 'dft': dft,
    }
    inv = 1.0 / (10000.0 ** (np.arange(0, 64, 2, dtype=np.float32) / 64.0))
    in_maps = []
    n2 = np.arange(128, dtype=np.float64)[:, None]
    nrm = 2.0 ** -10.5
    for cidx in range(NCORES):
        m = dict(common)
        m['xT'] = np.ascontiguousarray(xT[:, cidx * T:cidx * T + T + 32])
        k1 = np.arange(128)[:, None]
        k2 = np.arange(16)[None, :]
        kk = (cidx * T + 128 * k2 + k1).reshape(1, 2048).astype(np.float64)
        ang = 2.0 * np.pi * ((n2 * kk) % S) / S
        m['tcs'] = np.concatenate([np.cos(ang) * nrm, np.sin(ang) * nrm], axis=1).astype(np.float32)
        pos = np.arange(cidx * T, (cidx + 1) * T, dtype=np.float32)
        a = (pos[:, None] * inv[None, :]).T
        cs, sn = np.cos(a), np.sin(a)
        CC = np.concatenate([cs, cs], axis=0)
        SS = np.concatenate([-sn, sn], axis=0)
        m['rope'] = np.ascontiguousarray(np.concatenate([CC, SS], axis=1).astype(np.float32))
        sel = np.zeros((8, 2), np.float32)
        if cidx > 0:
            sel[cidx - 1, 0] = 1.0
        if cidx < NCORES - 1:
            sel[cidx + 1, 1] = 1.0
        m['sel'] = sel
        m['ident'] = np.eye(128, dtype=np.float32)
        in_maps.append(m)
    return in_maps


def kernel(**inputs):
    if 'nc' not in _NC_CACHE:
        _NC_CACHE['nc'] = build_program()
    nc = _NC_CACHE['nc']
    in_maps = _prep_inputs(inputs)
    res = run_bass_kernel_spmd(nc, in_maps, core_ids=list(range(NCORES)))
    out = np.concatenate([np.asarray(res.results[i]['outT']).T for i in range(NCORES)], axis=0)
    if DEBUG:
        _NC_CACHE['dbg'] = [np.asarray(res.results[i]['dbg']) for i in range(NCORES)]
        _NC_CACHE['dbg2'] = [np.asarray(res.results[i]['dbg2']) for i in range(NCORES)]
    return np.ascontiguousarray(out[None].astype(np.float32))
```

```python
import numpy as np
from contextlib import ExitStack
import concourse.bass as bass
import concourse.mybir as mybir
from concourse.bass_utils import run_bass_kernel_spmd

F32 = mybir.dt.float32
BF16 = mybir.dt.bfloat16
AF = mybir.ActivationFunctionType
ALU = mybir.AluOpType

NCORES = 8
S = 16384
T = 2048
D = 2048
EPS = 1e-6
ENG_NAMES = ['sync', 'scalar', 'vector', 'gpsimd', 'tensor']
LAST_PHASE = 99
DEBUG = False
DEBUG_SRC = ('mixT', 'mixT')


class Reg:
    __slots__ = ('w', 'r')

    def __init__(self):
        self.w = {}
        self.r = {}


class Rot:
    def __init__(self, c, bufs):
        self.bufs = bufs
        self.regs = [c.reg() for _ in bufs]
        self.i = 0

    def next(self):
        i = self.i % len(self.bufs)
        self.i += 1
        return self.bufs[i], self.regs[i]

    def nexti(self):
        i = self.i % len(self.bufs)
        self.i += 1
        return i, self.bufs[i], self.regs[i]


class Ctx:
    def __init__(self, nc, stack):
        self.nc = nc
        self.stack = stack
        self.sems = {}
        self.cur = None
        self.eng = None
        self.count = {}
        self.waited = {}
        self.waited_by = {n: {} for n in ENG_NAMES}
        self.pregs = []

    def reg(self):
        return Reg()

    def preg(self):
        r = Reg()
        self.pregs.append(r)
        return r

    def snapshot(self):
        return (dict(self.count), [(dict(r.w), dict(r.r)) for r in self.pregs])

    def restore(self, snap):
        self.count = dict(snap[0])
        for r, (w, rr) in zip(self.pregs, snap[1]):
            r.w = dict(w)
            r.r = dict(rr)

    def _deps(self, reads, writes, accs, extra):
        need = {}
        for r in reads:
            for s, v in r.w.items():
                if need.get(s, 0) < v:
                    need[s] = v
        for r in writes:
            for d in (r.w, r.r):
                for s, v in d.items():
                    if need.get(s, 0) < v:
                        need[s] = v
        for r in accs:
            for s, v in r.r.items():
                if need.get(s, 0) < v:
                    need[s] = v
        for t in extra:
            if t is not None and need.get(t[0], 0) < t[1]:
                need[t[0]] = t[1]
        return need

    def _emit_waits(self, need, skip_sem=None):
        for s, v in need.items():
            if s == skip_sem:
                continue
            if self.waited.get(s, 0) >= v:
                continue
            self.eng.wait_ge(self.sems[s], v)
            self.waited[s] = v

    def _update(self, tok, reads, writes, accs):
        s, v = tok
        for r in reads:
            if r.r.get(s, 0) < v:
                r.r[s] = v
        for r in writes:
            r.w = {s: v}
            r.r = {}
        for r in accs:
            if r.w.get(s, 0) < v:
                r.w[s] = v

    def op(self, engname, fn, reads=(), writes=(), accs=(), extra=(), sig=True):
        sname = 'm_' + engname
        if not sig:
            if engname == self.cur:
                need = self._deps(reads, writes, accs, extra)
                self._emit_waits(need, skip_sem=sname if engname == 'tensor' else None)
                fn(self.eng)
            return None
        need = self._deps(reads, writes, accs, extra)
        for r in accs:
            for s_, v_ in r.w.items():
                if s_ != sname and need.get(s_, 0) < v_:
                    need[s_] = v_
        self.count[sname] = self.count.get(sname, 0) + 1
        tok = (sname, self.count[sname])
        if engname == self.cur:
            self._emit_waits(need, skip_sem=sname if engname == 'tensor' else None)
            ins = fn(self.eng)
            ins.then_inc(self.sems[sname], 1)
        self._update(tok, reads, writes, accs)
        return tok

    def dma(self, qname, out, in_, sem, reads=(), writes=(), accs=(), extra=(), fn=None, inc=16):
        need = self._deps(reads, writes, accs, extra)
        self.count[sem] = self.count.get(sem, 0) + inc
        tok = (sem, self.count[sem])
        if qname == self.cur:
            self._emit_waits(need)
            if fn is not None:
                ins = fn(self.eng)
            else:
                ins = self.eng.dma_start(out=out, in_=in_)
            ins.then_inc(self.sems[sem], inc)
        self._update(tok, reads, writes, accs)
        return tok

    def barrier(self, include_cc=False):
        if self.cur is not None:
            self._emit_waits({k: v for k, v in self.count.items() if include_cc or not k.startswith('cc')})

    def run_phase(self, alloc_fn, prog_fn):
        nc = self.nc
        with ExitStack() as st:
            bufs = alloc_fn(nc, st)
            snap = self.snapshot()
            self.cur = None
            self.eng = None
            prog_fn(self, bufs)
            end = self.snapshot()
            for name in sorted(self.count.keys()):
                if name not in self.sems:
                    self.sems[name] = self.stack.enter_context(nc.semaphore(name))
            with nc.Block() as block:
                for name in ENG_NAMES:
                    def body(eng, name=name):
                        self.restore(snap)
                        self.cur = name
                        self.eng = eng
                        self.waited = self.waited_by[name]
                        prog_fn(self, bufs)
                        self.barrier()
                    getattr(block, name)(body)
            self.cur = None
            self.eng = None
            self.restore(end)


def stream(items, nslots, load_fn, compute_fn):
    n = len(items)
    for i in range(min(nslots - 1, n)):
        load_fn(items[i], i % nslots)
    for i in range(n):
        j = i + nslots - 1
        if j < n:
            load_fn(items[j], j % nslots)
        compute_fn(items[i], i % nslots)


VOFF = {}
_o = 0
for _name, _n in [('mix_norm_e', 16), ('mlp_norm0', 16), ('mix_norm_o', 16), ('mlp_norm1', 16),
                  ('final_norm', 16), ('conv_a_b', 8), ('ln_a_g', 8), ('ln_a_b', 8),
                  ('conv_a_w', 31 * 8), ('conv_c_w', 3 * 8), ('q_norm_g', 4), ('kv_norm_g', 2)]:
    VOFF[_name] = _o
    _o += _n
NV = _o


def build_program():
    nc = bass.Bass("TRN2", target_bir_lowering=False)
    stack = ExitStack()
    G = {}

    def din(name, shape, dt=F32):
        G[name] = nc.dram_tensor(name, list(shape), dt, kind="ExternalInput").ap()

    din('xT', [D, T + 32])
    din('w_in_e', [D, 3072])
    din('w_out_e', [D, D])
    din('w_in_o', [D, 3968])
    din('w_uq3', [512, 2048])
    din('w_ukv2', [256, 2048])
    din('w_out_o', [D, D])
    din('w_up0', [D, 8192])
    din('w_up1', [D, 8192])
    din('w_down0', [8192, D])
    din('w_down1', [8192, D])
    din('vecs', [128, NV])
    din('dft', [128, 512])
    din('tcs', [128, 2 * 2048])
    din('rope', [64, 2 * 2048])
    din('sel', [8, 2])
    din('ident', [128, 128])
    G['outT'] = nc.dram_tensor('outT', [D, T], F32, kind="ExternalOutput").ap()
    if DEBUG:
        G['dbg'] = nc.dram_tensor('dbg', [D, T], BF16, kind="ExternalOutput").ap()
        G['dbg2'] = nc.dram_tensor('dbg2', [D, T], F32, kind="ExternalOutput").ap()
    G['mixT'] = nc.dram_tensor('mixT', [D, T], BF16).ap()
    G['agin1'] = nc.dram_tensor('agin1', [16, 262144], BF16).ap()
    G['agout1'] = nc.dram_tensor('agout1', [128, 262144], BF16).ap()
    G['x1T'] = nc.dram_tensor('x1T', [D, T], F32).ap()
    for hf in range(2):
        G[f'ag2in{hf}'] = nc.dram_tensor(f'ag2in{hf}', [2112, 1024], BF16).ap()
        G[f'ag2out{hf}'] = nc.dram_tensor(f'ag2out{hf}', [8 * 2112, 1024], BF16).ap()
    G['ag3in'] = nc.dram_tensor('ag3in', [1, 2048], F32).ap()
    G['ag3out'] = nc.dram_tensor('ag3out', [8, 2048], F32).ap()
    G['qT'] = nc.dram_tensor('qT', [1536, T], BF16).ap()
    G['vT'] = nc.dram_tensor('vT', [1024, T], F32).ap()
    G['bT'] = nc.dram_tensor('bT', [1024, T], F32).ap()

    def gsb(name, shape, dt):
        return stack.enter_context(nc.sbuf_tensor(name, shape, dt))

    G['vecs_sb'] = gsb('vecs_sb', [128, NV], F32)
    G['ones_bf'] = gsb('ones_bf', [128, 128], BF16)
    G['ones32'] = gsb('ones32', [128, 128], F32)
    G['dft_sb'] = gsb('dft_sb', [128, 512], BF16)
    G['ident32'] = gsb('ident32', [128, 128], F32)
    G['ps'] = [stack.enter_context(nc.psum_tensor(f'ps{i}', [128, 512], F32)) for i in range(8)]

    c = Ctx(nc, stack)
    R = {k: c.preg() for k in ['mixT', 'agin1', 'agout1', 'x1T', 'ag2in0', 'ag2out0', 'ag2in1', 'ag2out1', 'ag3in', 'ag3out',
                               'qT', 'vT', 'bT', 'outT', 'const', 'dbg']}

    def vec(name, i):
        o = VOFF[name] + i
        return G['vecs_sb'][:, o:o + 1]

    def p0_alloc(nc, st):
        return {}

    def p0_prog(c, b):
        c.dma('sync', G['vecs_sb'][:], G['vecs'][:], 'ld0', accs=[R['const']])
        c.dma('gpsimd', G['dft_sb'][:], G['dft'][:], 'ld1', accs=[R['const']])
        c.dma('sync', G['ident32'][:], G['ident'][:], 'ld2', accs=[R['const']])
        c.op('vector', lambda e: e.memset(G['ones_bf'][:], 1.0), accs=[R['const']])
        c.op('vector', lambda e: e.memset(G['ones32'][:], 1.0), accs=[R['const']])

    c.run_phase(p0_alloc, p0_prog)

    def rstd_from(c, ps_ap, rps, out_ap, rout, n, acc=False):
        c.op('scalar', lambda e: e.activation(out=out_ap, in_=ps_ap, func=AF.Sqrt, bias=EPS, scale=1.0 / n),
             reads=[rps], accs=[rout] if acc else [], writes=[] if acc else [rout])
        c.op('vector', lambda e: e.reciprocal(out=out_ap, in_=out_ap), reads=[rout], accs=[rout])

    def pa_alloc(nc, st):
        sb = lambda name, shape, dt: st.enter_context(nc.sbuf_tensor(name, shape, dt))
        b = {}
        b['xs'] = sb('a_xs', [128, 16, 544], F32)
        b['hT'] = sb('a_hT', [128, 16, 544], BF16)
        b['w'] = [sb(f'a_w{i}', [128, 16, 512], BF16) for i in range(2)]
        b['val'] = sb('a_val', [128, 8, 544], F32)
        b['sigT'] = sb('a_sigT', [128, 8, 544], F32)
        b['gT'] = sb('a_gT', [128, 8, 544], F32)
        b['acc'] = sb('a_acc', [128, 8, 512], F32)
        b['sq'] = [sb(f'a_sq{i}', [128, 544], BF16) for i in range(2)]
        b['ub'] = [sb(f'a_ub{i}', [128, 512], BF16) for i in range(2)]
        b['stg'] = [sb(f'a_stg{i}', [128, 2, 512], BF16) for i in range(2)]
        b['rstd'] = sb('a_rstd', [128, 544], F32)
        b['ln'] = sb('a_ln', [128, 5, 512], F32)
        b['t'] = [sb(f'a_t{i}', [128, 512], F32) for i in range(2)]
        b['ya'] = [sb(f'a_ya{i}', [128, 512], BF16) for i in range(2)]
        return b

    def pa_prog(c, b):
        pb = Rot(c, G['ps'])
        wrot_regs = [c.reg() for _ in range(2)]
        sq = Rot(c, b['sq'])
        ubr = Rot(c, b['ub'])
        stg = Rot(c, b['stg'])
        tt = Rot(c, b['t'])
        yar = Rot(c, b['ya'])
        rxs, rhT, rval, rsig, rgT, racc, rrstd, rln = [c.reg() for _ in range(8)]
        agin1 = G['agin1'].rearrange("a (ri g kc n) -> a ri g kc n", ri=2, g=8, kc=128)
        xs, hT = b['xs'], b['hT']
        wstate = {'i': 0}

        def S1(q):
            c.dma('sync', xs[:], G['xT'][:, 512 * q:512 * q + 544].rearrange("(kc p) t -> p kc t", p=128),
                  'lda', writes=[rxs])
            psA, rA = pb.next()
            psB, rB = pb.next()
            for kc in range(16):
                s, rs = sq.next()
                c.op('scalar', lambda e: e.activation(out=s[:], in_=xs[:, kc, :], func=AF.Square),
                     reads=[rxs], writes=[rs])
                c.op('tensor', lambda e: e.matmul(psA[:], G['ones_bf'][:], s[:, 0:512], start=(kc == 0), stop=(kc == 15)),
                     reads=[rs, R['const']], writes=[rA] if kc == 0 else [], accs=[rA] if kc else [])
                c.op('tensor', lambda e: e.matmul(psB[:, 0:32], G['ones_bf'][:], s[:, 512:544], start=(kc == 0), stop=(kc == 15)),
                     reads=[rs, R['const']], writes=[rB] if kc == 0 else [], accs=[rB] if kc else [])
            rstd_from(c, psA[:], rA, b['rstd'][:, 0:512], rrstd, D)
            rstd_from(c, psB[:, 0:32], rB, b['rstd'][:, 512:544], rrstd, D, acc=True)
            for kc in range(16):
                c.op('vector', lambda e: e.scalar_tensor_tensor(
                    out=hT[:, kc, :], in0=xs[:, kc, :], scalar=vec('mix_norm_e', kc), in1=b['rstd'][:],
                    op0=ALU.mult, op1=ALU.mult),
                    reads=[rxs, rrstd, R['const']], writes=[rhT] if kc == 0 else [], accs=[rhT] if kc else [])

        groups = [(2048, 'ub', 0), (2560, 'ub', 1), (0, 'val', 0), (512, 'val', 1), (1024, 'gate', 0), (1536, 'gate', 1)]

        def INPROJ(q):
            def load_w(item, slot_unused):
                slot = wstate['i'] % 2
                wstate['i'] += 1
                wstate[(q, item[0])] = slot
                col0 = item[0]
                c.dma('gpsimd', b['w'][slot][:], G['w_in_e'][:, col0:col0 + 512].rearrange("(kc p) n -> p kc n", p=128),
                      f'w{slot}', writes=[wrot_regs[slot]])

            def compute(item, slot_unused):
                col0, kind, gi = item
                slot = wstate[(q, col0)]
                w = b['w'][slot]
                rw = wrot_regs[slot]
                for mi in range(4):
                    ch = gi * 4 + mi
                    parts = [(16, 512)] if kind == 'ub' else [(0, 512), (512, 32)]
                    for (c0, n) in parts:
                        ps, rp = pb.next()
                        for kc in range(16):
                            c.op('tensor', lambda e: e.matmul(
                                ps[:, 0:n], w[:, kc, mi * 128:(mi + 1) * 128], hT[:, kc, c0:c0 + n],
                                start=(kc == 0), stop=(kc == 15)),
                                reads=[rw, rhT], writes=[rp] if kc == 0 else [], accs=[rp] if kc == 15 else [],
                                sig=(kc in (0, 15)))
                        if kind == 'val':
                            c.op('scalar', lambda e: e.activation(out=b['val'][:, ch, c0:c0 + n], in_=ps[:, 0:n], func=AF.Copy),
                                 reads=[rp], accs=[rval])
                        elif kind == 'gate':
                            c.op('scalar', lambda e: e.activation(out=b['sigT'][:, ch, c0:c0 + n], in_=ps[:, 0:n], func=AF.Sigmoid),
                                 reads=[rp], accs=[rsig])
                        else:
                            u, ru = ubr.next()
                            c.op('scalar', lambda e: e.activation(out=u[:], in_=ps[:], func=AF.Copy), reads=[rp], writes=[ru])
                            g = ch
                            sgi, sgb, rsgb = stg.nexti()
                            for ri in range(2):
                                ps2, rp2 = pb.next()
                                c.op('tensor', lambda e: e.matmul(ps2[:], G['dft_sb'][:, ri * 128:(ri + 1) * 128], u[:], start=True, stop=True),
                                     reads=[ru, R['const']], writes=[rp2])
                                c.op('scalar', lambda e: e.activation(out=sgb[:, ri, :], in_=ps2[:], func=AF.Copy),
                                     reads=[rp2], writes=[rsgb] if ri == 0 else [], accs=[rsgb] if ri else [])
                            for ri in range(2):
                                c.dma('sync', agin1[4 * q:4 * q + 4, ri, g, :, :].rearrange("a k n -> k a n"),
                                      sgb[:, ri, :].rearrange("k (a n) -> k a n", a=4), f'sta1{sgi}',
                                      reads=[rsgb], accs=[R['agin1']])

            n = len(groups)
            load_w(groups[0], None)
            for i in range(n):
                if i + 1 < n:
                    load_w(groups[i + 1], None)
                compute(groups[i], None)

        def GLU(q):
            for ch in range(8):
                c.op('vector', lambda e: e.tensor_tensor(out=b['gT'][:, ch, :], in0=b['val'][:, ch, :], in1=b['sigT'][:, ch, :], op=ALU.mult),
                     reads=[rval, rsig], writes=[rgT] if ch == 0 else [], accs=[rgT] if ch else [])

        def CONV_LN(q):
            for ch in range(8):
                c.op('vector', lambda e: e.tensor_scalar(
                    out=b['acc'][:, ch, :], in0=b['gT'][:, ch, 1:513], scalar1=vec('conv_a_w', 0 * 8 + ch),
                    scalar2=vec('conv_a_b', ch), op0=ALU.mult, op1=ALU.add),
                    reads=[rgT, R['const']], writes=[racc] if ch == 0 else [], accs=[racc] if ch else [])
                for k in range(1, 31):
                    c.op('vector', lambda e: e.scalar_tensor_tensor(
                        out=b['acc'][:, ch, :], in0=b['gT'][:, ch, 1 + k:513 + k], scalar=vec('conv_a_w', k * 8 + ch),
                        in1=b['acc'][:, ch, :], op0=ALU.mult, op1=ALU.add),
                        reads=[rgT, racc], accs=[racc])
            psM, rM = pb.next()
            psQ, rQ = pb.next()
            for ch in range(8):
                c.op('tensor', lambda e: e.matmul(psM[:], G['ones32'][:], b['acc'][:, ch, :], start=(ch == 0), stop=(ch == 7)),
                     reads=[racc, R['const']], writes=[rM] if ch == 0 else [], accs=[rM] if ch else [])
                s, rs = sq.next()
                c.op('scalar', lambda e: e.activation(out=s[:, 0:512], in_=b['acc'][:, ch, :], func=AF.Square),
                     reads=[racc], writes=[rs])
                c.op('tensor', lambda e: e.matmul(psQ[:], G['ones_bf'][:], s[:, 0:512], start=(ch == 0), stop=(ch == 7)),
                     reads=[rs], writes=[rQ] if ch == 0 else [], accs=[rQ] if ch else [])
            ln = b['ln']
            mean, msq, var, rs2, nmr = [ln[:, i, :] for i in range(5)]
            c.op('vector', lambda e: e.tensor_scalar(out=mean, in0=psM[:], scalar1=1.0 / 1024, scalar2=None, op0=ALU.mult),
                 reads=[rM], writes=[rln])
            c.op('vector', lambda e: e.tensor_tensor(out=msq, in0=mean, in1=mean, op=ALU.mult), reads=[rln], accs=[rln])
            c.op('vector', lambda e: e.scalar_tensor_tensor(out=var, in0=psQ[:], scalar=1.0 / 1024, in1=msq,
                                                           op0=ALU.mult, op1=ALU.subtract),
                 reads=[rQ, rln], accs=[rln])
            c.op('scalar', lambda e: e.activation(out=rs2, in_=var, func=AF.Sqrt, bias=EPS, scale=1.0), reads=[rln], accs=[rln])
            c.op('vector', lambda e: e.reciprocal(out=rs2, in_=rs2), reads=[rln], accs=[rln])
            c.op('vector', lambda e: e.scalar_tensor_tensor(out=nmr, in0=mean, scalar=-1.0, in1=rs2,
                                                           op0=ALU.mult, op1=ALU.mult), reads=[rln], accs=[rln])
            for ch in range(8):
                t, rt = tt.next()
                c.op('vector', lambda e: e.tensor_tensor(out=t[:], in0=b['acc'][:, ch, :], in1=rs2, op=ALU.mult),
                     reads=[racc, rln], writes=[rt])
                c.op('vector', lambda e: e.tensor_tensor(out=t[:], in0=t[:], in1=nmr, op=ALU.add),
                     reads=[rt, rln], accs=[rt])
                yi, ya, rya = yar.nexti()
                c.op('scalar', lambda e: e.activation(out=ya[:], in_=t[:], func=AF.Silu, bias=vec('ln_a_b', ch), scale=vec('ln_a_g', ch)),
                     reads=[rt, R['const']], writes=[rya])
                c.dma('sync', G['mixT'][ch * 128:(ch + 1) * 128, 512 * q:512 * q + 512], ya[:], f'sta2{yi}',
                      reads=[rya], accs=[R['mixT']])

        S1(0)
        INPROJ(0)
        for q in range(4):
            if q + 1 < 4:
                S1(q + 1)
            GLU(q)
            if q + 1 < 4:
                INPROJ(q + 1)
            CONV_LN(q)

    c.run_phase(pa_alloc, pa_prog)

    def ag_phase(kind_in, kind_out, semname):
        def alloc(nc, st):
            return {}

        def prog(c, b):
            c.dma('gpsimd', None, None, semname, reads=[R[kind_in]], writes=[R[kind_out]], inc=1,
                  fn=lambda e: e.collective_compute("AllGather", ALU.bypass, replica_groups=[list(range(NCORES))],
                                                    ins=[G[kind_in][:]], outs=[G[kind_out][:]]))
        c.run_phase(alloc, prog)

    if LAST_PHASE >= 1:
        ag_phase('agin1', 'agout1', 'cc1')

    def pb_alloc(nc, st):
        sb = lambda name, shape, dt: st.enter_context(nc.sbuf_tensor(name, shape, dt))
        b = {}
        b['At'] = [sb(f'b_At{i}', [128, 2, 64, 128], BF16) for i in range(2)]
        b['Bt'] = sb('b_Bt', [128, 2, 128, 128], BF16)
        b['tcs'] = sb('b_tcs', [128, 2, 2048], BF16)
        b['yb'] = [sb(f'b_yb{i}', [128, 2048], BF16) for i in range(2)]
        return b

    def pb_prog(c, b):
        pbk = Rot(c, G['ps'][0:4])
        ybanks = G['ps'][4:8]
        rY = [c.reg() for _ in range(4)]
        rAt = [c.reg() for _ in range(2)]
        ybr = Rot(c, b['yb'])
        rtcs = c.reg()
        c.dma('gpsimd', b['tcs'][:], G['tcs'].rearrange("p (a n) -> p a n", a=2), 'ldb0', writes=[rtcs])
        agv = G['agout1'].rearrange("a (ri g kc n) -> a ri g kc n", ri=2, g=8, kc=128)
        items = [(g, kh) for g in range(8) for kh in range(2)]

        def load(item, slot):
            g, kh = item
            for ri in range(2):
                c.dma('sync', b['At'][slot][:, ri, :, :], agv[:, ri, g, kh * 64:(kh + 1) * 64, :], f'ldb{slot + 1}',
                      reads=[R['agout1']], writes=[rAt[slot]] if ri == 0 else [], accs=[rAt[slot]] if ri else [])

        Bt = b['Bt']
        rBt = c.reg()

        def compute(item, slot):
            g, kh = item
            At = b['At'][slot]
            first = (kh == 0)
            for kcp in range(32):
                ps, rp = pbk.next()
                for j in range(2):
                    kc = 2 * kcp + j
                    c.op('tensor', lambda e: e.matmul(
                        ps[:, j * 256:(j + 1) * 256], At[:, 0, kc, :], G['dft_sb'][:, 0:256], start=True, stop=False),
                        reads=[rAt[slot], R['const']], writes=[rp] if j == 0 else [], sig=(j == 0))
                    c.op('tensor', lambda e: e.matmul(
                        ps[:, j * 256:(j + 1) * 256], At[:, 1, kc, :], G['dft_sb'][:, 256:512], start=False, stop=True),
                        reads=[rAt[slot]], accs=[rp] if j == 1 else [], sig=(j == 1))
                src = ps[:].rearrange("p (j ri k) -> p ri j k", j=2, ri=2)
                dst = Bt[:, :, kh * 64 + 2 * kcp:kh * 64 + 2 * kcp + 2, :]
                if kcp % 2 == 0:
                    c.op('scalar', lambda e: e.activation(out=dst, in_=src, func=AF.Copy),
                         reads=[rp], writes=[rBt] if first else [], accs=[] if first else [rBt])
                else:
                    c.op('vector', lambda e: e.tensor_copy(out=dst, in_=src),
                         reads=[rp], writes=[rBt] if first else [], accs=[] if first else [rBt])
                first = False
            if kh == 0:
                return
            for k1 in range(128):
                bk = k1 // 32
                o = (k1 % 32) * 16
                yb_ps = ybanks[bk]
                fst = (k1 % 32 == 0)
                lst = (k1 % 32 == 31)
                c.op('tensor', lambda e: e.matmul(
                    yb_ps[:, o:o + 16], Bt[:, 0, :, k1], b['tcs'][:, 0, k1 * 16:(k1 + 1) * 16], start=True, stop=False),
                    reads=[rBt, rtcs], writes=[rY[bk]] if fst else [], sig=fst)
                c.op('tensor', lambda e: e.matmul(
                    yb_ps[:, o:o + 16], Bt[:, 1, :, k1], b['tcs'][:, 1, k1 * 16:(k1 + 1) * 16], start=False, stop=True),
                    reads=[rBt], accs=[rY[bk]] if lst else [], sig=lst)
            ybi, yb, ryb = ybr.nexti()
            for bk in range(4):
                src = ybanks[bk][:, :].rearrange("p (k1 k2) -> p k2 k1", k2=16)
                dst = yb[:, :].rearrange("p (k2 k1) -> p k2 k1", k2=16)[:, :, bk * 32:(bk + 1) * 32]
                c.op('scalar', lambda e: e.activation(out=dst, in_=src, func=AF.Copy),
                     reads=[rY[bk]], writes=[ryb] if bk == 0 else [], accs=[ryb] if bk else [])
            r0 = 1024 + g * 128
            c.dma('sync', G['mixT'][r0:r0 + 128, :], yb[:, :], f'stb{ybi}', reads=[ryb], accs=[R['mixT']])

        stream(items, 2, load, compute)

    if LAST_PHASE >= 2:
        c.run_phase(pb_alloc, pb_prog)

    def make_mixmlp(layer, W_out, W_up, W_down, xin, xin_off, xin_reg, norm_name, final):
        def alloc(nc, st):
            sb = lambda name, shape, dt: st.enter_context(nc.sbuf_tensor(name, shape, dt))
            b = {}
            b['xacc'] = sb(f'c{layer}_xacc', [128, 16, 1024], F32)
            b['act'] = sb(f'c{layer}_act', [128, 16, 1024], BF16)
            b['w'] = [sb(f'c{layer}_w{i}', [128, 16, 512], BF16) for i in range(2)]
            b['dw'] = [sb(f'c{layer}_dw{i}', [128, 4, 2048], BF16) for i in range(2)]
            b['aT'] = [sb(f'c{layer}_aT{i}', [128, 4, 1024], BF16) for i in range(2)]
            b['rl'] = [sb(f'c{layer}_rl{i}', [128, 512], F32) for i in range(2)]
            b['sq'] = [sb(f'c{layer}_sq{i}', [128, 1024], BF16) for i in range(2)]
            b['rstd'] = sb(f'c{layer}_rstd', [128, 1024], F32)
            return b

        def rmsnorm_acc(c, b, pb, sq, rx, rrstd):
            xacc = b['xacc']
            psA, rA = pb.next()
            psB, rB = pb.next()
            for kc in range(16):
                s, rs = sq.next()
                c.op('scalar', lambda e, kc=kc, s=s: e.activation(out=s[:], in_=xacc[:, kc, :], func=AF.Square),
                     reads=[rx], writes=[rs])
                c.op('tensor', lambda e, kc=kc, s=s: e.matmul(psA[:], G['ones_bf'][:], s[:, 0:512],
                                                              start=(kc == 0), stop=(kc == 15)),
                     reads=[rs, R['const']], writes=[rA] if kc == 0 else [], accs=[rA] if kc else [])
                c.op('tensor', lambda e, kc=kc, s=s: e.matmul(psB[:], G['ones_bf'][:], s[:, 512:1024],
                                                              start=(kc == 0), stop=(kc == 15)),
                     reads=[rs, R['const']], writes=[rB] if kc == 0 else [], accs=[rB] if kc else [])
            rstd_from(c, psA[:], rA, b['rstd'][:, 0:512], rrstd, D)
            rstd_from(c, psB[:], rB, b['rstd'][:, 512:1024], rrstd, D, acc=True)

        def prog(c, b):
            pb = Rot(c, G['ps'])
            sq = Rot(c, b['sq'])
            rl = Rot(c, b['rl'])
            xacc, act = b['xacc'], b['act']
            rx = [c.reg() for _ in range(16)]
            ract, rrstd = c.reg(), c.reg()
            rw = [c.reg() for _ in range(2)]
            rdw = [c.reg() for _ in range(2)]
            raT = [c.reg() for _ in range(2)]
            for half in range(2):
                t0 = half * 1024
                for kc in range(16):
                    c.dma('sync', xacc[:, kc, :], xin[kc * 128:(kc + 1) * 128, xin_off + t0:xin_off + t0 + 1024], 'ldc0',
                          reads=[xin_reg] if xin_reg is not None else [], writes=[rx[kc]])
                for kc in range(16):
                    rx[kc].w = {'ldc0': c.count['ldc0']}
                c.dma('sync', act[:], G['mixT'][:, t0:t0 + 1024].rearrange("(kc p) t -> p kc t", p=128), 'ldc1',
                      reads=[R['mixT']], writes=[ract])

                def load_o(item, slot):
                    c.dma('gpsimd', b['w'][slot][:], W_out[:, item * 512:(item + 1) * 512].rearrange("(kc p) n -> p kc n", p=128),
                          f'w{slot}', writes=[rw[slot]])

                def comp_o(item, slot):
                    w = b['w'][slot]
                    for mi in range(4):
                        m = item * 4 + mi
                        for tl in range(2):
                            ps, rp = pb.next()
                            for kc in range(16):
                                c.op('tensor', lambda e, ps=ps, kc=kc, w=w, mi=mi, tl=tl: e.matmul(
                                    ps[:], w[:, kc, mi * 128:(mi + 1) * 128], act[:, kc, tl * 512:(tl + 1) * 512],
                                    start=(kc == 0), stop=(kc == 15)),
                                    reads=[rw[slot], ract], writes=[rp] if kc == 0 else [], accs=[rp] if kc == 15 else [],
                                    sig=(kc in (0, 15)))
                            c.op('vector', lambda e, ps=ps, m=m, tl=tl: e.tensor_tensor(
                                out=xacc[:, m, tl * 512:(tl + 1) * 512], in0=xacc[:, m, tl * 512:(tl + 1) * 512], in1=ps[:], op=ALU.add),
                                reads=[rp, rx[m]], accs=[rx[m]])

                stream(list(range(4)), 2, load_o, comp_o)
                psA, rA = pb.next()
                psB, rB = pb.next()
                for kc in range(16):
                    s, rs = sq.next()
                    c.op('scalar', lambda e, kc=kc, s=s: e.activation(out=s[:], in_=xacc[:, kc, :], func=AF.Square),
                         reads=[rx[kc]], writes=[rs])
                    c.op('tensor', lambda e, kc=kc, s=s: e.matmul(psA[:], G['ones_bf'][:], s[:, 0:512],
                                                                  start=(kc == 0), stop=(kc == 15)),
                         reads=[rs, R['const']], writes=[rA] if kc == 0 else [], accs=[rA] if kc else [])
                    c.op('tensor', lambda e, kc=kc, s=s: e.matmul(psB[:], G['ones_bf'][:], s[:, 512:1024],
                                                                  start=(kc == 0), stop=(kc == 15)),
                         reads=[rs, R['const']], writes=[rB] if kc == 0 else [], accs=[rB] if kc else [])
                rstd_from(c, psA[:], rA, b['rstd'][:, 0:512], rrstd, D)
                rstd_from(c, psB[:], rB, b['rstd'][:, 512:1024], rrstd, D, acc=True)
                for kc in range(16):
                    c.op('vector', lambda e, kc=kc: e.scalar_tensor_tensor(
                        out=act[:, kc, :], in0=xacc[:, kc, :], scalar=vec(norm_name, kc), in1=b['rstd'][:],
                        op0=ALU.mult, op1=ALU.mult),
                        reads=[rx[kc], rrstd, R['const']], writes=[ract] if kc == 0 else [], accs=[ract] if kc else [])

                def load_f(F, slot):
                    c.dma('gpsimd', b['w'][slot][:], W_up[:, F * 512:(F + 1) * 512].rearrange("(kc p) n -> p kc n", p=128),
                          f'w{slot}', writes=[rw[slot]])
                    c.dma('gpsimd', b['dw'][slot][:], W_down[F * 512:(F + 1) * 512, :].rearrange("(fc p) n -> p fc n", p=128),
                          f'dw{slot}', writes=[rdw[slot]])

                def comp_f(F, slot):
                    w, dw, aT = b['w'][slot], b['dw'][slot], b['aT'][slot]
                    firsta = True
                    for fi in range(4):
                        for tl in range(2):
                            ps, rp = pb.next()
                            for kc in range(16):
                                c.op('tensor', lambda e, ps=ps, kc=kc, fi=fi, tl=tl: e.matmul(
                                    ps[:], w[:, kc, fi * 128:(fi + 1) * 128], act[:, kc, tl * 512:(tl + 1) * 512],
                                    start=(kc == 0), stop=(kc == 15)),
                                    reads=[rw[slot], ract], writes=[rp] if kc == 0 else [], accs=[rp] if kc == 15 else [],
                                    sig=(kc in (0, 15)))
                            r_, rr_ = rl.next()
                            c.op('scalar', lambda e, ps=ps, r_=r_: e.activation(out=r_[:], in_=ps[:], func=AF.Relu),
                                 reads=[rp], writes=[rr_])
                            c.op('vector', lambda e, r_=r_, fi=fi, tl=tl: e.tensor_tensor(
                                out=aT[:, fi, tl * 512:(tl + 1) * 512], in0=r_[:], in1=r_[:], op=ALU.mult),
                                reads=[rr_], writes=[raT[slot]] if firsta else [], accs=[] if firsta else [raT[slot]])
                            firsta = False
                    for o in range(16):
                        for tl in range(2):
                            ps, rp = pb.next()
                            for fi in range(4):
                                c.op('tensor', lambda e, ps=ps, fi=fi, o=o, tl=tl: e.matmul(
                                    ps[:], dw[:, fi, o * 128:(o + 1) * 128], aT[:, fi, tl * 512:(tl + 1) * 512],
                                    start=(fi == 0), stop=(fi == 3)),
                                    reads=[rdw[slot], raT[slot]], writes=[rp] if fi == 0 else [], accs=[rp] if fi == 3 else [],
                                    sig=(fi in (0, 3)))
                            c.op('vector', lambda e, ps=ps, o=o, tl=tl: e.tensor_tensor(
                                out=xacc[:, o, tl * 512:(tl + 1) * 512], in0=xacc[:, o, tl * 512:(tl + 1) * 512], in1=ps[:], op=ALU.add),
                                reads=[rp, rx[o]], accs=[rx[o]])

                stream(list(range(16)), 2, load_f, comp_f)

                if not final:
                    for kc in range(16):
                        c.dma('sync', G['x1T'][kc * 128:(kc + 1) * 128, t0:t0 + 1024], xacc[:, kc, :], 'stc',
                              reads=[rx[kc]], accs=[R['x1T']])
                    for kc in range(16):
                        rx[kc].r['stc'] = c.count['stc']
                else:
                    psA, rA = pb.next()
                    psB, rB = pb.next()
                    for kc in range(16):
                        s, rs = sq.next()
                        c.op('scalar', lambda e, kc=kc, s=s: e.activation(out=s[:], in_=xacc[:, kc, :], func=AF.Square),
                             reads=[rx[kc]], writes=[rs])
                        c.op('tensor', lambda e, kc=kc, s=s: e.matmul(psA[:], G['ones_bf'][:], s[:, 0:512],
                                                                      start=(kc == 0), stop=(kc == 15)),
                             reads=[rs, R['const']], writes=[rA] if kc == 0 else [], accs=[rA] if kc else [])
                        c.op('tensor', lambda e, kc=kc, s=s: e.matmul(psB[:], G['ones_bf'][:], s[:, 512:1024],
                                                                      start=(kc == 0), stop=(kc == 15)),
                             reads=[rs, R['const']], writes=[rB] if kc == 0 else [], accs=[rB] if kc else [])
                    rstd_from(c, psA[:], rA, b['rstd'][:, 0:512], rrstd, D)
                    rstd_from(c, psB[:], rB, b['rstd'][:, 512:1024], rrstd, D, acc=True)
                    for kc in range(16):
                        c.op('vector', lambda e, kc=kc: e.scalar_tensor_tensor(
                            out=xacc[:, kc, :], in0=xacc[:, kc, :], scalar=vec('final_norm', kc), in1=b['rstd'][:],
                            op0=ALU.mult, op1=ALU.mult),
                            reads=[rrstd, R['const']], writes=[rx[kc]])
                        c.dma('sync', G['outT'][kc * 128:(kc + 1) * 128, t0:t0 + 1024], xacc[:, kc, :], 'stc',
                              reads=[rx[kc]], accs=[R['outT']])
                    for kc in range(16):
                        rx[kc].r['stc'] = c.count['stc']

        return alloc, prog

    if LAST_PHASE >= 3:
        a, p = make_mixmlp(0, G['w_out_e'], G['w_up0'], G['w_down0'], G['xT'], 16, None, 'mlp_norm0', False)
        c.run_phase(a, p)

    def pd_alloc(nc, st):
        sb = lambda name, shape, dt: st.enter_context(nc.sbuf_tensor(name, shape, dt))
        b = {}
        b['xc'] = [sb(f'd_xc{i}', [128, 1024], F32) for i in range(3)]
        b['act'] = sb('d_act', [128, 16, 1024], BF16)
        b['w'] = [sb(f'd_w{i}', [128, 16, 512], BF16) for i in range(3)]
        b['csb'] = sb('d_csb', [128, 4, 1024], F32)
        b['cq'] = sb('d_cq', [128, 4, 1024], F32)
        b['ckv'] = sb('d_ckv', [128, 2, 1024], F32)
        b['qn'] = sb('d_qn', [128, 4, 1024], BF16)
        b['kvn'] = sb('d_kvn', [128, 2, 1024], BF16)
        b['s32'] = [sb(f'd_s32{i}', [128, 512], F32) for i in range(3)]
        b['s16'] = [sb(f'd_s16{i}', [128, 512], BF16) for i in range(3)]
        b['sq'] = [sb(f'd_sq{i}', [128, 1024], BF16) for i in range(2)]
        b['rstd'] = sb('d_rstd', [128, 1024], F32)
        b['rope'] = sb('d_rope', [64, 2, 2048], F32)
        b['t1'] = [sb(f'd_t1{i}', [64, 512], F32) for i in range(2)]
        b['t2'] = [sb(f'd_t2{i}', [64, 512], F32) for i in range(2)]
        b['brow'] = sb('d_brow', [1, 2048], F32)
        return b

    def pd_prog(c, b):
        pb = Rot(c, G['ps'])
        xcr = Rot(c, b['xc'])
        sq = Rot(c, b['sq'])
        s32 = Rot(c, b['s32'])
        s16 = Rot(c, b['s16'])
        t1r = Rot(c, b['t1'])
        t2r = Rot(c, b['t2'])
        act = b['act']
        ract, rrstd, rcsb, rcq, rckv, rqn, rkvn, rrope, rbrow = [c.reg() for _ in range(9)]
        rw = [c.reg() for _ in range(3)]
        c.dma('sync', b['rope'][:], G['rope'].rearrange("p (a n) -> p a n", a=2), 'ldd9', writes=[rrope])

        def small_norm(src, rsrc, nch, gname, dst, rdst):
            psA, rA = pb.next()
            psB, rB = pb.next()
            for kc in range(nch):
                s, rs = sq.next()
                c.op('scalar', lambda e, kc=kc, s=s: e.activation(out=s[:], in_=src[:, kc, :], func=AF.Square),
                     reads=[rsrc], writes=[rs])
                c.op('tensor', lambda e, kc=kc, s=s: e.matmul(psA[:], G['ones_bf'][:], s[:, 0:512],
                                                              start=(kc == 0), stop=(kc == nch - 1)),
                     reads=[rs, R['const']], writes=[rA] if kc == 0 else [], accs=[rA] if kc else [])
                c.op('tensor', lambda e, kc=kc, s=s: e.matmul(psB[:], G['ones_bf'][:], s[:, 512:1024],
                                                              start=(kc == 0), stop=(kc == nch - 1)),
                     reads=[rs, R['const']], writes=[rB] if kc == 0 else [], accs=[rB] if kc else [])
            rstd_from(c, psA[:], rA, b['rstd'][:, 0:512], rrstd, nch * 128)
            rstd_from(c, psB[:], rB, b['rstd'][:, 512:1024], rrstd, nch * 128, acc=True)
            for kc in range(nch):
                c.op('vector', lambda e, kc=kc: e.scalar_tensor_tensor(
                    out=dst[:, kc, :], in0=src[:, kc, :], scalar=vec(gname, kc), in1=b['rstd'][:],
                    op0=ALU.mult, op1=ALU.mult),
                    reads=[rsrc, rrstd, R['const']], writes=[rdst] if kc == 0 else [], accs=[rdst] if kc else [])

        def rope_out(psx, rpx, psy, rpy, tok0, dst_ap, dst_reg):
            t1, rt1 = t1r.next()
            t2, rt2 = t2r.next()
            c.op('vector', lambda e: e.tensor_tensor(out=t1[:], in0=psx[0:64, :], in1=b['rope'][:, 0, tok0:tok0 + 512], op=ALU.mult),
                 reads=[rpx, rrope], writes=[rt1])
            c.op('vector', lambda e: e.tensor_tensor(out=t2[:], in0=psy[0:64, :], in1=b['rope'][:, 1, tok0:tok0 + 512], op=ALU.mult),
                 reads=[rpy, rrope], writes=[rt2])
            si, s, rs = s16.nexti()
            c.op('vector', lambda e: e.tensor_tensor(out=s[0:64, :], in0=t1[:], in1=t2[:], op=ALU.add),
                 reads=[rt1, rt2], writes=[rs])
            c.dma('sync', dst_ap, s[0:64, :], f'sd16_{si}', reads=[rs], accs=[dst_reg])

        pending_ag = []

        def issue_ag2(hf):
            c.dma('gpsimd', None, None, f'cc2{hf}', reads=[R[f'ag2in{hf}']], writes=[R[f'ag2out{hf}']], inc=1,
                  fn=lambda e: e.collective_compute("AllGather", ALU.bypass, replica_groups=[list(range(NCORES))],
                                                    ins=[G[f'ag2in{hf}'][:]], outs=[G[f'ag2out{hf}'][:]]))

        for half in range(2):
            t0 = half * 1024
            psA, rA = pb.next()
            psB, rB = pb.next()
            for kc in range(16):
                xi, xc, rxc = xcr.nexti()
                c.dma('sync', xc[:], G['x1T'][kc * 128:(kc + 1) * 128, t0:t0 + 1024], f'ldd0{xi}', reads=[R['x1T']], writes=[rxc])
                s, rs = sq.next()
                c.op('scalar', lambda e, xc=xc, s=s: e.activation(out=s[:], in_=xc[:], func=AF.Square), reads=[rxc], writes=[rs])
                c.op('tensor', lambda e, kc=kc, s=s: e.matmul(psA[:], G['ones_bf'][:], s[:, 0:512], start=(kc == 0), stop=(kc == 15)),
                     reads=[rs, R['const']], writes=[rA] if kc == 0 else [], accs=[rA] if kc else [])
                c.op('tensor', lambda e, kc=kc, s=s: e.matmul(psB[:], G['ones_bf'][:], s[:, 512:1024], start=(kc == 0), stop=(kc == 15)),
                     reads=[rs, R['const']], writes=[rB] if kc == 0 else [], accs=[rB] if kc else [])
            rstd_from(c, psA[:], rA, b['rstd'][:, 0:512], rrstd, D)
            rstd_from(c, psB[:], rB, b['rstd'][:, 512:1024], rrstd, D, acc=True)
            for kc in range(16):
                xi, xc, rxc = xcr.nexti()
                c.dma('sync', xc[:], G['x1T'][kc * 128:(kc + 1) * 128, t0:t0 + 1024], f'ldd0{xi}', reads=[R['x1T']], writes=[rxc])
                c.op('vector', lambda e, kc=kc, xc=xc: e.scalar_tensor_tensor(
                    out=act[:, kc, :], in0=xc[:], scalar=vec('mix_norm_o', kc), in1=b['rstd'][:], op0=ALU.mult, op1=ALU.mult),
                    reads=[rxc, rrstd, R['const']], writes=[ract] if kc == 0 else [], accs=[ract] if kc else [])

            groups = [(0, 'b', 0), (512, 'b', 1), (1024, 'c', 0), (2048, 'hc', 0), (1536, 'c', 1), (2560, 'hc', 1),
                      (3072, 'cq', 0), (3584, 'kv', 0)]

            def load_w(item, slot):
                col0, kind, gi = item
                if pending_ag and kind == 'hc' and gi == 0:
                    issue_ag2(pending_ag.pop())
                n = 384 if kind == 'kv' else 512
                c.dma('gpsimd', b['w'][slot][:, :, 0:n], G['w_in_o'][:, col0:col0 + n].rearrange("(kc p) n -> p kc n", p=128),
                      f'w{slot}', writes=[rw[slot]])

            def mm16(ps, rp, w, slot, c0, m, tl):
                for kc in range(16):
                    c.op('tensor', lambda e, kc=kc: e.matmul(
                        ps[0:m, :], w[:, kc, c0:c0 + m], act[:, kc, tl * 512:(tl + 1) * 512],
                        start=(kc == 0), stop=(kc == 15)),
                        reads=[rw[slot], ract], writes=[rp] if kc == 0 else [], accs=[rp] if kc == 15 else [],
                        sig=(kc in (0, 15)))

            def compute(item, slot):
                col0, kind, gi = item
                w = b['w'][slot]
                if kind == 'kv':
                    for mi in range(2):
                        for tl in range(2):
                            ps, rp = pb.next()
                            mm16(ps, rp, w, slot, mi * 128, 128, tl)
                            c.op('scalar', lambda e, ps=ps, mi=mi, tl=tl: e.activation(
                                out=b['ckv'][:, mi, tl * 512:(tl + 1) * 512], in_=ps[:], func=AF.Copy),
                                reads=[rp], writes=[rckv] if (mi == 0 and tl == 0) else [], accs=[] if (mi == 0 and tl == 0) else [rckv])
                    for tl in range(2):
                        psx, rpx = pb.next()
                        mm16(psx, rpx, w, slot, 256, 64, tl)
                        psy, rpy = pb.next()
                        mm16(psy, rpy, w, slot, 320, 64, tl)
                        tok0 = t0 + tl * 512
                        rope_out(psx, rpx, psy, rpy, tok0, G[f'ag2in{half}'][1024:1088, tl * 512:tl * 512 + 512], R[f'ag2in{half}'])
                    return
                for mi in range(4):
                    ch = gi * 4 + mi
                    for tl in range(2):
                        tok0 = t0 + tl * 512
                        ps, rp = pb.next()
                        mm16(ps, rp, w, slot, mi * 128, 128, tl)
                        if kind == 'b':
                            si, s, rs = s32.nexti()
                            c.op('scalar', lambda e, ps=ps, s=s: e.activation(out=s[:], in_=ps[:], func=AF.Copy),
                                 reads=[rp], writes=[rs])
                            c.dma('sync', G['bT'][ch * 128:(ch + 1) * 128, tok0:tok0 + 512], s[:], f'sd32_{si}', reads=[rs], accs=[R['bT']])
                        elif kind == 'c':
                            fw = (mi == 0 and tl == 0)
                            c.op('scalar', lambda e, ps=ps, mi=mi, tl=tl: e.activation(
                                out=b['csb'][:, mi, tl * 512:(tl + 1) * 512], in_=ps[:], func=AF.Copy),
                                reads=[rp], writes=[rcsb] if fw else [], accs=[] if fw else [rcsb])
                        elif kind == 'hc':
                            si, s, rs = s32.nexti()
                            c.op('vector', lambda e, ps=ps, s=s, mi=mi, tl=tl: e.tensor_tensor(
                                out=s[:], in0=b['csb'][:, mi, tl * 512:(tl + 1) * 512], in1=ps[:], op=ALU.mult),
                                reads=[rp, rcsb], writes=[rs])
                            c.dma('sync', G['vT'][ch * 128:(ch + 1) * 128, tok0:tok0 + 512], s[:], f'sd32_{si}', reads=[rs], accs=[R['vT']])
                            for (cond, col, off) in ((half == 0 and tl == 0, 0, 0), (half == 1 and tl == 1, 511, 1024)):
                                if cond:
                                    psb, rpb = pb.next()
                                    c.op('tensor', lambda e, psb=psb, s=s, col=col: e.matmul(
                                        psb[0:1, 0:128], s[:, col:col + 1], G['ident32'][:], start=True, stop=True),
                                        reads=[rs, R['const']], writes=[rpb])
                                    c.op('scalar', lambda e, psb=psb, off=off, ch=ch: e.activation(
                                        out=b['brow'][0:1, off + ch * 128:off + (ch + 1) * 128], in_=psb[0:1, 0:128], func=AF.Copy),
                                        reads=[rpb], accs=[rbrow])
                        elif kind == 'cq':
                            fw = (mi == 0 and tl == 0)
                            c.op('scalar', lambda e, ps=ps, mi=mi, tl=tl: e.activation(
                                out=b['cq'][:, mi, tl * 512:(tl + 1) * 512], in_=ps[:], func=AF.Copy),
                                reads=[rp], writes=[rcq] if fw else [], accs=[] if fw else [rcq])

            stream(groups, 3, load_w, compute)
            if half == 1:
                c.dma('sync', G['ag3in'][:], b['brow'][:], 'stbrow', reads=[rbrow], accs=[R['ag3in']])
                c.dma('gpsimd', None, None, 'cc3', reads=[R['ag3in']], writes=[R['ag3out']], inc=1,
                      fn=lambda e: e.collective_compute("AllGather", ALU.bypass, replica_groups=[list(range(NCORES))],
                                                        ins=[G['ag3in'][:]], outs=[G['ag3out'][:]]))

            small_norm(b['cq'], rcq, 4, 'q_norm_g', b['qn'], rqn)
            small_norm(b['ckv'], rckv, 2, 'kv_norm_g', b['kvn'], rkvn)
            wq = b['w'][0][:].rearrange("p a n -> p (a n)").rearrange("p (kc n) -> p kc n", kc=4)
            wkv = b['w'][1][:].rearrange("p a n -> p (a n)")[:, 0:4096].rearrange("p (kc n) -> p kc n", kc=2)
            c.dma('gpsimd', wq, G['w_uq3'].rearrange("(kc p) n -> p kc n", p=128), 'w0', writes=[rw[0]])
            c.dma('gpsimd', wkv, G['w_ukv2'].rearrange("(kc p) n -> p kc n", p=128), 'w1', writes=[rw[1]])

            def mmq(ps, rp, c0, m, tl):
                for kc in range(4):
                    c.op('tensor', lambda e, kc=kc: e.matmul(
                        ps[0:m, :], wq[:, kc, c0:c0 + m], b['qn'][:, kc, tl * 512:(tl + 1) * 512], start=(kc == 0), stop=(kc == 3)),
                        reads=[rw[0], rqn], writes=[rp] if kc == 0 else [], accs=[rp] if kc == 3 else [], sig=(kc in (0, 3)))

            for h in range(8):
                for tl in range(2):
                    tok0 = t0 + tl * 512
                    ps, rp = pb.next()
                    mmq(ps, rp, h * 128, 128, tl)
                    si, s, rs = s16.nexti()
                    c.op('scalar', lambda e, ps=ps, s=s: e.activation(out=s[:], in_=ps[:], func=AF.Copy), reads=[rp], writes=[rs])
                    c.dma('sync', G['qT'][h * 128:(h + 1) * 128, tok0:tok0 + 512], s[:], f'sd16_{si}', reads=[rs], accs=[R['qT']])
                    psx, rpx = pb.next()
                    mmq(psx, rpx, 1024 + h * 64, 64, tl)
                    psy, rpy = pb.next()
                    mmq(psy, rpy, 1536 + h * 64, 64, tl)
                    rope_out(psx, rpx, psy, rpy, tok0, G['qT'][1024 + h * 64:1024 + (h + 1) * 64, tok0:tok0 + 512], R['qT'])
            for h in range(8):
                for tl in range(2):
                    tok0 = t0 + tl * 512
                    ps, rp = pb.next()
                    for kc in range(2):
                        c.op('tensor', lambda e, kc=kc, ps=ps, h=h, tl=tl: e.matmul(
                            ps[:], wkv[:, kc, h * 128:(h + 1) * 128], b['kvn'][:, kc, tl * 512:(tl + 1) * 512], start=(kc == 0), stop=(kc == 1)),
                            reads=[rw[1], rkvn], writes=[rp] if kc == 0 else [], accs=[rp] if kc else [])
                    si, s, rs = s16.nexti()
                    c.op('scalar', lambda e, ps=ps, s=s: e.activation(out=s[:], in_=ps[:], func=AF.Copy), reads=[rp], writes=[rs])
                    c.dma('sync', G[f'ag2in{half}'][h * 128:(h + 1) * 128, tl * 512:tl * 512 + 512], s[:], f'sd16_{si}', reads=[rs], accs=[R[f'ag2in{half}']])
            for tb in range(8):
                blk = half * 8 + tb
                for nh in range(2):
                    ps, rp = pb.next()
                    for kc in range(2):
                        c.op('tensor', lambda e, kc=kc, ps=ps, tb=tb, nh=nh: e.matmul(
                            ps[:], b['kvn'][:, kc, tb * 128:(tb + 1) * 128], wkv[:, kc, 1024 + nh * 512:1024 + (nh + 1) * 512],
                            start=(kc == 0), stop=(kc == 1)),
                            reads=[rw[1], rkvn], writes=[rp] if kc == 0 else [], accs=[rp] if kc else [])
                    si, s, rs = s16.nexti()
                    c.op('vector', lambda e, ps=ps, s=s: e.tensor_copy(out=s[:], in_=ps[:]), reads=[rp], writes=[rs])
                    r0 = 1088 + nh * 512
                    dst = G[f'ag2in{half}'][r0:r0 + 512, tb * 128:(tb + 1) * 128].rearrange("(hh p) d -> p hh d", p=128)
                    c.dma('sync', dst, s[:].rearrange("p (hh d) -> p hh d", hh=4), f'sd16_{si}', reads=[rs], accs=[R[f'ag2in{half}']])
            issue_ag2(half)

    def pd_prog_all(c, b):
        pd_prog(c, b)

    if LAST_PHASE >= 4:
        c.run_phase(pd_alloc, pd_prog_all)
    if LAST_PHASE >= 5:
        pass

    def pe_alloc(nc, st):
        sb = lambda name, shape, dt: st.enter_context(nc.sbuf_tensor(name, shape, dt))
        b = {}
        b['KT'] = [sb(f'e_KT{i}', [128, 16384], BF16) for i in range(2)]
        b['V'] = [sb(f'e_V{i}', [128, 8, 2048], BF16) for i in range(2)]
        b['kr'] = sb('e_kr', [128, 16384], BF16)
        b['qn'] = [sb(f'e_qn{i}', [128, 2048], BF16) for i in range(2)]
        b['qr'] = [sb(f'e_qr{i}', [128, 2048], BF16) for i in range(2)]
        b['P'] = [sb(f'e_P{i}', [128, 512], BF16) for i in range(6)]
        b['dacc'] = [sb(f'e_dacc{i}', [128, 512], F32) for i in range(2)]
        b['rd'] = sb('e_rd', [128, 512], F32)
        b['o'] = [sb(f'e_o{i}', [128, 512], BF16) for i in range(2)]
        return b

    def pe0_alloc(nc, st):
        sb = lambda name, shape, dt: st.enter_context(nc.sbuf_tensor(name, shape, dt))
        b = {}
        b['bt'] = sb('e_bt', [8, 2048], F32)
        b['sel'] = sb('e_sel', [8, 2], F32)
        b['vt'] = sb('e_vt', [128, 2050], F32)
        b['bb'] = sb('e_bb', [128, 2048], F32)
        b['ca'] = sb('e_ca', [128, 2048], F32)
        b['yc'] = sb('e_yc', [128, 2048], BF16)
        return b

    def pe0_prog(c, b):
        rbt, rsel, rvt, rbb, rca, ryc = [c.reg() for _ in range(6)]
        sb_rot = Rot(c, G['ps'][0:4])
        pbs = sb_rot
        c.dma('sync', b['bt'][:], G['ag3out'][:], 'lde0', reads=[R['ag3out']], writes=[rbt])
        c.dma('sync', b['sel'][:], G['sel'][:], 'lde8', writes=[rsel])
        for j in range(8):
            c.dma('sync', b['vt'][:, 1:2049], G['vT'][j * 128:(j + 1) * 128, :], 'lde1', reads=[R['vT']], writes=[rvt])
            c.dma('sync', b['bb'][:], G['bT'][j * 128:(j + 1) * 128, :], 'lde2', reads=[R['bT']], writes=[rbb])
            ps, rp = pbs.next()
            c.op('tensor', lambda e, ps=ps, j=j: e.matmul(ps[:, 0:1], b['bt'][0:8, 1024 + j * 128:1024 + (j + 1) * 128], b['sel'][0:8, 0:1],
                                                          start=True, stop=True), reads=[rbt, rsel], writes=[rp])
            c.op('tensor', lambda e, ps=ps, j=j: e.matmul(ps[:, 1:2], b['bt'][0:8, j * 128:(j + 1) * 128], b['sel'][0:8, 1:2],
                                                          start=True, stop=True), reads=[rbt, rsel], accs=[rp])
            c.op('scalar', lambda e, ps=ps: e.activation(out=b['vt'][:, 0:1], in_=ps[:, 0:1], func=AF.Copy), reads=[rp], accs=[rvt])
            c.op('scalar', lambda e, ps=ps: e.activation(out=b['vt'][:, 2049:2050], in_=ps[:, 1:2], func=AF.Copy), reads=[rp], accs=[rvt])
            c.op('vector', lambda e, j=j: e.tensor_scalar(out=b['ca'][:], in0=b['vt'][:, 0:2048], scalar1=vec('conv_c_w', 0 * 8 + j),
                                                          scalar2=None, op0=ALU.mult), reads=[rvt, R['const']], writes=[rca])
            for k in (1, 2):
                c.op('vector', lambda e, j=j, k=k: e.scalar_tensor_tensor(
                    out=b['ca'][:], in0=b['vt'][:, k:k + 2048], scalar=vec('conv_c_w', k * 8 + j), in1=b['ca'][:],
                    op0=ALU.mult, op1=ALU.add), reads=[rvt, rca], accs=[rca])
            c.op('vector', lambda e: e.tensor_tensor(out=b['yc'][:], in0=b['ca'][:], in1=b['bb'][:], op=ALU.mult),
                 reads=[rca, rbb], writes=[ryc])
            c.dma('sync', G['mixT'][j * 128:(j + 1) * 128, :], b['yc'][:], 'ste0', reads=[ryc], accs=[R['mixT']])

    def pe_prog(c, b):
        sb_rot = Rot(c, G['ps'][0:4])
        SCALE = 192.0 ** -0.5
        rKT = [c.reg() for _ in range(2)]
        rV = [c.reg() for _ in range(2)]
        rqn = [c.reg() for _ in range(2)]
        rqr = [c.reg() for _ in range(2)]
        rkr, rrd = c.reg(), c.reg()
        Pr = Rot(c, b['P'])
        orot = Rot(c, b['o'])
        obanks = Rot(c, G['ps'][4:6])
        dbanks = Rot(c, G['ps'][6:8])
        daccr = Rot(c, b['dacc'])
        c.op('vector', lambda e: e.memset(b['kr'][64:128, :], 0.0), accs=[rkr])
        for i in range(2):
            c.op('vector', lambda e, i=i: e.memset(b['qr'][i][64:128, :], 0.0), accs=[rqr[i]])
        for r in range(8):
            for hf in range(2):
                c.dma('sync', b['kr'][0:64, r * 2048 + hf * 1024:r * 2048 + (hf + 1) * 1024],
                      G[f'ag2out{hf}'][r * 2112 + 1024:r * 2112 + 1088, :], 'lde3', reads=[R[f'ag2out{hf}']], accs=[rkr])

        def load(h, slot):
            for r in range(8):
                for hf in range(2):
                    fst = (r == 0 and hf == 0)
                    c.dma('sync', b['KT'][slot][:, r * 2048 + hf * 1024:r * 2048 + (hf + 1) * 1024],
                          G[f'ag2out{hf}'][r * 2112 + h * 128:r * 2112 + (h + 1) * 128, :],
                          f'lde4{slot}', reads=[R[f'ag2out{hf}']], writes=[rKT[slot]] if fst else [], accs=[] if fst else [rKT[slot]])
                    c.dma('sync', b['V'][slot][:, r, hf * 1024:(hf + 1) * 1024],
                          G[f'ag2out{hf}'][r * 2112 + 1088 + h * 128:r * 2112 + 1088 + (h + 1) * 128, :],
                          f'lde5{slot}', reads=[R[f'ag2out{hf}']], writes=[rV[slot]] if fst else [], accs=[] if fst else [rV[slot]])
            c.dma('sync', b['qn'][slot][:], G['qT'][h * 128:(h + 1) * 128, :], f'lde6{slot}', reads=[R['qT']], writes=[rqn[slot]])
            c.dma('sync', b['qr'][slot][0:64, :], G['qT'][1024 + h * 64:1024 + (h + 1) * 64, :], f'lde7{slot}', reads=[R['qT']],
                  accs=[rqr[slot]], extra=list(rqr[slot].w.items()))

        def compute(h, slot):
            KT, V, qn, qr = b['KT'][slot], b['V'][slot], b['qn'][slot], b['qr'][slot]
            for qt in range(4):
                q0 = qt * 512
                ops_, rO = obanks.next()
                dps_, rD = dbanks.next()
                dacc, rda = daccr.next()

                def s_mm(kb):
                    ps, rp = sb_rot.next()
                    c.op('tensor', lambda e: e.matmul(ps[:], KT[:, kb * 128:(kb + 1) * 128], qn[:, q0:q0 + 512], start=True, stop=False),
                         reads=[rKT[slot], rqn[slot]], writes=[rp])
                    c.op('tensor', lambda e: e.matmul(ps[:], b['kr'][:, kb * 128:(kb + 1) * 128], qr[:, q0:q0 + 512], start=False, stop=True),
                         reads=[rkr, rqr[slot]], accs=[rp])
                    return ps, rp

                pend = [s_mm(kb) for kb in range(3)]
                for kb in range(128):
                    if kb + 3 < 128:
                        pend.append(s_mm(kb + 3))
                    ps, rp = pend.pop(0)
                    P, rP = Pr.next()
                    c.op('scalar', lambda e, ps=ps, P=P: e.activation(out=P[:], in_=ps[:], func=AF.Exp, scale=SCALE),
                         reads=[rp], writes=[rP])
                    r_, bb = kb // 16, kb % 16
                    c.op('tensor', lambda e, P=P, r_=r_, bb=bb, kb=kb: e.matmul(
                        ops_[:], V[:, r_, bb * 128:(bb + 1) * 128], P[:], start=(kb == 0), stop=(kb == 127)),
                        reads=[rV[slot], rP], writes=[rO] if kb == 0 else [], accs=[rO] if kb == 127 else [],
                        sig=(kb in (0, 127)))
                    if kb == 0:
                        c.op('vector', lambda e, P=P: e.tensor_copy(out=dacc[:], in_=P[:]), reads=[rP], writes=[rda])
                    else:
                        c.op('vector', lambda e, P=P: e.tensor_tensor(out=dacc[:], in0=dacc[:], in1=P[:], op=ALU.add),
                             reads=[rP, rda], accs=[rda])
                c.op('tensor', lambda e: e.matmul(dps_[:], G['ones32'][:], dacc[:], start=True, stop=True),
                     reads=[rda, R['const']], writes=[rD])
                c.op('vector', lambda e: e.reciprocal(out=b['rd'][:], in_=dps_[:]), reads=[rD], writes=[rrd])
                oi, o, ro = orot.nexti()
                c.op('vector', lambda e, o=o: e.tensor_tensor(out=o[:], in0=ops_[:], in1=b['rd'][:], op=ALU.mult),
                     reads=[rO, rrd], writes=[ro])
                c.dma('sync', G['mixT'][1024 + h * 128:1024 + (h + 1) * 128, q0:q0 + 512], o[:], f'ste1{oi}', reads=[ro], accs=[R['mixT']])

        stream(list(range(8)), 2, load, compute)

    if LAST_PHASE >= 6:
        c.run_phase(pe0_alloc, pe0_prog)
        c.run_phase(pe_alloc, pe_prog)

    if LAST_PHASE >= 7:
        a, p = make_mixmlp(1, G['w_out_o'], G['w_up1'], G['w_down1'], G['x1T'], 0, R['x1T'], 'mlp_norm1', True)
        c.run_phase(a, p)

    if DEBUG:
        def dbg_alloc(nc, st):
            return {}

        def dbg_prog(c, b):
            for kc in range(16):
                c.dma('sync', G['dbg'][kc * 128:(kc + 1) * 128, :], G['mixT'][kc * 128:(kc + 1) * 128, :], 'dbg1', reads=[R['mixT']], accs=[R['dbg']])
                c.dma('sync', G['dbg2'][kc * 128:(kc + 1) * 128, :], G['x1T'][kc * 128:(kc + 1) * 128, :], 'dbg1', reads=[R['x1T']], accs=[R['dbg']])
        c.run_phase(dbg_alloc, dbg_prog)

    stack.close()
    return nc


_NC_CACHE = {}


def _chunked(v, nch):
    return np.ascontiguousarray(np.asarray(v, np.float32).reshape(nch, 128).T)


def _host_consts():
    n = np.arange(128, dtype=np.float64)
    ang = 2.0 * np.pi * np.outer(n, n) / 128.0
    C1, S1 = np.cos(ang), np.sin(ang)
    dft = np.concatenate([C1, -S1, S1, C1], axis=1).astype(np.float32)
    return dft


def _prep_inputs(inp):
    f32 = np.float32
    x = np.asarray(inp['x'], f32)[0]
    xT = np.zeros((D, S + 32), f32)
    xT[:, 16:16 + S] = x.T
    w_in_o = np.asarray(inp['w_in_o'], f32)[0]
    kr = w_in_o[:, 3840:3904]
    w_in_o2 = np.concatenate([w_in_o, kr[:, 32:64], kr[:, 0:32]], axis=1)
    w_uq = np.asarray(inp['w_uq'], f32)[0].reshape(512, 8, 192)
    qn = w_uq[:, :, 0:128].reshape(512, 1024)
    qr = w_uq[:, :, 128:192]
    qrs = np.concatenate([qr[:, :, 32:64], qr[:, :, 0:32]], axis=2)
    w_uq3 = np.ascontiguousarray(np.concatenate([qn, qr.reshape(512, 512), qrs.reshape(512, 512)], axis=1))
    w_ukv = np.asarray(inp['w_ukv'], f32)[0].reshape(256, 8, 256)
    w_ukv2 = np.ascontiguousarray(np.concatenate([w_ukv[:, :, 0:128].reshape(256, 1024),
                                                  w_ukv[:, :, 128:256].reshape(256, 1024)], axis=1))
    vecs = np.zeros((128, NV), f32)

    def put(name, arr, nch):
        vecs[:, VOFF[name]:VOFF[name] + nch] = _chunked(arr, nch)
    put('mix_norm_e', inp['mix_norm_e'][0], 16)
    put('mlp_norm0', inp['mlp_norm'][0], 16)
    put('mix_norm_o', inp['mix_norm_o'][0], 16)
    put('mlp_norm1', inp['mlp_norm'][1], 16)
    put('final_norm', inp['final_norm'], 16)
    put('conv_a_b', inp['conv_a_b'][0], 8)
    put('ln_a_g', inp['ln_a_g'][0], 8)
    put('ln_a_b', inp['ln_a_b'][0], 8)
    caw = np.asarray(inp['conv_a_w'], f32)[0]
    for k in range(31):
        vecs[:, VOFF['conv_a_w'] + k * 8:VOFF['conv_a_w'] + (k + 1) * 8] = _chunked(caw[k], 8)
    ccw = np.asarray(inp['conv_c_w'], f32)[0]
    for k in range(3):
        vecs[:, VOFF['conv_c_w'] + k * 8:VOFF['conv_c_w'] + (k + 1) * 8] = _chunked(ccw[k], 8)
    put('q_norm_g', inp['q_norm_g'][0], 4)
    put('kv_norm_g', inp['kv_norm_g'][0], 2)
    dft = _host_consts()
    common = {
        'w_in_e': np.ascontiguousarray(np.asarray(inp['w_in_e'], f32)[0]),
        'w_out_e': np.ascontiguousarray(np.asarray(inp['w_out_e'], f32)[0]),
        'w_in_o': np.ascontiguousarray(w_in_o2),
        'w_uq3': w_uq3, 'w_ukv2': w_ukv2,
        'w_out_o': np.ascontiguousarray(np.asarray(inp['w_out_o'], f32)[0]),
        'w_up0': np.ascontiguousarray(np.asarray(inp['w_up'], f32)[0]),
        'w_up1': np.ascontiguousarray(np.asarray(inp['w_up'], f32)[1]),
        'w_down0': np.ascontiguousarray(np.asarray(inp['w_down'], f32)[0]),
        'w_down1': np.ascontiguousarray(np.asarray(inp['w_down'], f32)[1]),
        'vecs': vecs, 'dft': dft,
    }
    inv = 1.0 / (10000.0 ** (np.arange(0, 64, 2, dtype=np.float32) / 64.0))
    in_maps = []
    n2 = np.arange(128, dtype=np.float64)[:, None]
    nrm = 2.0 ** -10.5
    for cidx in range(NCORES):
        m = dict(common)
        m['xT'] = np.ascontiguousarray(xT[:, cidx * T:cidx * T + T + 32])
        k1 = np.arange(128)[:, None]
        k2 = np.arange(16)[None, :]
        kk = (cidx * T + 128 * k2 + k1).reshape(1, 2048).astype(np.float64)
        ang = 2.0 * np.pi * ((n2 * kk) % S) / S
        m['tcs'] = np.concatenate([np.cos(ang) * nrm, np.sin(ang) * nrm], axis=1).astype(np.float32)
        pos = np.arange(cidx * T, (cidx + 1) * T, dtype=np.float32)
        a = (pos[:, None] * inv[None, :]).T
        cs, sn = np.cos(a), np.sin(a)
        CC = np.concatenate([cs, cs], axis=0)
        SS = np.concatenate([-sn, sn], axis=0)
        m['rope'] = np.ascontiguousarray(np.concatenate([CC, SS], axis=1).astype(np.float32))
        sel = np.zeros((8, 2), np.float32)
        if cidx > 0:
            sel[cidx - 1, 0] = 1.0
        if cidx < NCORES - 1:
            sel[cidx + 1, 1] = 1.0
        m['sel'] = sel
        m['ident'] = np.eye(128, dtype=np.float32)
        in_maps.append(m)
    return in_maps


def kernel(**inputs):
    if 'nc' not in _NC_CACHE:
        _NC_CACHE['nc'] = build_program()
    nc = _NC_CACHE['nc']
    in_maps = _prep_inputs(inputs)
    res = run_bass_kernel_spmd(nc, in_maps, core_ids=list(range(NCORES)))
    out = np.concatenate([np.asarray(res.results[i]['outT']).T for i in range(NCORES)], axis=0)
    if DEBUG:
        _NC_CACHE['dbg'] = [np.asarray(res.results[i]['dbg']) for i in range(NCORES)]
        _NC_CACHE['dbg2'] = [np.asarray(res.results[i]['dbg2']) for i in range(NCORES)]
    return np.ascontiguousarray(out[None].astype(np.float32))
```
